# Optimizing a Trainium2 kernel written in Bass

```python
import math
import jax, jax.numpy as jnp
from jax import lax
import numpy as np

D_MODEL = 1024
BATCH = 2
SEQ = 8192
DEPTH = 4

N_MIXERS = 2
N_S5_LAYERS = (DEPTH + 1) // 2
N_ATTN_LAYERS = DEPTH // 2
N_META = 16
GRID_W = 64
HEAD_DIM = 64
N_Q_HEADS = D_MODEL // HEAD_DIM
N_KV_HEADS = N_Q_HEADS // 4
Q_PER_KV = N_Q_HEADS // N_KV_HEADS
QKV_WIDTH = (N_Q_HEADS + 2 * N_KV_HEADS) * HEAD_DIM
QUERY_BLOCK = 128
ROPE_THETA = 10000.0
ROPE_AXIS_DIM = HEAD_DIM // 2
QK_EPS = 1e-6
S5_GROUP_CH = 16
S5_GROUPS = D_MODEL // S5_GROUP_CH
S5_STATE = 64
S5_DT_MIN = 1e-3
S5_DT_MAX = 1e-1
D_FF = -(-8 * D_MODEL // (3 * 256)) * 256
LN_EPS = 1e-5
DEEPNORM_ALPHA = (2.0 * DEPTH) ** 0.25
DEEPNORM_BETA = (8.0 * DEPTH) ** -0.25

kernel_name = "hybrid_s5_gqa_deepnorm_encoder"


def layer_norm(x, gain, bias):
    xf = x.astype(jnp.float32)
    mean = jnp.mean(xf, axis=-1, keepdims=True)
    var = jnp.mean(jnp.square(xf - mean), axis=-1, keepdims=True)
    y = (xf - mean) * lax.rsqrt(var + LN_EPS) * gain.astype(jnp.float32) + bias.astype(jnp.float32)
    return y.astype(x.dtype)


def swiglu_ffn(h, w_gate, w_up, w_down):
    return (jax.nn.silu(h @ w_gate) * (h @ w_up)) @ w_down


def _s5_combine(left, right):
    ar1, ai1, br1, bi1 = left
    ar2, ai2, br2, bi2 = right
    ar = ar1 * ar2 - ai1 * ai2
    ai = ar1 * ai2 + ai1 * ar2
    br = ar2 * br1 - ai2 * bi1 + br2
    bi = ar2 * bi1 + ai2 * br1 + bi2
    return (ar, ai, br, bi)


def s5_direction(u, lam_re, lam_im, log_dt, b_re, b_im, c_re, c_im, reverse):
    L = u.shape[1]
    lr = lam_re.astype(jnp.float32)
    li = lam_im.astype(jnp.float32)
    dt = jnp.exp(log_dt.astype(jnp.float32))[:, None]
    mag = jnp.exp(lr * dt)
    abr = mag * jnp.cos(li * dt)
    abi = mag * jnp.sin(li * dt)
    nr, ni = abr - 1.0, abi
    den = lr * lr + li * li
    cr = (nr * lr + ni * li) / den
    ci = (ni * lr - nr * li) / den
    br = b_re.astype(jnp.float32)
    bi = b_im.astype(jnp.float32)
    bbr = cr[..., None] * br - ci[..., None] * bi
    bbi = cr[..., None] * bi + ci[..., None] * br
    bu_r = jnp.einsum('blgc,gpc->blgp', u, bbr)
    bu_i = jnp.einsum('blgc,gpc->blgp', u, bbi)
    a_r = jnp.broadcast_to(abr[None, None], (1, L) + abr.shape)
    a_i = jnp.broadcast_to(abi[None, None], (1, L) + abi.shape)
    _, _, sr, si = lax.associative_scan(_s5_combine, (a_r, a_i, bu_r, bu_i),
                                        reverse=reverse, axis=1)
    return (jnp.einsum('blgp,gcp->blgc', sr, c_re.astype(jnp.float32))
            - jnp.einsum('blgp,gcp->blgc', si, c_im.astype(jnp.float32)))


def s5_mixer(h, lam_re, lam_im, log_dt, b_re, b_im, c_re, c_im, d_skip, w_glu, w_out):
    bsz, L, _ = h.shape
    hf = h.astype(jnp.float32)
    u = hf.reshape(bsz, L, S5_GROUPS, S5_GROUP_CH)
    y = (s5_direction(u, lam_re[0], lam_im[0], log_dt[0], b_re[0], b_im[0], c_re[0], c_im[0], False)
         + s5_direction(u, lam_re[1], lam_im[1], log_dt[1], b_re[1], b_im[1], c_re[1], c_im[1], True))
    y = y.reshape(bsz, L, D_MODEL) + d_skip.astype(jnp.float32) * hf
    g = jax.nn.gelu(y, approximate=False).astype(h.dtype)
    z = g * jax.nn.sigmoid(g @ w_glu)
    return z @ w_out


def axial_rope_tables(n_real):
    rows = n_real // GRID_W
    row_ids = jnp.repeat(jnp.arange(rows, dtype=jnp.int32), GRID_W)
    col_ids = jnp.tile(jnp.arange(GRID_W, dtype=jnp.int32), rows)
    pad = jnp.zeros((N_META,), jnp.int32)
    row_ids = jnp.concatenate([pad, row_ids]).astype(jnp.float32)
    col_ids = jnp.concatenate([pad, col_ids]).astype(jnp.float32)
    inv_freq = ROPE_THETA ** (-jnp.arange(0, ROPE_AXIS_DIM, 2, dtype=jnp.float32) / ROPE_AXIS_DIM)
    ang_r = row_ids[:, None] * inv_freq[None, :]
    ang_c = col_ids[:, None] * inv_freq[None, :]
    return jnp.cos(ang_r), jnp.sin(ang_r), jnp.cos(ang_c), jnp.sin(ang_c)


def _rotate_half_block(xh, cos, sin):
    half = ROPE_AXIS_DIM // 2
    x1, x2 = xh[..., :half], xh[..., half:]
    c = cos[None, :, None, :]
    s = sin[None, :, None, :]
    return jnp.concatenate([x1 * c - x2 * s, x2 * c + x1 * s], axis=-1)


def rms_axial_rope(t, gain, cos_r, sin_r, cos_c, sin_c):
    tf = t.astype(jnp.float32)
    tf = tf * lax.rsqrt(jnp.mean(jnp.square(tf), axis=-1, keepdims=True) + QK_EPS) * gain.astype(jnp.float32)
    return jnp.concatenate([_rotate_half_block(tf[..., :ROPE_AXIS_DIM], cos_r, sin_r),
                            _rotate_half_block(tf[..., ROPE_AXIS_DIM:], cos_c, sin_c)], axis=-1)


def _attend(q, k, v):
    s = jnp.einsum('bqkgd,bskd->bkgqs', q, k).astype(jnp.float32) * (HEAD_DIM ** -0.5)
    p = jax.nn.softmax(s, axis=-1).astype(v.dtype)
    return jnp.einsum('bkgqs,bskd->bqkgd', p, v)


def gqa_mixer(h, w_qkv, q_gain, k_gain, w_out, cos_r, sin_r, cos_c, sin_c):
    bsz, L, _ = h.shape
    qkv = h @ w_qkv
    nq, nk = N_Q_HEADS * HEAD_DIM, N_KV_HEADS * HEAD_DIM
    q = qkv[..., :nq].reshape(bsz, L, N_Q_HEADS, HEAD_DIM)
    k = qkv[..., nq:nq + nk].reshape(bsz, L, N_KV_HEADS, HEAD_DIM)
    v = qkv[..., nq + nk:].reshape(bsz, L, N_KV_HEADS, HEAD_DIM)
    q = rms_axial_rope(q, q_gain, cos_r, sin_r, cos_c, sin_c).astype(v.dtype)
    k = rms_axial_rope(k, k_gain, cos_r, sin_r, cos_c, sin_c).astype(v.dtype)
    q = q.reshape(bsz, L, N_KV_HEADS, Q_PER_KV, HEAD_DIM)
    out_meta = _attend(q[:, :N_META], k, v)
    n_real = L - N_META
    n_blk = n_real // QUERY_BLOCK
    qb = q[:, N_META:].reshape(bsz, n_blk, QUERY_BLOCK, N_KV_HEADS, Q_PER_KV, HEAD_DIM).swapaxes(0, 1)
    out_real = lax.map(lambda blk: _attend(blk, k, v), qb)
    out_real = out_real.swapaxes(0, 1).reshape(bsz, n_real, N_KV_HEADS, Q_PER_KV, HEAD_DIM)
    out = jnp.concatenate([out_meta, out_real], axis=1).reshape(bsz, L, D_MODEL)
    return out @ w_out


def setup_inputs(seed: int = 0) -> dict:
    key = jax.random.key(seed)
    ks = jax.random.split(key, 24)
    f32 = jnp.float32
    D, G, P, C = D_MODEL, S5_GROUPS, S5_STATE, S5_GROUP_CH
    nS, nA = N_S5_LAYERS, N_ATTN_LAYERS
    x = jax.random.normal(ks[0], (BATCH, SEQ, D), f32)
    meta_tokens = jax.random.normal(ks[1], (N_META, D), f32)
    s5_lambda_re = -0.5 * jnp.exp(0.02 * jax.random.normal(ks[2], (nS, 2, G, P), f32))
    s5_lambda_im = (math.pi * jnp.arange(P, dtype=f32))[None, None, None, :] \
        + 0.01 * jax.random.normal(ks[3], (nS, 2, G, P), f32)
    s5_log_dt = jax.random.uniform(ks[4], (nS, 2, G), f32,
                                   minval=math.log(S5_DT_MIN), maxval=math.log(S5_DT_MAX))
    b_scale = (2.0 * C) ** -0.5
    s5_b_re = jax.random.normal(ks[5], (nS, 2, G, P, C), f32) * b_scale
    s5_b_im = jax.random.normal(ks[6], (nS, 2, G, P, C), f32) * b_scale
    c_scale = (2.0 * P) ** -0.5
    s5_c_re = jax.random.normal(ks[7], (nS, 2, G, C, P), f32) * c_scale
    s5_c_im = jax.random.normal(ks[8], (nS, 2, G, C, P), f32) * c_scale
    s5_d = jax.random.normal(ks[9], (nS, D), f32)
    s5_w_glu = jax.random.normal(ks[10], (nS, D, D), f32) * D ** -0.5
    s5_w_out = jax.random.normal(ks[11], (nS, D, D), f32) * (D ** -0.5 * DEEPNORM_BETA)
    w_qk = jax.random.normal(ks[12], (nA, D, (N_Q_HEADS + N_KV_HEADS) * HEAD_DIM), f32) * D ** -0.5
    w_v = jax.random.normal(ks[13], (nA, D, N_KV_HEADS * HEAD_DIM), f32) * (D ** -0.5 * DEEPNORM_BETA)
    attn_w_qkv = jnp.concatenate([w_qk, w_v], axis=-1)
    attn_q_gain = 1.0 + 0.02 * jax.random.normal(ks[14], (nA, HEAD_DIM), f32)
    attn_k_gain = 1.0 + 0.02 * jax.random.normal(ks[15], (nA, HEAD_DIM), f32)
    attn_w_out = jax.random.normal(ks[16], (nA, D, D), f32) * (D ** -0.5 * DEEPNORM_BETA)
    ffn_w_gate = jax.random.normal(ks[17], (DEPTH, D, D_FF), f32) * D ** -0.5
    ffn_w_up = jax.random.normal(ks[18], (DEPTH, D, D_FF), f32) * D ** -0.5
    ffn_w_down = jax.random.normal(ks[19], (DEPTH, D_FF, D), f32) * (D_FF ** -0.5 * DEEPNORM_BETA)
    ln_gain = 1.0 + 0.02 * jax.random.normal(ks[20], (DEPTH, 2, D), f32)
    ln_bias = 0.02 * jax.random.normal(ks[21], (DEPTH, 2, D), f32)
    return {"x": x, "meta_tokens": meta_tokens,
            "s5_lambda_re": s5_lambda_re, "s5_lambda_im": s5_lambda_im, "s5_log_dt": s5_log_dt,
            "s5_b_re": s5_b_re, "s5_b_im": s5_b_im, "s5_c_re": s5_c_re, "s5_c_im": s5_c_im,
            "s5_d": s5_d, "s5_w_glu": s5_w_glu, "s5_w_out": s5_w_out,
            "attn_w_qkv": attn_w_qkv, "attn_q_gain": attn_q_gain, "attn_k_gain": attn_k_gain,
            "attn_w_out": attn_w_out,
            "ffn_w_gate": ffn_w_gate, "ffn_w_up": ffn_w_up, "ffn_w_down": ffn_w_down,
            "ln_gain": ln_gain, "ln_bias": ln_bias}


def reference(x, meta_tokens, s5_lambda_re, s5_lambda_im, s5_log_dt, s5_b_re, s5_b_im,
              s5_c_re, s5_c_im, s5_d, s5_w_glu, s5_w_out, attn_w_qkv, attn_q_gain,
              attn_k_gain, attn_w_out, ffn_w_gate, ffn_w_up, ffn_w_down, ln_gain, ln_bias):
    bsz, n_real, d = x.shape
    meta = jnp.broadcast_to(meta_tokens.astype(x.dtype)[None], (bsz, N_META, d))
    h = jnp.concatenate([meta, x], axis=1)
    cos_r, sin_r, cos_c, sin_c = axial_rope_tables(n_real)
    for i in range(DEPTH):
        j = i // N_MIXERS
        if i % N_MIXERS == 0:
            mix = s5_mixer(h, s5_lambda_re[j], s5_lambda_im[j], s5_log_dt[j], s5_b_re[j], s5_b_im[j],
                           s5_c_re[j], s5_c_im[j], s5_d[j], s5_w_glu[j], s5_w_out[j])
        else:
            mix = gqa_mixer(h, attn_w_qkv[j], attn_q_gain[j], attn_k_gain[j], attn_w_out[j],
                            cos_r, sin_r, cos_c, sin_c)
        h = layer_norm(DEEPNORM_ALPHA * h + mix, ln_gain[i, 0], ln_bias[i, 0])
        h = layer_norm(DEEPNORM_ALPHA * h + swiglu_ffn(h, ffn_w_gate[i], ffn_w_up[i], ffn_w_down[i]),
                       ln_gain[i, 1], ln_bias[i, 1])
    return h[:, N_META:]
```

```python
import numpy as np
from contextlib import ExitStack
import concourse.bass as bass
import concourse.mybir as mybir
from concourse.bass_utils import run_bass_kernel_spmd
from concourse.alu_op_type import AluOpType as ALU

F32 = mybir.dt.float32
BF16 = mybir.dt.bfloat16
AF = mybir.ActivationFunctionType
AX = mybir.AxisListType

ENGS = ["pe", "dve", "act", "pool", "sp"]
N_DMA_SEMS = 8


class Prog:
    def __init__(self, name="k"):
        self.nc = bass.Bass("TRN2", target_bir_lowering=False)
        self.es = ExitStack()
        self.ins = []
        self.last_w = {}
        self.readers = {}
        self.dram_in = {}
        self.dram_out = {}

    def dram(self, name, shape, dtype=F32, kind="ExternalInput"):
        t = self.nc.dram_tensor(name, list(shape), dtype, kind=kind)
        return t.ap()

    def sbuf(self, name, shape, dtype=F32):
        return self.es.enter_context(self.nc.sbuf_tensor("sb_" + name, list(shape), dtype))

    def psum(self, name, shape, dtype=F32):
        return self.es.enter_context(self.nc.psum_tensor("pm_" + name, list(shape), dtype))

    def op(self, eng, fn, r=(), w=(), dma=False):
        deps = set()
        for k in r:
            if k in self.last_w:
                deps.add(self.last_w[k])
        for k in w:
            if k in self.last_w:
                deps.add(self.last_w[k])
            for i in self.readers.get(k, ()):
                deps.add(i)
        idx = len(self.ins)
        self.ins.append(dict(eng=eng, fn=fn, deps=deps, dma=dma))
        for k in w:
            self.last_w[k] = idx
            self.readers[k] = []
        for k in r:
            self.readers.setdefault(k, []).append(idx)
        return idx

    def dma(self, out, in_, r=(), w=(), q="sp", **kw):
        return self.op(q, lambda e: e.dma_start(out=out, in_=in_, **kw), r, w, dma=True)

    def build(self):
        nc = self.nc
        ins = self.ins
        target = [False] * len(ins)
        for rec in ins:
            for d in rec["deps"]:
                if rec["eng"] == "pe" and ins[d]["eng"] == "pe":
                    continue
                target[d] = True
        sems = {e: self.es.enter_context(nc.semaphore("s_" + e)) for e in ENGS}
        dsems = [self.es.enter_context(nc.semaphore("d%d" % i)) for i in range(N_DMA_SEMS)]
        cnt = {e: 0 for e in ENGS}
        dcnt = [0] * N_DMA_SEMS
        dnext = 0
        seen = {e: {} for e in ENGS}
        sig = [None] * len(ins)
        progs = {e: [] for e in ENGS}
        for i, rec in enumerate(ins):
            e = rec["eng"]
            waits = []
            for d in sorted(rec["deps"]):
                if e == "pe" and ins[d]["eng"] == "pe":
                    continue
                sh, sn, val = sig[d]
                if seen[e].get(sn, 0) < val:
                    seen[e][sn] = val
                    waits.append((sh, val))
            inc = None
            if rec["dma"]:
                j = dnext
                dnext = (dnext + 1) % N_DMA_SEMS
                if dcnt[j] > 0 and seen[e].get("d%d" % j, 0) < dcnt[j]:
                    seen[e]["d%d" % j] = dcnt[j]
                    waits.append((dsems[j], dcnt[j]))
                dcnt[j] += 16
                sig[i] = (dsems[j], "d%d" % j, dcnt[j])
                inc = (dsems[j], 16)
            elif target[i]:
                cnt[e] += 1
                sig[i] = (sems[e], "s_" + e, cnt[e])
                inc = (sems[e], 1)
                seen[e]["s_" + e] = max(seen[e].get("s_" + e, 0), 0)
            else:
                sig[i] = (None, None, 0)
            progs[e].append((waits, rec["fn"], inc))
        self.final = {e: cnt[e] for e in ENGS}
        self.max_sem = max(list(cnt.values()) + dcnt)

        def runner(ename):
            def body(eng):
                for waits, fn, inc in progs[ename]:
                    for sh, val in waits:
                        eng.wait_ge(sh, val)
                    if fn is None:
                        if inc is None:
                            continue
                        r = eng.nop()
                    else:
                        r = fn(eng)
                    if inc is not None:
                        r.then_inc(inc[0], inc[1])
            return body

        with nc.Block() as block:
            block.tensor(runner("pe"))
            block.vector(runner("dve"))
            block.scalar(runner("act"))
            block.gpsimd(runner("pool"))
            block.sync(runner("sp"))
        self.es.close()
        return nc


def _fence(self):
    lasts = set()
    seen_e = {}
    for i, rec in enumerate(self.ins):
        if rec["dma"]:
            lasts.add(i)
        else:
            seen_e[rec["eng"]] = i
    start = getattr(self, "_fence_at", 0)
    deps = set(i for i in lasts if i >= start) | set(seen_e.values())
    self._fence_at = len(self.ins)
    for e in ENGS:
        self.ins.append(dict(eng=e, fn=None, deps=set(deps), dma=False))
    self.last_w = {}
    self.readers = {}


Prog.fence = _fence


D = 1024
DFF = 2816
NT = 17
NR = NT * 128
ALPHA = 8.0 ** 0.25
LN_EPS = 1e-5
FG = 512


class Ctx:
    pass


def setup_common(p):
    c = Ctx()
    c.identf = p.sbuf("identf", [128, 128], F32)
    c.identb = p.sbuf("identb", [128, 128], BF16)
    p.op("pool", lambda e: e.memset(c.identf[:], 0.0), w=["identf"])
    p.op("pool", lambda e: e.affine_select(out=c.identf[:], in_=c.identf[:], pattern=[[-1, 128]],
                                           compare_op=ALU.not_equal, fill=1.0, base=0, channel_multiplier=1),
         r=["identf"], w=["identf"])
    p.op("dve", lambda e: e.tensor_copy(out=c.identb[:], in_=c.identf[:]), r=["identf"], w=["identb"])
    c.ps = [p.psum("ps%d" % i, [128, 512], F32) for i in range(8)]
    c.epsc = p.sbuf("epsc", [128, 1], F32)
    p.op("pool", lambda e: e.memset(c.epsc[:], LN_EPS), w=["epsc"])
    return c


def make_T(p, c, src, dstT, nt, name, psb=(0, 1), nparts=128):
    tmp = [p.sbuf("%s_cb%d" % (name, i), [128, 1024], BF16) for i in range(2)]
    for t in range(nt):
        tb = tmp[t % 2]
        tk = "%s_cb%d" % (name, t % 2)
        eng = "act" if t % 2 == 0 else "pool"
        if eng == "act":
            p.op("act", lambda e, t=t, tb=tb: e.activation(out=tb[:], in_=src[:, t, :], func=AF.Copy),
                 r=["%s:%d" % (name, t)], w=[tk])
        else:
            p.op("pool", lambda e, t=t, tb=tb: e.tensor_copy(out=tb[:], in_=src[:, t, :]),
                 r=["%s:%d" % (name, t)], w=[tk])
        pb = c.ps[psb[t % 2]]
        pk = "ps%d" % psb[t % 2]
        pbb = pb[:].bitcast(BF16)
        for k in range(8):
            p.op("pe", lambda e, k=k, tb=tb, pbb=pbb: e.transpose(out=pbb[:, k * 128:(k + 1) * 128],
                                                                 in_=tb[:, k * 128:(k + 1) * 128], identity=c.identb[:]),
                 r=[tk, "identb"], w=[pk + ":%d" % k])
        p.op("dve", lambda e, t=t, pbb=pbb: e.tensor_copy(
            out=dstT[:, :, t * 128:(t + 1) * 128], in_=pbb.rearrange("p (k n) -> p k n", k=8)),
            r=[pk + ":%d" % k for k in range(8)], w=["%sT:%d" % (name, t)])


_LN_UID = [0]


def layernorm(p, c, h, nt, name, gb, bb, lname):
    _LN_UID[0] += 1
    u = "%d" % _LN_UID[0]
    st = p.sbuf(lname + "_st" + u, [128, 2, 12], F32)
    mv = p.sbuf(lname + "_mv" + u, [128, 2, 2], F32)
    rs = p.sbuf(lname + "_rs" + u, [128, 2, 1], F32)
    for t in range(nt):
        b = t % 2
        hk = "%s:%d" % (name, t)
        sk = "%s_st%d" % (lname, b)
        p.op("dve", lambda e, t=t, b=b: e.bn_stats(out=st[:, b, 0:6], in_=h[:, t, 0:512]), r=[hk], w=[sk + "a"])
        p.op("dve", lambda e, t=t, b=b: e.bn_stats(out=st[:, b, 6:12], in_=h[:, t, 512:1024]), r=[hk], w=[sk + "b"])
        p.op("dve", lambda e, b=b: e.bn_aggr(out=mv[:, b, :], in_=st[:, b, :]), r=[sk + "a", sk + "b"], w=[lname + "_mv%d" % b])
        p.op("act", lambda e, b=b: e.activation(out=rs[:, b, :], in_=mv[:, b, 1:2], func=AF.Sqrt, bias=c.epsc[:], scale=1.0),
             r=[lname + "_mv%d" % b, "epsc"], w=[lname + "_sd%d" % b])
        p.op("dve", lambda e, b=b: e.reciprocal(out=rs[:, b, :], in_=rs[:, b, :]), r=[lname + "_sd%d" % b], w=[lname + "_sd%d" % b])
        p.op("dve", lambda e, t=t, b=b: e.tensor_scalar(out=h[:, t, :], in0=h[:, t, :], scalar1=mv[:, b, 0:1], scalar2=rs[:, b, :],
                                                       op0=ALU.subtract, op1=ALU.mult),
             r=[hk, lname + "_mv%d" % b, lname + "_sd%d" % b], w=[hk])
        p.op("pool", lambda e, t=t: e.tensor_tensor(out=h[:, t, :], in0=h[:, t, :], in1=gb[:], op=ALU.mult), r=[hk, lname + "_g"], w=[hk])
        p.op("pool", lambda e, t=t: e.tensor_tensor(out=h[:, t, :], in0=h[:, t, :], in1=bb[:], op=ALU.add), r=[hk, lname + "_b"], w=[hk])


def ffn(p, c, h, hT, nt, name, wg, wu, wd, bufs):
    wgb, wub, wdb, actT, sg = bufs
    nrows = nt * 128
    wgv = wg.rearrange("(k p) f -> p k f", p=128)
    wuv = wu.rearrange("(k p) f -> p k f", p=128)
    wdv = wd.rearrange("(c p) d -> p c d", p=128)
    ngroups = (DFF + FG - 1) // FG
    blocks = [(s, min(512, nrows - s)) for s in range(0, nrows, 512)]
    it = 0
    for g in range(ngroups):
        f0 = g * FG
        fw = min(FG, DFF - f0)
        nfc = fw // 128
        b = g % 2
        p.dma(wgb[b][:, :, 0:fw], wgv[:, :, f0:f0 + fw], w=["wg%d" % b], q="pool")
        p.dma(wub[b][:, :, 0:fw], wuv[:, :, f0:f0 + fw], w=["wu%d" % b], q="pool")
        p.dma(wdb[b][:, 0:nfc, :], wdv[:, f0 // 128:f0 // 128 + nfc, :], w=["wd%d" % b], q="pool")
        for fc in range(nfc):
            for (s, n) in blocks:
                tiles = list(range(s // 128, (s + n) // 128))
                pg, pu = (4, 5) if it % 2 == 0 else (6, 7)
                sgi = it % 2
                it += 1
                for k in range(8):
                    p.op("pe", lambda e, k=k, b=b, fc=fc, s=s, n=n, pg=pg: e.matmul(
                        c.ps[pg][:, 0:n], lhsT=wgb[b][:, k, fc * 128:(fc + 1) * 128], rhs=hT[:, k, s:s + n],
                        start=(k == 0), stop=(k == 7)),
                        r=["wg%d" % b] + ["%sT:%d" % (name, t) for t in tiles], w=["ps%d" % pg])
                for k in range(8):
                    p.op("pe", lambda e, k=k, b=b, fc=fc, s=s, n=n, pu=pu: e.matmul(
                        c.ps[pu][:, 0:n], lhsT=wub[b][:, k, fc * 128:(fc + 1) * 128], rhs=hT[:, k, s:s + n],
                        start=(k == 0), stop=(k == 7)),
                        r=["wu%d" % b] + ["%sT:%d" % (name, t) for t in tiles], w=["ps%d" % pu])
                p.op("act", lambda e, n=n, pg=pg, sgi=sgi: e.activation(out=sg[sgi][:, 0:n], in_=c.ps[pg][:, 0:n], func=AF.Silu),
                     r=["ps%d" % pg], w=["sg%d" % sgi])
                p.op("dve", lambda e, n=n, s=s, fc=fc, pu=pu, sgi=sgi: e.tensor_tensor(
                    out=actT[:, fc, s:s + n], in0=sg[sgi][:, 0:n], in1=c.ps[pu][:, 0:n], op=ALU.mult),
                    r=["sg%d" % sgi, "ps%d" % pu], w=["actT:%d:%d" % (fc, t) for t in tiles])
        for t in range(nt):
            for half in range(2):
                pd = 2 + (t * 2 + half) % 2
                for fc in range(nfc):
                    p.op("pe", lambda e, t=t, half=half, fc=fc, b=b, pd=pd, nfc=nfc: e.matmul(
                        c.ps[pd][:, :], lhsT=actT[:, fc, t * 128:(t + 1) * 128], rhs=wdb[b][:, fc, half * 512:(half + 1) * 512],
                        start=(fc == 0), stop=(fc == nfc - 1)),
                        r=["actT:%d:%d" % (fc, t), "wd%d" % b], w=["ps%d" % pd])
                hk = "%s:%d:%d" % (name, t, half)
                if g == 0:
                    p.op("dve", lambda e, t=t, half=half, pd=pd: e.scalar_tensor_tensor(
                        out=h[:, t, half * 512:(half + 1) * 512], in0=h[:, t, half * 512:(half + 1) * 512], scalar=ALPHA,
                        in1=c.ps[pd][:, :], op0=ALU.mult, op1=ALU.add),
                        r=["ps%d" % pd, "%s:%d" % (name, t)], w=[hk])
                else:
                    p.op("dve", lambda e, t=t, half=half, pd=pd: e.tensor_tensor(
                        out=h[:, t, half * 512:(half + 1) * 512], in0=h[:, t, half * 512:(half + 1) * 512],
                        in1=c.ps[pd][:, :], op=ALU.add),
                        r=["ps%d" % pd, hk], w=[hk])
    for t in range(nt):
        p.op("pool", None, r=["%s:%d:0" % (name, t), "%s:%d:1" % (name, t)], w=["%s:%d" % (name, t)])


def alloc_ffn_bufs(p, nt):
    wgb = [p.sbuf("wgb%d" % i, [128, 8, FG], BF16) for i in range(2)]
    wub = [p.sbuf("wub%d" % i, [128, 8, FG], BF16) for i in range(2)]
    wdb = [p.sbuf("wdb%d" % i, [128, FG // 128, 1024], BF16) for i in range(2)]
    actT = p.sbuf("actT", [128, FG // 128, nt * 128], BF16)
    sg = [p.sbuf("sg%d" % i, [128, 512], BF16) for i in range(2)]
    return wgb, wub, wdb, actT, sg


QK_EPS = 1e-6
NQK = 1280


def build_qkv():
    nt = NT
    p = Prog()
    c = setup_common(p)
    hin = p.dram("hin", [NR, D])
    wqkv = p.dram("wqkv", [D, 1536])
    gain = p.dram("gain", [NQK])
    ctab = p.dram("ctab", [NR, NQK])
    stab = p.dram("stab", [NR, NQK])
    out = p.dram("qkv", [NR, 1536], BF16, kind="ExternalOutput")
    h = p.sbuf("h", [128, nt, D], F32)
    hT = p.sbuf("hT", [128, 8, NR], BF16)
    wsb = p.sbuf("wsb", [128, 8, 1536], BF16)
    gb = p.sbuf("gainb", [128, NQK], F32)
    qeps = p.sbuf("qeps", [128, 1], F32)
    p.op("pool", lambda e: e.memset(qeps[:], QK_EPS), w=["qeps"])
    hv = hin.rearrange("(t p) d -> p t d", p=128)
    for t in range(nt):
        p.dma(h[:, t, :], hv[:, t, :], w=["h:%d" % t], q="sp")
    p.dma(gb[:], gain.partition_broadcast(128), w=["gainb"])
    wv = wqkv.rearrange("(k p) f -> p k f", p=128)
    for k in range(8):
        p.dma(wsb[:, k, :], wv[:, k, :], w=["wsb:%d" % k], q="pool")
    make_T(p, c, h, hT, nt, "h")
    xq = [p.sbuf("xq%d" % i, [128, 1536], F32) for i in range(2)]
    sq = [p.sbuf("sq%d" % i, [128, NQK], F32) for i in range(2)]
    t1 = [p.sbuf("t1%d" % i, [128, NQK], F32) for i in range(2)]
    ct = [p.sbuf("ct%d" % i, [128, NQK], F32) for i in range(2)]
    stt = [p.sbuf("stt%d" % i, [128, NQK], F32) for i in range(2)]
    ss = [p.sbuf("ss%d" % i, [128, 20], F32) for i in range(2)]
    ob = [p.sbuf("ob%d" % i, [128, 1536], BF16) for i in range(2)]
    cv = ctab.rearrange("(t p) d -> p t d", p=128)
    sv = stab.rearrange("(t p) d -> p t d", p=128)
    ov = out.rearrange("(t p) d -> p t d", p=128)
    for t in range(nt):
        b = t % 2
        B = "%d" % b
        p.dma(ct[b][:], cv[:, t, :], w=["ct" + B], q="sp")
        p.dma(stt[b][:], sv[:, t, :], w=["stt" + B], q="sp")
        for cb in range(3):
            pb = 2 + (t * 3 + cb) % 6
            for k in range(8):
                p.op("pe", lambda e, k=k, t=t, cb=cb, pb=pb: e.matmul(
                    c.ps[pb][:, :], lhsT=hT[:, k, t * 128:(t + 1) * 128], rhs=wsb[:, k, cb * 512:(cb + 1) * 512],
                    start=(k == 0), stop=(k == 7)), r=["hT:%d" % t] + ["wsb:%d" % kk for kk in range(8)], w=["ps%d" % pb])
            p.op("act", lambda e, cb=cb, pb=pb, b=b: e.activation(out=xq[b][:, cb * 512:(cb + 1) * 512], in_=c.ps[pb][:, :], func=AF.Copy),
                 r=["ps%d" % pb], w=["xq%s:%d" % (B, cb)])
        xk = ["xq%s:%d" % (B, cb) for cb in range(3)]
        x = xq[b]
        p.op("dve", lambda e, b=b, x=x: e.tensor_tensor(out=sq[b][:], in0=x[:, 0:NQK], in1=x[:, 0:NQK], op=ALU.mult), r=xk, w=["sq" + B])
        p.op("dve", lambda e, b=b: e.tensor_reduce(out=ss[b][:], in_=sq[b][:].rearrange("p (h d) -> p h d", d=64), axis=AX.X, op=ALU.add),
             r=["sq" + B], w=["ss" + B])
        p.op("act", lambda e, b=b: e.activation(out=ss[b][:], in_=ss[b][:], func=AF.Sqrt, bias=qeps[:], scale=1.0 / 64),
             r=["ss" + B, "qeps"], w=["ss" + B])
        p.op("dve", lambda e, b=b: e.reciprocal(out=ss[b][:], in_=ss[b][:]), r=["ss" + B], w=["ss" + B])
        p.op("pool", lambda e, b=b, x=x: e.tensor_tensor(out=sq[b][:].rearrange("p (h d) -> p h d", d=64),
                                                       in0=x[:, 0:NQK].rearrange("p (h d) -> p h d", d=64),
                                                       in1=ss[b][:].unsqueeze(2).to_broadcast([128, 20, 64]), op=ALU.mult),
             r=xk + ["ss" + B, "sq" + B], w=["sq" + B])
        p.op("pool", lambda e, b=b: e.tensor_tensor(out=sq[b][:], in0=sq[b][:], in1=gb[:], op=ALU.mult), r=["sq" + B, "gainb"], w=["sq" + B])
        p.op("dve", lambda e, b=b: e.tensor_tensor(out=t1[b][:], in0=sq[b][:], in1=ct[b][:], op=ALU.mult), r=["sq" + B, "ct" + B], w=["t1" + B])
        xv4 = sq[b][:].rearrange("p (m two d) -> p m two d", two=2, d=16)
        sv4 = stt[b][:].rearrange("p (m two d) -> p m two d", two=2, d=16)
        cv4 = ct[b][:].rearrange("p (m two d) -> p m two d", two=2, d=16)
        p.op("pool", lambda e, xv4=xv4, sv4=sv4, cv4=cv4: e.tensor_tensor(out=cv4[:, :, 0, :], in0=xv4[:, :, 1, :], in1=sv4[:, :, 0, :], op=ALU.mult),
             r=["sq" + B, "stt" + B, "t1" + B], w=["ct" + B + "a"])
        p.op("pool", lambda e, xv4=xv4, sv4=sv4, cv4=cv4: e.tensor_tensor(out=cv4[:, :, 1, :], in0=xv4[:, :, 0, :], in1=sv4[:, :, 1, :], op=ALU.mult),
             r=["sq" + B, "stt" + B, "t1" + B], w=["ct" + B + "b"])
        p.op("dve", lambda e, b=b: e.tensor_tensor(out=ob[b][:, 0:NQK], in0=t1[b][:], in1=ct[b][:], op=ALU.add),
             r=["t1" + B, "ct" + B + "a", "ct" + B + "b"], w=["ob" + B + "q", "ct" + B])
        p.op("act", lambda e, b=b, x=x: e.activation(out=ob[b][:, NQK:1536], in_=x[:, NQK:1536], func=AF.Copy), r=xk, w=["ob" + B + "v"])
        p.dma(ov[:, t, :], ob[b][:], r=["ob" + B + "q", "ob" + B + "v"], w=["out:%d" % t], q="sp")
    p.op("sp", None, r=["out:%d" % t for t in range(nt)])
    return p.build()


DBG = {}
NK = 8320
NKT = 65
ARENA = 34816


def build_post(mode):
    nt = NT
    p = Prog()
    c = setup_common(p)
    hin = p.dram("hin", [NR, D])
    lnp = p.dram("lnp", [4, D])
    wg = p.dram("wg", [D, DFF]); wu = p.dram("wu", [D, DFF]); wd = p.dram("wd", [DFF, D])
    wo = p.dram("wo", [8, 128, D])
    hout = p.dram("hout", [NR, D], kind="ExternalOutput")
    h = p.sbuf("h", [128, nt, D], F32)
    A = p.sbuf("arenaA", [128, 8, NR], BF16)
    B = p.sbuf("arenaB", [128, ARENA], BF16)
    gb = p.sbuf("gb", [128, D], F32); bb = p.sbuf("bb", [128, D], F32)
    hv = hin.rearrange("(t p) d -> p t d", p=128)
    for t in range(nt):
        p.dma(h[:, t, :], hv[:, t, :], w=["h:%d" % t], q="sp")
    off = [0]

    def carve(n):
        a = B[:, off[0]:off[0] + n]
        off[0] += n
        return a

    if mode == "att":
        qT_d = p.dram("qT", [4, 64, 17 * 512], BF16)
        kT_d = p.dram("kT", [4, 64, NK], BF16)
        vA_d = p.dram("vA", [4, 128, NKT * 128], BF16)
        om_d = p.dram("omask", [2, 128, 128], BF16)
        kT = carve(NK); vA = carve(NKT * 128).rearrange("p (k f) -> p k f", f=128); qT = carve(17 * 512)
        pT = [carve(512), carve(512)]
        rb = p.sbuf("rb", [128, 512], F32)
        wo_sb = carve(8192).rearrange("p (k f) -> p k f", k=8)
        om = p.sbuf("om", [128, 2, 128], BF16)
        p.dma(om[:, 0, :], om_d[0, :, :], w=["om"], q="sp")
        p.dma(om[:, 1, :], om_d[1, :, :], w=["om"], q="sp")
        for k in range(8):
            p.dma(wo_sb[:, k, :], wo[k, :, :], w=["wo:%d" % k], q="pool")
        oT = A
        it = 0
        for g in range(DBG.get('ng', 4)):
            p.dma(kT[0:64, :], kT_d[g, :, :], w=["kT"], q="sp")
            p.dma(vA[:, :, :], vA_d[g, :, :].rearrange("p (k f) -> p k f", f=128), w=["vA"], q="sp")
            p.dma(qT[0:64, :], qT_d[g, :, :], w=["qT"], q="sp")
            for qb in range(DBG.get('nqb', 17)):
                ob = 4 + (g * 17 + qb) % 2
                db = 6 + (g * 17 + qb) % 2
                for kt in range(NKT):
                    sb = 2 + it % 2
                    pi = it % 2
                    it += 1
                    oi = 0 if kt < NKT - 1 else 1
                    p.op("pe", lambda e, kt=kt, qb=qb, sb=sb: e.matmul(
                        c.ps[sb][:, :], lhsT=kT[0:64, kt * 128:(kt + 1) * 128], rhs=qT[0:64, qb * 512:(qb + 1) * 512],
                        start=True, stop=True), r=["kT", "qT"], w=["ps%d" % sb])
                    p.op("act", lambda e, sb=sb, pi=pi: e.activation(out=pT[pi][:, :], in_=c.ps[sb][:, :], func=AF.Exp, scale=0.125),
                         r=["ps%d" % sb], w=["pT%d" % pi])
                    p.op("pe", lambda e, kt=kt, ob=ob, pi=pi: e.matmul(
                        c.ps[ob][:, :], lhsT=vA[:, kt, :], rhs=pT[pi][:, :],
                        start=(kt == 0), stop=(kt == NKT - 1)), r=["vA", "pT%d" % pi], w=["ps%d" % ob])
                    p.op("pe", lambda e, kt=kt, db=db, pi=pi, oi=oi: e.matmul(
                        c.ps[db][:, :], lhsT=om[:, oi, :], rhs=pT[pi][:, :],
                        start=(kt == 0), stop=(kt == NKT - 1)), r=["om", "pT%d" % pi], w=["ps%d" % db])
                O = "ps%d" % ob
                Dk = "ps%d" % db
                p.op("act", lambda e, db=db: e.activation(out=rb[:, :], in_=c.ps[db][:, :], func=AF.Ln), r=[Dk], w=["rb"])
                p.op("act", lambda e: e.activation(out=rb[:, :], in_=rb[:, :], func=AF.Exp, scale=-1.0), r=["rb"], w=["rb"])
                p.op("dve", lambda e, g=g, qb=qb, ob=ob: e.tensor_tensor(
                    out=oT[0:64, 2 * g:2 * g + 2, qb * 128:(qb + 1) * 128],
                    in0=c.ps[ob][0:64, 0:256].rearrange("p (j n) -> p j n", j=2),
                    in1=rb[0:64, 0:256].rearrange("p (j n) -> p j n", j=2), op=ALU.mult),
                    r=[O, "rb"], w=["xT:%d" % qb, O + "a"])
                p.op("dve", lambda e, g=g, qb=qb, ob=ob: e.tensor_tensor(
                    out=oT[64:128, 2 * g:2 * g + 2, qb * 128:(qb + 1) * 128],
                    in0=c.ps[ob][64:128, 256:512].rearrange("p (j n) -> p j n", j=2),
                    in1=rb[64:128, 256:512].rearrange("p (j n) -> p j n", j=2), op=ALU.mult),
                    r=[O, "rb"], w=["xT:%d" % qb, O + "b"])
                p.op("dve", None, r=[O + "a", O + "b"], w=[O])
        xT = oT
        w_sb = wo_sb
    else:
        yf_d = p.dram("yf", [NR, D]); yb_d = p.dram("yb", [NR, D])
        dsk = p.dram("dskip", [D])
        wglu = p.dram("wglu", [8, 128, D])
        gT = A
        zT = carve(8 * NR).rearrange("p (k n) -> p k n", k=8)
        wglu_sb = carve(8192).rearrange("p (k f) -> p k f", k=8)
        wo_sb = carve(8192).rearrange("p (k f) -> p k f", k=8)
        sgs = [carve(512), carve(512)]
        for k in range(8):
            p.dma(wglu_sb[:, k, :], wglu[k, :, :], w=["wglu:%d" % k], q="pool")
            p.dma(wo_sb[:, k, :], wo[k, :, :], w=["wo:%d" % k], q="pool")
        dB = gb
        p.dma(dB[:], dsk.partition_broadcast(128), w=["ln_g"])
        yt = [p.sbuf("yt%d" % i, [128, D], F32) for i in range(2)]
        _yt2 = p.sbuf("yt2", [128, D], F32)
        yt2 = [_yt2, _yt2]
        _g16 = p.sbuf("g16", [128, D], BF16)
        g16 = [_g16, _g16]
        yfv = yf_d.rearrange("(t p) d -> p t d", p=128); ybv = yb_d.rearrange("(t p) d -> p t d", p=128)
        for t in range(nt):
            b = t % 2
            Bk = "%d" % b
            p.dma(yt[b][:], yfv[:, t, :], w=["yt" + Bk], q="sp")
            p.dma(yt2[b][:], ybv[:, t, :], w=["yt2"], q="sp")
            p.op("dve", lambda e, b=b: e.tensor_tensor(out=yt[b][:], in0=yt[b][:], in1=yt2[b][:], op=ALU.add), r=["yt" + Bk, "yt2"], w=["yt" + Bk])
            p.op("pool", lambda e, b=b, t=t: e.tensor_tensor(out=yt2[b][:], in0=h[:, t, :], in1=dB[:], op=ALU.mult), r=["h:%d" % t, "ln_g", "yt" + Bk], w=["yt2"])
            p.op("dve", lambda e, b=b: e.tensor_tensor(out=yt[b][:], in0=yt[b][:], in1=yt2[b][:], op=ALU.add), r=["yt" + Bk, "yt2"], w=["yt" + Bk])
            p.op("act", lambda e, b=b: e.activation(out=g16[b][:], in_=yt[b][:], func=AF.Gelu), r=["yt" + Bk], w=["g16"])
            pb = c.ps[t % 2]
            pk = "ps%d" % (t % 2)
            pbb = pb[:].bitcast(BF16)
            for k in range(8):
                p.op("pe", lambda e, k=k, b=b, pbb=pbb: e.transpose(out=pbb[:, k * 128:(k + 1) * 128], in_=g16[b][:, k * 128:(k + 1) * 128], identity=c.identb[:]),
                     r=["g16", "identb"], w=[pk + ":%d" % k])
            p.op("dve", lambda e, t=t, pbb=pbb: e.tensor_copy(out=gT[:, :, t * 128:(t + 1) * 128], in_=pbb.rearrange("p (k n) -> p k n", k=8)),
                 r=[pk + ":%d" % k for k in range(8)], w=["gT:%d" % t])
        blocks = [(s, min(512, NR - s)) for s in range(0, NR, 512)]
        it = 0
        for m in range(8):
            for (s, n) in blocks:
                tiles = list(range(s // 128, (s + n) // 128))
                pg = 2 + it % 2
                si = it % 2
                it += 1
                for k in range(8):
                    p.op("pe", lambda e, k=k, m=m, s=s, n=n, pg=pg: e.matmul(
                        c.ps[pg][:, 0:n], lhsT=wglu_sb[:, k, m * 128:(m + 1) * 128], rhs=gT[:, k, s:s + n], start=(k == 0), stop=(k == 7)),
                        r=["wglu:%d" % k] + ["gT:%d" % t for t in tiles], w=["ps%d" % pg])
                p.op("act", lambda e, n=n, pg=pg, si=si: e.activation(out=sgs[si][:, 0:n], in_=c.ps[pg][:, 0:n], func=AF.Sigmoid), r=["ps%d" % pg], w=["sgs%d" % si])
                p.op("dve", lambda e, m=m, s=s, n=n, si=si: e.tensor_tensor(out=zT[:, m, s:s + n], in0=sgs[si][:, 0:n], in1=gT[:, m, s:s + n], op=ALU.mult),
                     r=["sgs%d" % si] + ["gT:%d" % t for t in tiles], w=["zT:%d:%d" % (m, t) for t in tiles])
        for t in range(nt):
            p.op("pool", None, r=["zT:%d:%d" % (m, t) for m in range(8)], w=["xT:%d" % t])
        xT = zT
        w_sb = wo_sb
    p.dma(gb[:], lnp[0, :].partition_broadcast(128), r=["ln_g"], w=["ln_g"])
    p.dma(bb[:], lnp[1, :].partition_broadcast(128), w=["ln_b"])
    for t in range(nt):
        for half in range(2):
            pd = 2 + (t * 2 + half) % 2
            for k in range(8):
                p.op("pe", lambda e, t=t, half=half, k=k, pd=pd: e.matmul(
                    c.ps[pd][:, :], lhsT=xT[:, k, t * 128:(t + 1) * 128], rhs=w_sb[:, k, half * 512:(half + 1) * 512],
                    start=(k == 0), stop=(k == 7)), r=["xT:%d" % t, "wo:%d" % k], w=["ps%d" % pd])
            p.op("dve", lambda e, t=t, half=half, pd=pd: e.scalar_tensor_tensor(
                out=h[:, t, half * 512:(half + 1) * 512], in0=h[:, t, half * 512:(half + 1) * 512], scalar=ALPHA,
                in1=c.ps[pd][:, :], op0=ALU.mult, op1=ALU.add), r=["ps%d" % pd, "h:%d" % t], w=["h:%d:%d" % (t, half)])
        p.op("pool", None, r=["h:%d:0" % t, "h:%d:1" % t], w=["h:%d" % t])
    layernorm(p, c, h, nt, "h", gb, bb, "ln")
    p.fence()
    p.dma(gb[:], lnp[2, :].partition_broadcast(128), w=["ln_g"])
    p.dma(bb[:], lnp[3, :].partition_broadcast(128), w=["ln_b"])
    off[0] = 0
    wgb = [carve(4096).rearrange("p (k f) -> p k f", k=8) for i in range(2)]
    wub = [carve(4096).rearrange("p (k f) -> p k f", k=8) for i in range(2)]
    wdb = [carve(4096).rearrange("p (k f) -> p k f", k=4) for i in range(2)]
    actT = carve(4 * NR).rearrange("p (k n) -> p k n", k=4)
    sg = [carve(512), carve(512)]
    hT = A
    make_T(p, c, h, hT, nt, "h")
    ffn(p, c, h, hT, nt, "h", wg, wu, wd, (wgb, wub, wdb, actT, sg))
    layernorm(p, c, h, nt, "h", gb, bb, "ln")
    ov = hout.rearrange("(t p) d -> p t d", p=128)
    for t in range(nt):
        p.dma(ov[:, t, :], h[:, t, :], r=["h:%d" % t], w=["out:%d" % t])
    p.op("sp", None, r=["out:%d" % t for t in range(nt)])
    return p.build()


S5_NC = 1152
S5_PAD = 1024
S5_EV = [float(e) for e in range(-7, 8)] + [8.0 * 2 ** k for k in range(11)]
S5_NE = len(S5_EV)


def s5_eidx(e):
    return S5_EV.index(float(e))


def build_s5scan():
    p = Prog()
    c = setup_common(p)
    NC = S5_NC
    U_d = p.dram("U", [2, 128, 16, NC])
    lam_d = p.dram("lam", [2, 3, 128, 8])
    b_d = p.dram("bpar", [2, 2, 128, 8 * 16])
    c_d = p.dram("cpar", [2, 2, 128, 8 * 16])
    ev_d = p.dram("ev", [128, S5_NE])
    mask_d = p.dram("mask", [128, 128])
    Y_d = p.dram("Y", [2, 128, 9, 16 * 128], kind="ExternalOutput")
    Ub = p.sbuf("Ub", [128, 16, NC], BF16)
    ev = p.sbuf("ev", [128, S5_NE], F32)
    mask = p.sbuf("mask", [128, 128], F32)
    p.dma(ev[:], ev_d[:, :], w=["ev"])
    p.dma(mask[:], mask_d[:, :], w=["mask"])
    lam = p.sbuf("lam", [128, 3, 8], F32)
    bp = p.sbuf("bp", [128, 2, 8, 16], F32)
    cp = p.sbuf("cp", [128, 2, 8, 16], F32)
    NE = S5_NE

    def T(name, shape, dt=F32):
        return p.sbuf(name, shape, dt)
    dt_ = T("dt", [128, 8]); x_ = T("x", [128, 8]); th_ = T("th", [128, 8])
    xe = T("xe", [128, 8, NE]); the = T("the", [128, 8, NE]); mag = T("mag", [128, 8, NE])
    ys = T("ys", [128, 8, NE]); yc = T("yc", [128, 8, NE]); ki = T("ki", [128, 8, NE], mybir.dt.int32)
    kf = T("kf", [128, 8, NE]); m1 = T("m1", [128, 8, NE])
    pr = T("pr", [128, 8, NE]); pi_ = T("pi", [128, 8, NE]); npi = T("npi", [128, 8, NE])
    s1 = [T("s1_%d" % i, [128, 8]) for i in range(8)]
    bbr = T("bbr", [128, 8, 16]); bbi = T("bbi", [128, 8, 16]); nci = T("nci", [128, 8, 16])
    t1 = T("t1", [128, 8, 16]); t2 = T("t2", [128, 8, 16])
    BinR = T("BinR", [128, 8, 128], BF16); BinI = T("BinI", [128, 8, 128], BF16)
    CoR = T("CoR", [128, 8, 128], BF16); CoIn = T("CoIn", [128, 8, 128], BF16)
    BinT = T("BinT", [128, 8, 2, 2, 128], BF16)
    p.op("pool", lambda e: e.memset(BinT[:], 0.0), w=["BinTz"])
    Kin = T("Kin", [128, 16, 128], BF16)
    Q = [[T("Q%d%d" % (i, j), [128, S5_PAD + NC]) for j in range(2)] for i in range(2)]
    V = [T("V%d" % j, [128, NC]) for j in range(2)]
    Zb = [T("Zb%d" % j, [128, NC], BF16) for j in range(2)]
    ysb = [T("ysb%d" % i, [128, 9, 256]) for i in range(2)]
    for i in range(2):
        for j in range(2):
            p.op("pool", lambda e, i=i, j=j: e.memset(Q[i][j][:, 0:S5_PAD], 0.0), w=["Qpad%d%d" % (i, j)])

    def dve(fn, r, w):
        p.op("dve", fn, r=r, w=w)

    def bc8(ap):
        return ap.unsqueeze(2).to_broadcast([128, 8, 16])

    for d in range(2):
        for g in range(16):
            p.dma(Ub[:, g, :], U_d[d, :, g, :], w=["U:%d" % g], q="pool")
        for i in range(3):
            p.dma(lam[:, i, :], lam_d[d, i, :, :], w=["lam"], q="sp")
        for i in range(2):
            p.dma(bp[:, i, :, :], b_d[d, i, :, :].rearrange("p (a c) -> p a c", c=16), w=["bp"], q="sp")
            p.dma(cp[:, i, :, :], c_d[d, i, :, :].rearrange("p (a c) -> p a c", c=16), w=["cp"], q="sp")
        lr, li, ldt = lam[:, 0, :], lam[:, 1, :], lam[:, 2, :]
        p.op("act", lambda e: e.activation(out=dt_[:], in_=ldt, func=AF.Exp), r=["lam"], w=["dt"])
        dve(lambda e: e.tensor_tensor(out=x_[:], in0=lr, in1=dt_[:], op=ALU.mult), ["lam", "dt"], ["x"])
        dve(lambda e: e.tensor_tensor(out=th_[:], in0=li, in1=dt_[:], op=ALU.mult), ["lam", "dt"], ["th"])
        evb = ev[:].unsqueeze(1).to_broadcast([128, 8, NE])
        dve(lambda e: e.tensor_tensor(out=xe[:], in0=x_[:].unsqueeze(2).to_broadcast([128, 8, NE]), in1=evb, op=ALU.mult), ["x", "ev"], ["xe"])
        dve(lambda e: e.tensor_tensor(out=the[:], in0=th_[:].unsqueeze(2).to_broadcast([128, 8, NE]), in1=evb, op=ALU.mult), ["th", "ev"], ["the"])
        p.op("act", lambda e: e.activation(out=mag[:], in_=xe[:], func=AF.Exp), r=["xe"], w=["mag"])
        inv2pi = 1.0 / (2.0 * np.pi)
        dve(lambda e: e.tensor_scalar(out=ys[:], in0=the[:], scalar1=inv2pi, scalar2=None, op0=ALU.mult), ["the"], ["ys"])
        dve(lambda e: e.tensor_scalar(out=yc[:], in0=the[:], scalar1=inv2pi, scalar2=0.25, op0=ALU.mult, op1=ALU.add), ["the"], ["yc"])
        for (yy, nm) in ((ys, "ys"), (yc, "yc")):
            dve(lambda e, yy=yy: e.tensor_copy(out=ki[:], in_=yy[:]), [nm], ["ki"])
            dve(lambda e: e.tensor_copy(out=kf[:], in_=ki[:]), ["ki"], ["kf"])
            dve(lambda e, yy=yy: e.tensor_tensor(out=yy[:], in0=yy[:], in1=kf[:], op=ALU.subtract), [nm, "kf"], [nm])
            dve(lambda e, yy=yy: e.tensor_scalar(out=m1[:], in0=yy[:], scalar1=0.5, scalar2=None, op0=ALU.is_gt), [nm], ["m1"])
            dve(lambda e, yy=yy: e.tensor_tensor(out=yy[:], in0=yy[:], in1=m1[:], op=ALU.subtract), [nm, "m1"], [nm])
            dve(lambda e, yy=yy: e.tensor_scalar(out=m1[:], in0=yy[:], scalar1=-0.5, scalar2=None, op0=ALU.is_lt), [nm], ["m1"])
            dve(lambda e, yy=yy: e.tensor_tensor(out=yy[:], in0=yy[:], in1=m1[:], op=ALU.add), [nm, "m1"], [nm])
        p.op("act", lambda e: e.activation(out=ys[:], in_=ys[:], func=AF.Sin, scale=2.0 * np.pi), r=["ys"], w=["ys"])
        p.op("act", lambda e: e.activation(out=yc[:], in_=yc[:], func=AF.Sin, scale=2.0 * np.pi), r=["yc"], w=["yc"])
        dve(lambda e: e.tensor_tensor(out=pr[:], in0=mag[:], in1=yc[:], op=ALU.mult), ["mag", "yc"], ["pr"])
        dve(lambda e: e.tensor_tensor(out=pi_[:], in0=mag[:], in1=ys[:], op=ALU.mult), ["mag", "ys"], ["pi"])
        dve(lambda e: e.tensor_scalar(out=npi[:], in0=pi_[:], scalar1=-1.0, scalar2=None, op0=ALU.mult), ["pi"], ["npi"])
        e1 = s5_eidx(1)
        a1r, a1i = pr[:, :, e1], pi_[:, :, e1]
        nr, den, rden, crr, cii, tA, tB = s1[0], s1[1], s1[2], s1[3], s1[4], s1[5], s1[6]
        dve(lambda e: e.tensor_scalar(out=nr[:], in0=a1r, scalar1=-1.0, scalar2=None, op0=ALU.add), ["pr"], ["nr"])
        dve(lambda e: e.tensor_tensor(out=den[:], in0=lr, in1=lr, op=ALU.mult), ["lam"], ["den"])
        dve(lambda e: e.tensor_tensor(out=tA[:], in0=li, in1=li, op=ALU.mult), ["lam"], ["tA"])
        dve(lambda e: e.tensor_tensor(out=den[:], in0=den[:], in1=tA[:], op=ALU.add), ["den", "tA"], ["den"])
        dve(lambda e: e.reciprocal(out=rden[:], in_=den[:]), ["den"], ["rden"])
        dve(lambda e: e.tensor_tensor(out=crr[:], in0=nr[:], in1=lr, op=ALU.mult), ["nr", "lam"], ["crr"])
        dve(lambda e: e.tensor_tensor(out=tA[:], in0=a1i, in1=li, op=ALU.mult), ["pi", "lam", "den"], ["tA"])
        dve(lambda e: e.tensor_tensor(out=crr[:], in0=crr[:], in1=tA[:], op=ALU.add), ["crr", "tA"], ["crr"])
        dve(lambda e: e.tensor_tensor(out=crr[:], in0=crr[:], in1=rden[:], op=ALU.mult), ["crr", "rden"], ["crr"])
        dve(lambda e: e.tensor_tensor(out=cii[:], in0=a1i, in1=lr, op=ALU.mult), ["pi", "lam"], ["cii"])
        dve(lambda e: e.tensor_tensor(out=tB[:], in0=nr[:], in1=li, op=ALU.mult), ["nr", "lam"], ["tB"])
        dve(lambda e: e.tensor_tensor(out=cii[:], in0=cii[:], in1=tB[:], op=ALU.subtract), ["cii", "tB"], ["cii"])
        dve(lambda e: e.tensor_tensor(out=cii[:], in0=cii[:], in1=rden[:], op=ALU.mult), ["cii", "rden"], ["cii"])
        br, bi = bp[:, 0, :, :], bp[:, 1, :, :]
        cr, ci = cp[:, 0, :, :], cp[:, 1, :, :]
        dve(lambda e: e.tensor_tensor(out=bbr[:], in0=br, in1=bc8(crr[:]), op=ALU.mult), ["bp", "crr"], ["bbr"])
        dve(lambda e: e.tensor_tensor(out=t1[:], in0=bi, in1=bc8(cii[:]), op=ALU.mult), ["bp", "cii"], ["t1"])
        dve(lambda e: e.tensor_tensor(out=bbr[:], in0=bbr[:], in1=t1[:], op=ALU.subtract), ["bbr", "t1"], ["bbr"])
        dve(lambda e: e.tensor_tensor(out=bbi[:], in0=bi, in1=bc8(crr[:]), op=ALU.mult), ["bp", "crr"], ["bbi"])
        dve(lambda e: e.tensor_tensor(out=t1[:], in0=br, in1=bc8(cii[:]), op=ALU.mult), ["bp", "cii", "bbr"], ["t1"])
        dve(lambda e: e.tensor_tensor(out=bbi[:], in0=bbi[:], in1=t1[:], op=ALU.add), ["bbi", "t1"], ["bbi"])
        dve(lambda e: e.tensor_scalar(out=nci[:], in0=ci, scalar1=-1.0, scalar2=None, op0=ALU.mult), ["cp"], ["nci"])
        BinRv = BinR[:].rearrange("p a (t c) -> p a t c", c=16); BinIv = BinI[:].rearrange("p a (t c) -> p a t c", c=16)
        CoRv = CoR[:].rearrange("p a (t c) -> p a t c", c=16); CoInv = CoIn[:].rearrange("p a (t c) -> p a t c", c=16)
        for tau in range(8):
            en = s5_eidx(-tau); ep = s5_eidx(tau)
            prn, pin, npin = bc8(pr[:, :, en]), bc8(pi_[:, :, en]), bc8(npi[:, :, en])
            prp, pip, npip = bc8(pr[:, :, ep]), bc8(pi_[:, :, ep]), bc8(npi[:, :, ep])
            dve(lambda e, prn=prn: e.tensor_tensor(out=t1[:], in0=bbr[:], in1=prn, op=ALU.mult), ["bbr", "pr", "bbi"], ["t1"])
            dve(lambda e, npin=npin: e.tensor_tensor(out=t2[:], in0=bbi[:], in1=npin, op=ALU.mult), ["bbi", "npi"], ["t2"])
            dve(lambda e, tau=tau: e.tensor_tensor(out=BinRv[:, :, tau, :], in0=t1[:], in1=t2[:], op=ALU.add), ["t1", "t2"], ["BinR:%d" % tau])
            dve(lambda e, prn=prn: e.tensor_tensor(out=t1[:], in0=bbi[:], in1=prn, op=ALU.mult), ["bbi", "pr", "BinR:%d" % tau], ["t1"])
            dve(lambda e, pin=pin: e.tensor_tensor(out=t2[:], in0=bbr[:], in1=pin, op=ALU.mult), ["bbr", "pi", "BinR:%d" % tau], ["t2"])
            dve(lambda e, tau=tau: e.tensor_tensor(out=BinIv[:, :, tau, :], in0=t1[:], in1=t2[:], op=ALU.add), ["t1", "t2"], ["BinI:%d" % tau])
            dve(lambda e, prp=prp: e.tensor_tensor(out=t1[:], in0=cr, in1=prp, op=ALU.mult), ["cp", "pr", "BinI:%d" % tau], ["t1"])
            dve(lambda e, pip=pip: e.tensor_tensor(out=t2[:], in0=nci[:], in1=pip, op=ALU.mult), ["nci", "pi", "BinI:%d" % tau], ["t2"])
            dve(lambda e, tau=tau: e.tensor_tensor(out=CoRv[:, :, tau, :], in0=t1[:], in1=t2[:], op=ALU.add), ["t1", "t2"], ["CoR:%d" % tau])
            dve(lambda e, npip=npip: e.tensor_tensor(out=t1[:], in0=cr, in1=npip, op=ALU.mult), ["cp", "npi", "CoR:%d" % tau], ["t1"])
            dve(lambda e, prp=prp: e.tensor_tensor(out=t2[:], in0=nci[:], in1=prp, op=ALU.mult), ["nci", "pr", "CoR:%d" % tau], ["t2"])
            dve(lambda e, tau=tau: e.tensor_tensor(out=CoInv[:, :, tau, :], in0=t1[:], in1=t2[:], op=ALU.add), ["t1", "t2"], ["CoIn:%d" % tau])
        allB = ["BinR:%d" % t for t in range(8)] + ["BinI:%d" % t for t in range(8)]
        allC = ["CoR:%d" % t for t in range(8)] + ["CoIn:%d" % t for t in range(8)]
        if DBG.get('stage', 9) < 2:
            p.fence(); continue
        for gp in range(8):
            pb = c.ps[gp % 2]
            pk = "ps%d" % (gp % 2)
            pbb = pb[:].bitcast(BF16)
            for ri, Bt in enumerate((BinR, BinI)):
                p.op("pe", lambda e, gp=gp, ri=ri, Bt=Bt, pbb=pbb: e.transpose(out=pbb[:, ri * 128:(ri + 1) * 128], in_=Bt[:, gp, :], identity=c.identb[:]),
                     r=allB + ["identb"], w=[pk + ":%d" % ri])
            for g2 in range(2):
                p.op("act", lambda e, gp=gp, pbb=pbb, g2=g2: e.activation(
                    out=BinT[:, gp, :, g2, 64 * g2:64 * g2 + 64],
                    in_=pbb[:, 0:256].rearrange("p (r n) -> p r n", r=2)[:, :, 64 * g2:64 * g2 + 64], func=AF.Copy),
                    r=[pk + ":0", pk + ":1", "BinTz"], w=["BinT:%d:%d" % (gp, g2)])
        if DBG.get('stage', 9) < 3:
            p.fence(); continue
        for g in range(16):
            gp, g2 = g // 2, g % 2
            pb = 2 + g % 2
            lo, hi = 64 * g2, 64 * g2 + 64
            p.op("pe", lambda e, gp=gp, lo=lo, hi=hi, pb=pb: e.matmul(c.ps[pb][:, 0:128], lhsT=BinR[lo:hi, gp, :], rhs=CoR[lo:hi, gp, :], start=True, stop=False),
                 r=allB + allC, w=["ps%d" % pb])
            p.op("pe", lambda e, gp=gp, lo=lo, hi=hi, pb=pb: e.matmul(c.ps[pb][:, 0:128], lhsT=BinI[lo:hi, gp, :], rhs=CoIn[lo:hi, gp, :], start=False, stop=True),
                 r=allB + allC, w=["ps%d" % pb])
            dve(lambda e, g=g, pb=pb: e.tensor_tensor(out=Kin[:, g, :], in0=c.ps[pb][:, 0:128], in1=mask[:], op=ALU.mult), ["ps%d" % pb, "mask"], ["Kin:%d" % g])
        p.fence()
        if DBG.get('stage', 9) < 4:
            continue
        blocks = [(0, 512), (512, 512), (1024, 128)]
        for gp in range(8):
            for ri in range(2):
                for bi_, (s, n) in enumerate(blocks):
                    bank = 2 + ri * 3 + bi_
                    for g2 in range(2):
                        p.op("pe", lambda e, gp=gp, ri=ri, g2=g2, s=s, n=n, bank=bank: e.matmul(
                            c.ps[bank][:, 0:n], lhsT=BinT[:, gp, ri, g2, :], rhs=Ub[:, 2 * gp + g2, s:s + n],
                            start=(g2 == 0), stop=(g2 == 1)),
                            r=["BinT:%d:0" % gp, "BinT:%d:1" % gp, "U:%d" % (2 * gp + g2)], w=["ps%d:0" % bank, "ps%d:1" % bank])
                    if DBG.get('nomm'):
                        continue
                    if not DBG.get('nov'):
                      p.op("act", lambda e, ri=ri, s=s, n=n, bank=bank: e.activation(out=V[ri][:, s:s + n], in_=c.ps[bank][:, 0:n], func=AF.Copy),
                         r=["ps%d:0" % bank, "ps%d:1" % bank], w=["V%d:%d" % (ri, bi_)])
                    p.op("pool", lambda e, ri=ri, s=s, n=n: e.tensor_copy(out=Q[0][ri][:, S5_PAD + s:S5_PAD + s + n], in_=V[ri][:, s:s + n]),
                         r=["V%d:%d" % (ri, bi_)], w=["Q0%d" % ri])
            if DBG.get('stage', 9) < 5:
                continue
            P0 = S5_PAD
            for k in range(11):
                sh = 2 ** k
                src, dst = Q[k % 2], Q[(k + 1) % 2]
                sn, dn = "Q%d" % (k % 2), "Q%d" % ((k + 1) % 2)
                ek = 15 + k
                Ar, Ai, nAi = pr[:, gp, ek:ek + 1], pi_[:, gp, ek:ek + 1], npi[:, gp, ek:ek + 1]
                dve(lambda e, src=src, dst=dst, sh=sh, Ar=Ar: e.scalar_tensor_tensor(
                    out=dst[0][:, P0:P0 + NC], in0=src[0][:, P0 - sh:P0 - sh + NC], scalar=Ar, in1=src[0][:, P0:P0 + NC], op0=ALU.mult, op1=ALU.add),
                    [sn + "0", "pr", "Qpad%d0" % (k % 2)], [dn + "0"])
                dve(lambda e, src=src, dst=dst, sh=sh, nAi=nAi: e.scalar_tensor_tensor(
                    out=dst[0][:, P0:P0 + NC], in0=src[1][:, P0 - sh:P0 - sh + NC], scalar=nAi, in1=dst[0][:, P0:P0 + NC], op0=ALU.mult, op1=ALU.add),
                    [sn + "1", "npi", dn + "0", "Qpad%d1" % (k % 2)], [dn + "0"])
                dve(lambda e, src=src, dst=dst, sh=sh, Ai=Ai: e.scalar_tensor_tensor(
                    out=dst[1][:, P0:P0 + NC], in0=src[0][:, P0 - sh:P0 - sh + NC], scalar=Ai, in1=src[1][:, P0:P0 + NC], op0=ALU.mult, op1=ALU.add),
                    [sn + "0", sn + "1", "pi"], [dn + "1"])
                dve(lambda e, src=src, dst=dst, sh=sh, Ar=Ar: e.scalar_tensor_tensor(
                    out=dst[1][:, P0:P0 + NC], in0=src[1][:, P0 - sh:P0 - sh + NC], scalar=Ar, in1=dst[1][:, P0:P0 + NC], op0=ALU.mult, op1=ALU.add),
                    [sn + "1", "pr", dn + "1"], [dn + "1"])
            fin = Q[11 % 2]
            fn_ = "Q%d" % (11 % 2)
            for ri in range(2):
                p.op("pool", lambda e, ri=ri, fin=fin: e.tensor_tensor(out=Zb[ri][:], in0=fin[ri][:, P0:P0 + NC], in1=V[ri][:], op=ALU.subtract),
                     r=[fn_ + "%d" % ri] + ["V%d:%d" % (ri, b_) for b_ in range(3)], w=["Zb%d" % ri])
            if DBG.get('stage', 9) < 6:
                continue
            yb_ = ysb[gp % 2]
            yk = "ysb%d" % (gp % 2)
            for t in range(9):
                bank = t % 2
                for g2 in range(2):
                    lo, hi = 64 * g2, 64 * g2 + 64
                    g = 2 * gp + g2
                    o = c.ps[bank][:, g2 * 128:(g2 + 1) * 128]
                    p.op("pe", lambda e, t=t, lo=lo, hi=hi, gp=gp, o=o: e.matmul(o, lhsT=Zb[0][lo:hi, t * 128:(t + 1) * 128], rhs=CoR[lo:hi, gp, :], start=True, stop=False),
                         r=["Zb0"] + allC, w=["ps%d" % bank])
                    p.op("pe", lambda e, t=t, lo=lo, hi=hi, gp=gp, o=o: e.matmul(o, lhsT=Zb[1][lo:hi, t * 128:(t + 1) * 128], rhs=CoIn[lo:hi, gp, :], start=False, stop=False),
                         r=["Zb1"] + allC, w=["ps%d" % bank])
                    p.op("pe", lambda e, t=t, g=g, o=o: e.matmul(o, lhsT=Ub[:, g, t * 128:(t + 1) * 128], rhs=Kin[:, g, :], start=False, stop=True),
                         r=["U:%d" % g, "Kin:%d" % g], w=["ps%d" % bank])
                p.op("act", lambda e, t=t, bank=bank, yb_=yb_: e.activation(out=yb_[:, t, :], in_=c.ps[bank][:, 0:256], func=AF.Copy),
                     r=["ps%d" % bank], w=[yk + ":%d" % t])
            p.dma(Y_d[d, :, :, gp * 256:(gp + 1) * 256], yb_[:, :, :], r=[yk + ":%d" % t for t in range(9)], w=["Yout:%d:%d" % (d, gp)] + [yk + ":%d" % t for t in range(9)], q="sp")
        p.fence()
    p.op("sp", None, r=[])
    return p.build()


L_TOK = 8208
S5_LP = S5_NC * 8


def s5_layout_inputs(hb, j, lam_re, lam_im, log_dt, b_re, b_im, c_re, c_im):
    gs = slice(16 * j, 16 * j + 16)
    hp = np.zeros((2, S5_LP, 256), np.float32)
    hp[0, :L_TOK] = hb[:, 256 * j:256 * j + 256]
    hp[1, :L_TOK] = hb[::-1, 256 * j:256 * j + 256]
    U = hp.reshape(2, S5_NC, 8, 16, 16).transpose(0, 2, 4, 3, 1).reshape(2, 128, 16, S5_NC)

    def gp_layout(a):
        sh = a.shape
        a = a.reshape((2, 8, 2, 64) + sh[3:])
        a = np.moveaxis(a, 1, 3)
        return a.reshape((2, 128, 8) + sh[3:])
    lam = np.stack([gp_layout(lam_re[:, gs]), gp_layout(lam_im[:, gs]),
                    gp_layout(np.broadcast_to(log_dt[:, gs, None], (2, 16, 64)))], 1)
    bpar = np.stack([gp_layout(b_re[:, gs]), gp_layout(b_im[:, gs])], 1).reshape(2, 2, 128, 128)
    cT_re = np.swapaxes(c_re[:, gs], 2, 3); cT_im = np.swapaxes(c_im[:, gs], 2, 3)
    cpar = np.stack([gp_layout(cT_re), gp_layout(cT_im)], 1).reshape(2, 2, 128, 128)
    return dict(U=np.ascontiguousarray(U), lam=np.ascontiguousarray(lam.astype(np.float32)),
                bpar=np.ascontiguousarray(bpar), cpar=np.ascontiguousarray(cpar))


def s5_consts():
    ev = np.broadcast_to(np.array(S5_EV, np.float32)[None, :], (128, S5_NE)).copy()
    tau = np.arange(128) // 16
    mask = (tau[None, :] >= tau[:, None]).astype(np.float32)
    return dict(ev=ev, mask=mask)


def s5_unlayout(Y):
    Y = Y.reshape(2, 128, 9, 16, 8, 16)
    y = Y.transpose(0, 2, 1, 4, 3, 5).reshape(2, S5_LP, 256)
    yf = y[0, :L_TOK]
    yb = y[1, :L_TOK][::-1]
    return yf, yb


N_META = 16
SEQ = 8192
ROWS_PER_CORE = L_TOK // 4


def _rows_split(a):
    out = []
    for b in range(2):
        for q in range(4):
            blk = np.zeros((NR,) + a.shape[2:], a.dtype)
            blk[:ROWS_PER_CORE] = a[b, q * ROWS_PER_CORE:(q + 1) * ROWS_PER_CORE]
            out.append(blk)
    return out


def _rows_join(blocks, width, dtype):
    out = np.zeros((2, L_TOK, width), dtype)
    for b in range(2):
        for q in range(4):
            out[b, q * ROWS_PER_CORE:(q + 1) * ROWS_PER_CORE] = blocks[b * 4 + q][:ROWS_PER_CORE]
    return out


def _rope_tables():
    n_real = SEQ
    row_ids = np.concatenate([np.zeros(N_META), np.repeat(np.arange(n_real // 64), 64)]).astype(np.float32)
    col_ids = np.concatenate([np.zeros(N_META), np.tile(np.arange(64), n_real // 64)]).astype(np.float32)
    inv_freq = (np.float32(10000.0) ** (-(np.arange(0, 32, 2, dtype=np.float32) / np.float32(32.0)))).astype(np.float32)
    ang_r = (row_ids[:, None] * inv_freq[None, :]).astype(np.float32)
    ang_c = (col_ids[:, None] * inv_freq[None, :]).astype(np.float32)
    cr, sr, cc, sc = np.cos(ang_r), np.sin(ang_r), np.cos(ang_c), np.sin(ang_c)
    C1 = np.concatenate([cr, cr, cc, cc], -1)
    S1 = np.concatenate([-sr, sr, -sc, sc], -1)
    ctab = np.tile(C1, (1, 20)).astype(np.float32)
    stab = np.tile(S1, (1, 20)).astype(np.float32)
    return ctab, stab


def _run(nc, in_maps):
    res = run_bass_kernel_spmd(nc, in_maps, core_ids=list(range(8)))
    return res.results


def _ffn_ln_inputs(i, inputs):
    lnp = np.stack([inputs["ln_gain"][i, 0], inputs["ln_bias"][i, 0], inputs["ln_gain"][i, 1], inputs["ln_bias"][i, 1]]).astype(np.float32)
    return dict(lnp=np.ascontiguousarray(lnp), wg=np.ascontiguousarray(inputs["ffn_w_gate"][i]),
                wu=np.ascontiguousarray(inputs["ffn_w_up"][i]), wd=np.ascontiguousarray(inputs["ffn_w_down"][i]))


def kernel(**inputs):
    import ml_dtypes
    bf = ml_dtypes.bfloat16
    inputs = {k: np.asarray(v) for k, v in inputs.items()}
    x = inputs["x"].astype(np.float32)
    meta = np.broadcast_to(inputs["meta_tokens"].astype(np.float32)[None], (2, N_META, D))
    h = np.concatenate([meta, x], axis=1)
    ctab, stab = _rope_tables()
    ctab_b = np.broadcast_to(ctab[None], (2,) + ctab.shape)
    stab_b = np.broadcast_to(stab[None], (2,) + stab.shape)
    ct_rows = _rows_split(ctab_b); st_rows = _rows_split(stab_b)
    consts = s5_consts()
    omask = np.ones((2, 128, 128), bf)
    omask[1, L_TOK - 64 * 128:, :] = 0
    for i in range(4):
        j = i // 2
        common = _ffn_ln_inputs(i, inputs)
        hrows = _rows_split(h)
        if i % 2 == 0:
            in_maps = []
            for b in range(2):
                for q in range(4):
                    m = s5_layout_inputs(h[b], q, inputs["s5_lambda_re"][j], inputs["s5_lambda_im"][j], inputs["s5_log_dt"][j],
                                         inputs["s5_b_re"][j], inputs["s5_b_im"][j], inputs["s5_c_re"][j], inputs["s5_c_im"][j])
                    m.update(consts)
                    in_maps.append(m)
            res = _run(build_s5scan(), in_maps)
            yf = np.zeros((2, L_TOK, D), np.float32); yb = np.zeros((2, L_TOK, D), np.float32)
            for b in range(2):
                for q in range(4):
                    a, bb_ = s5_unlayout(res[b * 4 + q]["Y"])
                    yf[b, :, 256 * q:256 * q + 256] = a
                    yb[b, :, 256 * q:256 * q + 256] = bb_
            yfr = _rows_split(yf); ybr = _rows_split(yb)
            in_maps = []
            for cidx in range(8):
                m = dict(common)
                m.update(hin=hrows[cidx], yf=yfr[cidx], yb=ybr[cidx], dskip=np.ascontiguousarray(inputs["s5_d"][j]),
                         wglu=np.ascontiguousarray(inputs["s5_w_glu"][j].reshape(8, 128, D)),
                         wo=np.ascontiguousarray(inputs["s5_w_out"][j].reshape(8, 128, D)))
                in_maps.append(m)
            res = _run(build_post("s5"), in_maps)
        else:
            gain = np.concatenate([np.tile(inputs["attn_q_gain"][j], 16), np.tile(inputs["attn_k_gain"][j], 4)]).astype(np.float32)
            in_maps = [dict(hin=hrows[cidx], wqkv=np.ascontiguousarray(inputs["attn_w_qkv"][j]), gain=gain,
                            ctab=ct_rows[cidx], stab=st_rows[cidx]) for cidx in range(8)]
            res = _run(build_qkv(), in_maps)
            qkv = _rows_join([r["qkv"] for r in res], 1536, bf)
            Wo = inputs["attn_w_out"][j]
            wo_p = np.zeros((8, 128, D), np.float32)
            for g in range(4):
                for ii in range(2):
                    wo_p[2 * g + ii, 0:64] = Wo[64 * (4 * g + ii):64 * (4 * g + ii) + 64]
                    wo_p[2 * g + ii, 64:128] = Wo[64 * (4 * g + 2 + ii):64 * (4 * g + 2 + ii) + 64]
            in_maps = []
            for b in range(2):
                kpad = np.zeros((NK, 4, 64), bf); kpad[:L_TOK] = qkv[b, :, 1024:1280].reshape(L_TOK, 4, 64)
                kT = np.ascontiguousarray(kpad.transpose(1, 2, 0))
                vpad = np.zeros((NK, 4, 64), bf); vpad[:L_TOK] = qkv[b, :, 1280:1536].reshape(L_TOK, 4, 64)
                vA = np.zeros((4, NKT, 128, 128), bf)
                for g in range(4):
                    vv = vpad[:, g, :].reshape(NKT, 128, 64)
                    vA[g, :, :, 0:64] = vv; vA[g, :, :, 64:128] = vv
                vA = np.ascontiguousarray(vA.transpose(0, 2, 1, 3)).reshape(4, 128, NKT * 128)
                for q in range(4):
                    qrows = np.zeros((NR, 16, 64), bf)
                    qrows[:ROWS_PER_CORE] = qkv[b, q * ROWS_PER_CORE:(q + 1) * ROWS_PER_CORE, 0:1024].reshape(ROWS_PER_CORE, 16, 64)
                    qT = np.zeros((4, 64, 17, 4, 128), bf)
                    for g in range(4):
                        for jj in range(4):
                            qT[g, :, :, jj, :] = qrows[:, 4 * g + jj, :].reshape(17, 128, 64).transpose(2, 0, 1)
                    m = dict(common)
                    m.update(hin=hrows[b * 4 + q], qT=qT.reshape(4, 64, 17 * 512), kT=kT, vA=vA, omask=omask, wo=wo_p)
                    in_maps.append(m)
            res = _run(build_post("att"), in_maps)
        h = _rows_join([r["hout"] for r in res], D, np.float32)
    return np.ascontiguousarray(h[:, N_META:, :]).astype(np.float32)
```

```python
import numpy as np
from contextlib import ExitStack
import concourse.bass as bass
import concourse.mybir as mybir
from concourse.bass_utils import run_bass_kernel_spmd
from concourse.alu_op_type import AluOpType as ALU

F32 = mybir.dt.float32
BF16 = mybir.dt.bfloat16
AF = mybir.ActivationFunctionType
AX = mybir.AxisListType

ENGS = ["pe", "dve", "act", "pool", "sp"]
N_DMA_SEMS = 8


class Prog:
    def __init__(self, name="k"):
        self.nc = bass.Bass("TRN2", target_bir_lowering=False)
        self.es = ExitStack()
        self.ins = []
        self.last_w = {}
        self.readers = {}
        self.dram_in = {}
        self.dram_out = {}

    def dram(self, name, shape, dtype=F32, kind="ExternalInput"):
        t = self.nc.dram_tensor(name, list(shape), dtype, kind=kind)
        return t.ap()

    def sbuf(self, name, shape, dtype=F32):
        return self.es.enter_context(self.nc.sbuf_tensor("sb_" + name, list(shape), dtype))

    def psum(self, name, shape, dtype=F32):
        return self.es.enter_context(self.nc.psum_tensor("pm_" + name, list(shape), dtype))

    def op(self, eng, fn, r=(), w=(), dma=False):
        deps = set()
        for k in r:
            if k in self.last_w:
                deps.add(self.last_w[k])
        for k in w:
            if k in self.last_w:
                deps.add(self.last_w[k])
            for i in self.readers.get(k, ()):
                deps.add(i)
        idx = len(self.ins)
        self.ins.append(dict(eng=eng, fn=fn, deps=deps, dma=dma))
        for k in w:
            self.last_w[k] = idx
            self.readers[k] = []
        for k in r:
            self.readers.setdefault(k, []).append(idx)
        return idx

    def dma(self, out, in_, r=(), w=(), q="sp", **kw):
        return self.op(q, lambda e: e.dma_start(out=out, in_=in_, **kw), r, w, dma=True)

    def build(self):
        nc = self.nc
        ins = self.ins
        target = [False] * len(ins)
        for rec in ins:
            for d in rec["deps"]:
                if rec["eng"] == "pe" and ins[d]["eng"] == "pe":
                    continue
                target[d] = True
        sems = {e: self.es.enter_context(nc.semaphore("s_" + e)) for e in ENGS}
        dsems = [self.es.enter_context(nc.semaphore("d%d" % i)) for i in range(N_DMA_SEMS)]
        cnt = {e: 0 for e in ENGS}
        dcnt = [0] * N_DMA_SEMS
        dnext = 0
        seen = {e: {} for e in ENGS}
        sig = [None] * len(ins)
        progs = {e: [] for e in ENGS}
        for i, rec in enumerate(ins):
            e = rec["eng"]
            waits = []
            for d in sorted(rec["deps"]):
                if e == "pe" and ins[d]["eng"] == "pe":
                    continue
                sh, sn, val = sig[d]
                if seen[e].get(sn, 0) < val:
                    seen[e][sn] = val
                    waits.append((sh, val))
            inc = None
            if rec["dma"]:
                j = dnext
                dnext = (dnext + 1) % N_DMA_SEMS
                if dcnt[j] > 0 and seen[e].get("d%d" % j, 0) < dcnt[j]:
                    seen[e]["d%d" % j] = dcnt[j]
                    waits.append((dsems[j], dcnt[j]))
                dcnt[j] += 16
                sig[i] = (dsems[j], "d%d" % j, dcnt[j])
                inc = (dsems[j], 16)
            elif target[i]:
                cnt[e] += 1
                sig[i] = (sems[e], "s_" + e, cnt[e])
                inc = (sems[e], 1)
                seen[e]["s_" + e] = max(seen[e].get("s_" + e, 0), 0)
            else:
                sig[i] = (None, None, 0)
            progs[e].append((waits, rec["fn"], inc))
        self.final = {e: cnt[e] for e in ENGS}
        self.max_sem = max(list(cnt.values()) + dcnt)

        def runner(ename):
            def body(eng):
                for waits, fn, inc in progs[ename]:
                    for sh, val in waits:
                        eng.wait_ge(sh, val)
                    if fn is None:
                        if inc is None:
                            continue
                        r = eng.nop()
                    else:
                        r = fn(eng)
                    if inc is not None:
                        r.then_inc(inc[0], inc[1])
            return body

        with nc.Block() as block:
            block.tensor(runner("pe"))
            block.vector(runner("dve"))
            block.scalar(runner("act"))
            block.gpsimd(runner("pool"))
            block.sync(runner("sp"))
        self.es.close()
        return nc


def _fence(self):
    lasts = set()
    seen_e = {}
    for i, rec in enumerate(self.ins):
        if rec["dma"]:
            lasts.add(i)
        else:
            seen_e[rec["eng"]] = i
    start = getattr(self, "_fence_at", 0)
    deps = set(i for i in lasts if i >= start) | set(seen_e.values())
    self._fence_at = len(self.ins)
    for e in ENGS:
        self.ins.append(dict(eng=e, fn=None, deps=set(deps), dma=False))
    self.last_w = {}
    self.readers = {}


Prog.fence = _fence


D = 1024
DFF = 2816
NT = 17
NR = NT * 128
ALPHA = 8.0 ** 0.25
LN_EPS = 1e-5
FG = 512


class Ctx:
    pass


def setup_common(p):
    c = Ctx()
    c.identf = p.sbuf("identf", [128, 128], F32)
    c.identb = p.sbuf("identb", [128, 128], BF16)
    p.op("pool", lambda e: e.memset(c.identf[:], 0.0), w=["identf"])
    p.op("pool", lambda e: e.affine_select(out=c.identf[:], in_=c.identf[:], pattern=[[-1, 128]],
                                           compare_op=ALU.not_equal, fill=1.0, base=0, channel_multiplier=1),
         r=["identf"], w=["identf"])
    p.op("dve", lambda e: e.tensor_copy(out=c.identb[:], in_=c.identf[:]), r=["identf"], w=["identb"])
    c.ps = [p.psum("ps%d" % i, [128, 512], F32) for i in range(8)]
    c.epsc = p.sbuf("epsc", [128, 1], F32)
    p.op("pool", lambda e: e.memset(c.epsc[:], LN_EPS), w=["epsc"])
    return c


def make_T(p, c, src, dstT, nt, name, psb=(0, 1), nparts=128):
    tmp = [p.sbuf("%s_cb%d" % (name, i), [128, 1024], BF16) for i in range(2)]
    for t in range(nt):
        tb = tmp[t % 2]
        tk = "%s_cb%d" % (name, t % 2)
        eng = "act" if t % 2 == 0 else "pool"
        if eng == "act":
            p.op("act", lambda e, t=t, tb=tb: e.activation(out=tb[:], in_=src[:, t, :], func=AF.Copy),
                 r=["%s:%d" % (name, t)], w=[tk])
        else:
            p.op("pool", lambda e, t=t, tb=tb: e.tensor_copy(out=tb[:], in_=src[:, t, :]),
                 r=["%s:%d" % (name, t)], w=[tk])
        pb = c.ps[psb[t % 2]]
        pk = "ps%d" % psb[t % 2]
        pbb = pb[:].bitcast(BF16)
        for k in range(8):
            p.op("pe", lambda e, k=k, tb=tb, pbb=pbb: e.transpose(out=pbb[:, k * 128:(k + 1) * 128],
                                                                 in_=tb[:, k * 128:(k + 1) * 128], identity=c.identb[:]),
                 r=[tk, "identb"], w=[pk + ":%d" % k])
        p.op("dve", lambda e, t=t, pbb=pbb: e.tensor_copy(
            out=dstT[:, :, t * 128:(t + 1) * 128], in_=pbb.rearrange("p (k n) -> p k n", k=8)),
            r=[pk + ":%d" % k for k in range(8)], w=["%sT:%d" % (name, t)])


_LN_UID = [0]


def layernorm(p, c, h, nt, name, gb, bb, lname):
    _LN_UID[0] += 1
    u = "%d" % _LN_UID[0]
    st = p.sbuf(lname + "_st" + u, [128, 2, 12], F32)
    mv = p.sbuf(lname + "_mv" + u, [128, 2, 2], F32)
    rs = p.sbuf(lname + "_rs" + u, [128, 2, 1], F32)
    for t in range(nt):
        b = t % 2
        hk = "%s:%d" % (name, t)
        sk = "%s_st%d" % (lname, b)
        p.op("dve", lambda e, t=t, b=b: e.bn_stats(out=st[:, b, 0:6], in_=h[:, t, 0:512]), r=[hk], w=[sk + "a"])
        p.op("dve", lambda e, t=t, b=b: e.bn_stats(out=st[:, b, 6:12], in_=h[:, t, 512:1024]), r=[hk], w=[sk + "b"])
        p.op("dve", lambda e, b=b: e.bn_aggr(out=mv[:, b, :], in_=st[:, b, :]), r=[sk + "a", sk + "b"], w=[lname + "_mv%d" % b])
        p.op("act", lambda e, b=b: e.activation(out=rs[:, b, :], in_=mv[:, b, 1:2], func=AF.Sqrt, bias=c.epsc[:], scale=1.0),
             r=[lname + "_mv%d" % b, "epsc"], w=[lname + "_sd%d" % b])
        p.op("dve", lambda e, b=b: e.reciprocal(out=rs[:, b, :], in_=rs[:, b, :]), r=[lname + "_sd%d" % b], w=[lname + "_sd%d" % b])
        p.op("dve", lambda e, t=t, b=b: e.tensor_scalar(out=h[:, t, :], in0=h[:, t, :], scalar1=mv[:, b, 0:1], scalar2=rs[:, b, :],
                                                       op0=ALU.subtract, op1=ALU.mult),
             r=[hk, lname + "_mv%d" % b, lname + "_sd%d" % b], w=[hk])
        p.op("pool", lambda e, t=t: e.tensor_tensor(out=h[:, t, :], in0=h[:, t, :], in1=gb[:], op=ALU.mult), r=[hk, lname + "_g"], w=[hk])
        p.op("pool", lambda e, t=t: e.tensor_tensor(out=h[:, t, :], in0=h[:, t, :], in1=bb[:], op=ALU.add), r=[hk, lname + "_b"], w=[hk])


def ffn(p, c, h, hT, nt, name, wg, wu, wd, bufs):
    wgb, wub, wdb, actT, sg = bufs
    nrows = nt * 128
    wgv = wg.rearrange("(k p) f -> p k f", p=128)
    wuv = wu.rearrange("(k p) f -> p k f", p=128)
    wdv = wd.rearrange("(c p) d -> p c d", p=128)
    ngroups = (DFF + FG - 1) // FG
    blocks = [(s, min(512, nrows - s)) for s in range(0, nrows, 512)]
    it = 0
    for g in range(ngroups):
        f0 = g * FG
        fw = min(FG, DFF - f0)
        nfc = fw // 128
        b = g % 2
        p.dma(wgb[b][:, :, 0:fw], wgv[:, :, f0:f0 + fw], w=["wg%d" % b], q="pool")
        p.dma(wub[b][:, :, 0:fw], wuv[:, :, f0:f0 + fw], w=["wu%d" % b], q="pool")
        p.dma(wdb[b][:, 0:nfc, :], wdv[:, f0 // 128:f0 // 128 + nfc, :], w=["wd%d" % b], q="pool")
        for fc in range(nfc):
            for (s, n) in blocks:
                tiles = list(range(s // 128, (s + n) // 128))
                pg, pu = (4, 5) if it % 2 == 0 else (6, 7)
                sgi = it % 2
                it += 1
                for k in range(8):
                    p.op("pe", lambda e, k=k, b=b, fc=fc, s=s, n=n, pg=pg: e.matmul(
                        c.ps[pg][:, 0:n], lhsT=wgb[b][:, k, fc * 128:(fc + 1) * 128], rhs=hT[:, k, s:s + n],
                        start=(k == 0), stop=(k == 7)),
                        r=["wg%d" % b] + ["%sT:%d" % (name, t) for t in tiles], w=["ps%d" % pg])
                for k in range(8):
                    p.op("pe", lambda e, k=k, b=b, fc=fc, s=s, n=n, pu=pu: e.matmul(
                        c.ps[pu][:, 0:n], lhsT=wub[b][:, k, fc * 128:(fc + 1) * 128], rhs=hT[:, k, s:s + n],
                        start=(k == 0), stop=(k == 7)),
                        r=["wu%d" % b] + ["%sT:%d" % (name, t) for t in tiles], w=["ps%d" % pu])
                p.op("act", lambda e, n=n, pg=pg, sgi=sgi: e.activation(out=sg[sgi][:, 0:n], in_=c.ps[pg][:, 0:n], func=AF.Silu),
                     r=["ps%d" % pg], w=["sg%d" % sgi])
                p.op("dve", lambda e, n=n, s=s, fc=fc, pu=pu, sgi=sgi: e.tensor_tensor(
                    out=actT[:, fc, s:s + n], in0=sg[sgi][:, 0:n], in1=c.ps[pu][:, 0:n], op=ALU.mult),
                    r=["sg%d" % sgi, "ps%d" % pu], w=["actT:%d:%d" % (fc, t) for t in tiles])
        for t in range(nt):
            for half in range(2):
                pd = 2 + (t * 2 + half) % 2
                for fc in range(nfc):
                    p.op("pe", lambda e, t=t, half=half, fc=fc, b=b, pd=pd, nfc=nfc: e.matmul(
                        c.ps[pd][:, :], lhsT=actT[:, fc, t * 128:(t + 1) * 128], rhs=wdb[b][:, fc, half * 512:(half + 1) * 512],
                        start=(fc == 0), stop=(fc == nfc - 1)),
                        r=["actT:%d:%d" % (fc, t), "wd%d" % b], w=["ps%d" % pd])
                hk = "%s:%d:%d" % (name, t, half)
                if g == 0:
                    p.op("dve", lambda e, t=t, half=half, pd=pd: e.scalar_tensor_tensor(
                        out=h[:, t, half * 512:(half + 1) * 512], in0=h[:, t, half * 512:(half + 1) * 512], scalar=ALPHA,
                        in1=c.ps[pd][:, :], op0=ALU.mult, op1=ALU.add),
                        r=["ps%d" % pd, "%s:%d" % (name, t)], w=[hk])
                else:
                    p.op("dve", lambda e, t=t, half=half, pd=pd: e.tensor_tensor(
                        out=h[:, t, half * 512:(half + 1) * 512], in0=h[:, t, half * 512:(half + 1) * 512],
                        in1=c.ps[pd][:, :], op=ALU.add),
                        r=["ps%d" % pd, hk], w=[hk])
    for t in range(nt):
        p.op("pool", None, r=["%s:%d:0" % (name, t), "%s:%d:1" % (name, t)], w=["%s:%d" % (name, t)])


def alloc_ffn_bufs(p, nt):
    wgb = [p.sbuf("wgb%d" % i, [128, 8, FG], BF16) for i in range(2)]
    wub = [p.sbuf("wub%d" % i, [128, 8, FG], BF16) for i in range(2)]
    wdb = [p.sbuf("wdb%d" % i, [128, FG // 128, 1024], BF16) for i in range(2)]
    actT = p.sbuf("actT", [128, FG // 128, nt * 128], BF16)
    sg = [p.sbuf("sg%d" % i, [128, 512], BF16) for i in range(2)]
    return wgb, wub, wdb, actT, sg


QK_EPS = 1e-6
NQK = 1280


def build_qkv():
    nt = NT
    p = Prog()
    c = setup_common(p)
    hin = p.dram("hin", [NR, D])
    wqkv = p.dram("wqkv", [D, 1536])
    gain = p.dram("gain", [NQK])
    ctab = p.dram("ctab", [NR, NQK])
    stab = p.dram("stab", [NR, NQK])
    out = p.dram("qkv", [NR, 1536], BF16, kind="ExternalOutput")
    h = p.sbuf("h", [128, nt, D], F32)
    hT = p.sbuf("hT", [128, 8, NR], BF16)
    wsb = p.sbuf("wsb", [128, 8, 1536], BF16)
    gb = p.sbuf("gainb", [128, NQK], F32)
    qeps = p.sbuf("qeps", [128, 1], F32)
    p.op("pool", lambda e: e.memset(qeps[:], QK_EPS), w=["qeps"])
    hv = hin.rearrange("(t p) d -> p t d", p=128)
    for t in range(nt):
        p.dma(h[:, t, :], hv[:, t, :], w=["h:%d" % t], q="sp")
    p.dma(gb[:], gain.partition_broadcast(128), w=["gainb"])
    wv = wqkv.rearrange("(k p) f -> p k f", p=128)
    for k in range(8):
        p.dma(wsb[:, k, :], wv[:, k, :], w=["wsb:%d" % k], q="pool")
    make_T(p, c, h, hT, nt, "h")
    xq = [p.sbuf("xq%d" % i, [128, 1536], F32) for i in range(2)]
    sq = [p.sbuf("sq%d" % i, [128, NQK], F32) for i in range(2)]
    t1 = [p.sbuf("t1%d" % i, [128, NQK], F32) for i in range(2)]
    ct = [p.sbuf("ct%d" % i, [128, NQK], F32) for i in range(2)]
    stt = [p.sbuf("stt%d" % i, [128, NQK], F32) for i in range(2)]
    ss = [p.sbuf("ss%d" % i, [128, 20], F32) for i in range(2)]
    ob = [p.sbuf("ob%d" % i, [128, 1536], BF16) for i in range(2)]
    cv = ctab.rearrange("(t p) d -> p t d", p=128)
    sv = stab.rearrange("(t p) d -> p t d", p=128)
    ov = out.rearrange("(t p) d -> p t d", p=128)
    for t in range(nt):
        b = t % 2
        B = "%d" % b
        p.dma(ct[b][:], cv[:, t, :], w=["ct" + B], q="sp")
        p.dma(stt[b][:], sv[:, t, :], w=["stt" + B], q="sp")
        for cb in range(3):
            pb = 2 + (t * 3 + cb) % 6
            for k in range(8):
                p.op("pe", lambda e, k=k, t=t, cb=cb, pb=pb: e.matmul(
                    c.ps[pb][:, :], lhsT=hT[:, k, t * 128:(t + 1) * 128], rhs=wsb[:, k, cb * 512:(cb + 1) * 512],
                    start=(k == 0), stop=(k == 7)), r=["hT:%d" % t] + ["wsb:%d" % kk for kk in range(8)], w=["ps%d" % pb])
            p.op("act", lambda e, cb=cb, pb=pb, b=b: e.activation(out=xq[b][:, cb * 512:(cb + 1) * 512], in_=c.ps[pb][:, :], func=AF.Copy),
                 r=["ps%d" % pb], w=["xq%s:%d" % (B, cb)])
        xk = ["xq%s:%d" % (B, cb) for cb in range(3)]
        x = xq[b]
        p.op("dve", lambda e, b=b, x=x: e.tensor_tensor(out=sq[b][:], in0=x[:, 0:NQK], in1=x[:, 0:NQK], op=ALU.mult), r=xk, w=["sq" + B])
        p.op("dve", lambda e, b=b: e.tensor_reduce(out=ss[b][:], in_=sq[b][:].rearrange("p (h d) -> p h d", d=64), axis=AX.X, op=ALU.add),
             r=["sq" + B], w=["ss" + B])
        p.op("act", lambda e, b=b: e.activation(out=ss[b][:], in_=ss[b][:], func=AF.Sqrt, bias=qeps[:], scale=1.0 / 64),
             r=["ss" + B, "qeps"], w=["ss" + B])
        p.op("dve", lambda e, b=b: e.reciprocal(out=ss[b][:], in_=ss[b][:]), r=["ss" + B], w=["ss" + B])
        p.op("pool", lambda e, b=b, x=x: e.tensor_tensor(out=sq[b][:].rearrange("p (h d) -> p h d", d=64),
                                                       in0=x[:, 0:NQK].rearrange("p (h d) -> p h d", d=64),
                                                       in1=ss[b][:].unsqueeze(2).to_broadcast([128, 20, 64]), op=ALU.mult),
             r=xk + ["ss" + B, "sq" + B], w=["sq" + B])
        p.op("pool", lambda e, b=b: e.tensor_tensor(out=sq[b][:], in0=sq[b][:], in1=gb[:], op=ALU.mult), r=["sq" + B, "gainb"], w=["sq" + B])
        p.op("dve", lambda e, b=b: e.tensor_tensor(out=t1[b][:], in0=sq[b][:], in1=ct[b][:], op=ALU.mult), r=["sq" + B, "ct" + B], w=["t1" + B])
        xv4 = sq[b][:].rearrange("p (m two d) -> p m two d", two=2, d=16)
        sv4 = stt[b][:].rearrange("p (m two d) -> p m two d", two=2, d=16)
        cv4 = ct[b][:].rearrange("p (m two d) -> p m two d", two=2, d=16)
        p.op("pool", lambda e, xv4=xv4, sv4=sv4, cv4=cv4: e.tensor_tensor(out=cv4[:, :, 0, :], in0=xv4[:, :, 1, :], in1=sv4[:, :, 0, :], op=ALU.mult),
             r=["sq" + B, "stt" + B, "t1" + B], w=["ct" + B + "a"])
        p.op("pool", lambda e, xv4=xv4, sv4=sv4, cv4=cv4: e.tensor_tensor(out=cv4[:, :, 1, :], in0=xv4[:, :, 0, :], in1=sv4[:, :, 1, :], op=ALU.mult),
             r=["sq" + B, "stt" + B, "t1" + B], w=["ct" + B + "b"])
        p.op("dve", lambda e, b=b: e.tensor_tensor(out=ob[b][:, 0:NQK], in0=t1[b][:], in1=ct[b][:], op=ALU.add),
             r=["t1" + B, "ct" + B + "a", "ct" + B + "b"], w=["ob" + B + "q", "ct" + B])
        p.op("act", lambda e, b=b, x=x: e.activation(out=ob[b][:, NQK:1536], in_=x[:, NQK:1536], func=AF.Copy), r=xk, w=["ob" + B + "v"])
        p.dma(ov[:, t, :], ob[b][:], r=["ob" + B + "q", "ob" + B + "v"], w=["out:%d" % t], q="sp")
    p.op("sp", None, r=["out:%d" % t for t in range(nt)])
    return p.build()


DBG = {}
NK = 8320
NKT = 65
ARENA = 35328


def build_post(mode):
    nt = NT
    p = Prog()
    c = setup_common(p)
    hin = p.dram("hin", [NR, D])
    lnp = p.dram("lnp", [4, D])
    wg = p.dram("wg", [D, DFF]); wu = p.dram("wu", [D, DFF]); wd = p.dram("wd", [DFF, D])
    wo = p.dram("wo", [8, 128, D])
    hout = p.dram("hout", [NR, D], kind="ExternalOutput")
    h = p.sbuf("h", [128, nt, D], F32)
    A = p.sbuf("arenaA", [128, 8, NR], BF16)
    B = p.sbuf("arenaB", [128, ARENA], BF16)
    gb = p.sbuf("gb", [128, D], F32); bb = p.sbuf("bb", [128, D], F32)
    hv = hin.rearrange("(t p) d -> p t d", p=128)
    for t in range(nt):
        p.dma(h[:, t, :], hv[:, t, :], w=["h:%d" % t], q="sp")
    off = [0]

    def carve(n):
        a = B[:, off[0]:off[0] + n]
        off[0] += n
        return a

    if mode == "att":
        qT_d = p.dram("qT", [4, 128, 17 * 512], BF16)
        kT_d = p.dram("kT", [4, 128, NK], BF16)
        vA_d = p.dram("vA", [4, 128, NKT * 128], BF16)
        kT = carve(NK); vA = carve(NKT * 128).rearrange("p (k f) -> p k f", f=128); qT = carve(17 * 512)
        pT = [carve(512), carve(512), carve(512)]
        rb = p.sbuf("rb", [128, 512], F32)
        wo_sb = carve(8192).rearrange("p (k f) -> p k f", k=8)
        for k in range(8):
            p.dma(wo_sb[:, k, :], wo[k, :, :], w=["wo:%d" % k], q="pool")
        oT = A
        it = 0
        for g in range(DBG.get('ng', 4)):
            p.dma(kT[:, :], kT_d[g, :, :], w=["kT"], q="sp")
            p.dma(vA[:, :, :], vA_d[g, :, :].rearrange("p (k f) -> p k f", f=128), w=["vA"], q="sp")
            p.dma(qT[:, :], qT_d[g, :, :], w=["qT"], q="sp")
            for qb in range(DBG.get('nqb', 17)):
                ob = 4 + (g * 17 + qb) % 2
                db = 6 + (g * 17 + qb) % 2
                LOOK = 2

                def score(kt, qb=qb):
                    sb = 1 + kt % 3
                    p.op("pe", lambda e, kt=kt, qb=qb, sb=sb: e.matmul(
                        c.ps[sb][:, :], lhsT=kT[:, kt * 128:(kt + 1) * 128], rhs=qT[:, qb * 512:(qb + 1) * 512],
                        start=True, stop=True), r=["kT", "qT"], w=["ps%d" % sb])
                    pi = kt % 3
                    p.op("act", lambda e, sb=sb, pi=pi: e.activation(out=pT[pi][:, :], in_=c.ps[sb][:, :], func=AF.Exp, scale=0.125),
                         r=["ps%d" % sb], w=["pT%d" % pi])
                for kt in range(min(LOOK, NKT)):
                    score(kt)
                for kt in range(NKT):
                    if kt + LOOK < NKT:
                        score(kt + LOOK)
                    pi = kt % 3
                    p.op("pe", lambda e, kt=kt, ob=ob, pi=pi: e.matmul(
                        c.ps[ob][:, :], lhsT=vA[:, kt, :], rhs=pT[pi][:, :],
                        start=(kt == 0), stop=(kt == NKT - 1)), r=["vA", "pT%d" % pi], w=["ps%d" % ob])
                O = "ps%d" % ob
                p.op("act", lambda e, ob=ob: e.activation(out=rb[0:64, :], in_=c.ps[ob][64:128, :], func=AF.Ln), r=[O], w=["rb"])
                p.op("act", lambda e: e.activation(out=rb[0:64, :], in_=rb[0:64, :], func=AF.Exp, scale=-1.0), r=["rb"], w=["rb"])
                p.op("dve", lambda e, g=g, qb=qb, ob=ob: e.tensor_tensor(
                    out=oT[0:64, 2 * g:2 * g + 2, qb * 128:(qb + 1) * 128],
                    in0=c.ps[ob][0:64, 0:256].rearrange("p (j n) -> p j n", j=2),
                    in1=rb[0:64, 0:256].rearrange("p (j n) -> p j n", j=2), op=ALU.mult),
                    r=[O, "rb"], w=["xT:%d" % qb, O + "a"])
                p.op("dve", lambda e, g=g, qb=qb, ob=ob: e.tensor_tensor(
                    out=oT[64:128, 2 * g:2 * g + 2, qb * 128:(qb + 1) * 128],
                    in0=c.ps[ob][0:64, 256:512].rearrange("p (j n) -> p j n", j=2),
                    in1=rb[0:64, 256:512].rearrange("p (j n) -> p j n", j=2), op=ALU.mult),
                    r=[O, "rb"], w=["xT:%d" % qb, O + "b"])
                p.op("dve", None, r=[O + "a", O + "b"], w=[O])
        xT = oT
        w_sb = wo_sb
    else:
        yf_d = p.dram("yf", [NR, D]); yb_d = p.dram("yb", [NR, D])
        dsk = p.dram("dskip", [D])
        wglu = p.dram("wglu", [8, 128, D])
        gT = A
        zT = carve(8 * NR).rearrange("p (k n) -> p k n", k=8)
        wglu_sb = carve(8192).rearrange("p (k f) -> p k f", k=8)
        wo_sb = carve(8192).rearrange("p (k f) -> p k f", k=8)
        sgs = [carve(512), carve(512)]
        for k in range(8):
            p.dma(wglu_sb[:, k, :], wglu[k, :, :], w=["wglu:%d" % k], q="pool")
            p.dma(wo_sb[:, k, :], wo[k, :, :], w=["wo:%d" % k], q="pool")
        dB = gb
        p.dma(dB[:], dsk.partition_broadcast(128), w=["ln_g"])
        yt = [p.sbuf("yt%d" % i, [128, D], F32) for i in range(2)]
        _yt2 = p.sbuf("yt2", [128, D], F32)
        yt2 = [_yt2, _yt2]
        _g16 = p.sbuf("g16", [128, D], BF16)
        g16 = [_g16, _g16]
        yfv = yf_d.rearrange("(t p) d -> p t d", p=128); ybv = yb_d.rearrange("(t p) d -> p t d", p=128)
        for t in range(nt):
            b = t % 2
            Bk = "%d" % b
            p.dma(yt[b][:], yfv[:, t, :], w=["yt" + Bk], q="sp")
            p.dma(yt2[b][:], ybv[:, t, :], w=["yt2"], q="sp")
            p.op("dve", lambda e, b=b: e.tensor_tensor(out=yt[b][:], in0=yt[b][:], in1=yt2[b][:], op=ALU.add), r=["yt" + Bk, "yt2"], w=["yt" + Bk])
            p.op("pool", lambda e, b=b, t=t: e.tensor_tensor(out=yt2[b][:], in0=h[:, t, :], in1=dB[:], op=ALU.mult), r=["h:%d" % t, "ln_g", "yt" + Bk], w=["yt2"])
            p.op("dve", lambda e, b=b: e.tensor_tensor(out=yt[b][:], in0=yt[b][:], in1=yt2[b][:], op=ALU.add), r=["yt" + Bk, "yt2"], w=["yt" + Bk])
            p.op("act", lambda e, b=b: e.activation(out=g16[b][:], in_=yt[b][:], func=AF.Gelu), r=["yt" + Bk], w=["g16"])
            pb = c.ps[t % 2]
            pk = "ps%d" % (t % 2)
            pbb = pb[:].bitcast(BF16)
            for k in range(8):
                p.op("pe", lambda e, k=k, b=b, pbb=pbb: e.transpose(out=pbb[:, k * 128:(k + 1) * 128], in_=g16[b][:, k * 128:(k + 1) * 128], identity=c.identb[:]),
                     r=["g16", "identb"], w=[pk + ":%d" % k])
            p.op("dve", lambda e, t=t, pbb=pbb: e.tensor_copy(out=gT[:, :, t * 128:(t + 1) * 128], in_=pbb.rearrange("p (k n) -> p k n", k=8)),
                 r=[pk + ":%d" % k for k in range(8)], w=["gT:%d" % t])
        blocks = [(s, min(512, NR - s)) for s in range(0, NR, 512)]
        it = 0
        for m in range(8):
            for (s, n) in blocks:
                tiles = list(range(s // 128, (s + n) // 128))
                pg = 2 + it % 2
                si = it % 2
                it += 1
                for k in range(8):
                    p.op("pe", lambda e, k=k, m=m, s=s, n=n, pg=pg: e.matmul(
                        c.ps[pg][:, 0:n], lhsT=wglu_sb[:, k, m * 128:(m + 1) * 128], rhs=gT[:, k, s:s + n], start=(k == 0), stop=(k == 7)),
                        r=["wglu:%d" % k] + ["gT:%d" % t for t in tiles], w=["ps%d" % pg])
                p.op("act", lambda e, n=n, pg=pg, si=si: e.activation(out=sgs[si][:, 0:n], in_=c.ps[pg][:, 0:n], func=AF.Sigmoid), r=["ps%d" % pg], w=["sgs%d" % si])
                p.op("dve", lambda e, m=m, s=s, n=n, si=si: e.tensor_tensor(out=zT[:, m, s:s + n], in0=sgs[si][:, 0:n], in1=gT[:, m, s:s + n], op=ALU.mult),
                     r=["sgs%d" % si] + ["gT:%d" % t for t in tiles], w=["zT:%d:%d" % (m, t) for t in tiles])
        for t in range(nt):
            p.op("pool", None, r=["zT:%d:%d" % (m, t) for m in range(8)], w=["xT:%d" % t])
        xT = zT
        w_sb = wo_sb
    p.dma(gb[:], lnp[0, :].partition_broadcast(128), r=["ln_g"], w=["ln_g"])
    p.dma(bb[:], lnp[1, :].partition_broadcast(128), w=["ln_b"])
    for t in range(nt):
        for half in range(2):
            pd = 2 + (t * 2 + half) % 2
            for k in range(8):
                p.op("pe", lambda e, t=t, half=half, k=k, pd=pd: e.matmul(
                    c.ps[pd][:, :], lhsT=xT[:, k, t * 128:(t + 1) * 128], rhs=w_sb[:, k, half * 512:(half + 1) * 512],
                    start=(k == 0), stop=(k == 7)), r=["xT:%d" % t, "wo:%d" % k], w=["ps%d" % pd])
            p.op("dve", lambda e, t=t, half=half, pd=pd: e.scalar_tensor_tensor(
                out=h[:, t, half * 512:(half + 1) * 512], in0=h[:, t, half * 512:(half + 1) * 512], scalar=ALPHA,
                in1=c.ps[pd][:, :], op0=ALU.mult, op1=ALU.add), r=["ps%d" % pd, "h:%d" % t], w=["h:%d:%d" % (t, half)])
        p.op("pool", None, r=["h:%d:0" % t, "h:%d:1" % t], w=["h:%d" % t])
    layernorm(p, c, h, nt, "h", gb, bb, "ln")
    p.fence()
    p.dma(gb[:], lnp[2, :].partition_broadcast(128), w=["ln_g"])
    p.dma(bb[:], lnp[3, :].partition_broadcast(128), w=["ln_b"])
    off[0] = 0
    wgb = [carve(4096).rearrange("p (k f) -> p k f", k=8) for i in range(2)]
    wub = [carve(4096).rearrange("p (k f) -> p k f", k=8) for i in range(2)]
    wdb = [carve(4096).rearrange("p (k f) -> p k f", k=4) for i in range(2)]
    actT = carve(4 * NR).rearrange("p (k n) -> p k n", k=4)
    sg = [carve(512), carve(512)]
    hT = A
    make_T(p, c, h, hT, nt, "h")
    ffn(p, c, h, hT, nt, "h", wg, wu, wd, (wgb, wub, wdb, actT, sg))
    layernorm(p, c, h, nt, "h", gb, bb, "ln")
    ov = hout.rearrange("(t p) d -> p t d", p=128)
    for t in range(nt):
        p.dma(ov[:, t, :], h[:, t, :], r=["h:%d" % t], w=["out:%d" % t])
    p.op("sp", None, r=["out:%d" % t for t in range(nt)])
    return p.build()


S5_NC = 1152
S5_PAD = 1024
S5_EV = [float(e) for e in range(-7, 8)] + [8.0 * 2 ** k for k in range(11)]
S5_NE = len(S5_EV)


def s5_eidx(e):
    return S5_EV.index(float(e))


def build_s5scan():
    p = Prog()
    c = setup_common(p)
    NC = S5_NC
    U_d = p.dram("U", [2, 128, 16, NC])
    lam_d = p.dram("lam", [2, 3, 128, 8])
    b_d = p.dram("bpar", [2, 2, 128, 8 * 16])
    c_d = p.dram("cpar", [2, 2, 128, 8 * 16])
    ev_d = p.dram("ev", [128, S5_NE])
    mask_d = p.dram("mask", [128, 128])
    Y_d = p.dram("Y", [2, 128, 9, 16 * 128], kind="ExternalOutput")
    Ub = p.sbuf("Ub", [128, 16, NC], BF16)
    ev = p.sbuf("ev", [128, S5_NE], F32)
    mask = p.sbuf("mask", [128, 128], F32)
    p.dma(ev[:], ev_d[:, :], w=["ev"])
    p.dma(mask[:], mask_d[:, :], w=["mask"])
    lam = p.sbuf("lam", [128, 3, 8], F32)
    bp = p.sbuf("bp", [128, 2, 8, 16], F32)
    cp = p.sbuf("cp", [128, 2, 8, 16], F32)
    NE = S5_NE

    def T(name, shape, dt=F32):
        return p.sbuf(name, shape, dt)
    dt_ = T("dt", [128, 8]); x_ = T("x", [128, 8]); th_ = T("th", [128, 8])
    xe = T("xe", [128, 8, NE]); the = T("the", [128, 8, NE]); mag = T("mag", [128, 8, NE])
    ys = T("ys", [128, 8, NE]); yc = T("yc", [128, 8, NE]); ki = T("ki", [128, 8, NE], mybir.dt.int32)
    kf = T("kf", [128, 8, NE]); m1 = T("m1", [128, 8, NE])
    pr = T("pr", [128, 8, NE]); pi_ = T("pi", [128, 8, NE]); npi = T("npi", [128, 8, NE])
    s1 = [T("s1_%d" % i, [128, 8]) for i in range(8)]
    bbr = T("bbr", [128, 8, 16]); bbi = T("bbi", [128, 8, 16]); nci = T("nci", [128, 8, 16])
    t1 = T("t1", [128, 8, 16]); t2 = T("t2", [128, 8, 16])
    BinR = T("BinR", [128, 8, 128], BF16); BinI = T("BinI", [128, 8, 128], BF16)
    CoR = T("CoR", [128, 8, 128], BF16); CoIn = T("CoIn", [128, 8, 128], BF16)
    BinT = T("BinT", [128, 8, 2, 2, 128], BF16)
    p.op("pool", lambda e: e.memset(BinT[:], 0.0), w=["BinTz"])
    Kin = T("Kin", [128, 16, 128], BF16)
    Q = [[T("Q%d%d" % (i, j), [128, S5_PAD + NC]) for j in range(2)] for i in range(2)]
    V = [T("V%d" % j, [128, NC]) for j in range(2)]
    Zb = [T("Zb%d" % j, [128, NC], BF16) for j in range(2)]
    ysb = [T("ysb%d" % i, [128, 9, 256]) for i in range(2)]
    for i in range(2):
        for j in range(2):
            p.op("pool", lambda e, i=i, j=j: e.memset(Q[i][j][:, 0:S5_PAD], 0.0), w=["Qpad%d%d" % (i, j)])

    def dve(fn, r, w):
        p.op("dve", fn, r=r, w=w)

    def bc8(ap):
        return ap.unsqueeze(2).to_broadcast([128, 8, 16])

    for d in range(2):
        for g in range(16):
            p.dma(Ub[:, g, :], U_d[d, :, g, :], w=["U:%d" % g], q="pool")
        for i in range(3):
            p.dma(lam[:, i, :], lam_d[d, i, :, :], w=["lam"], q="sp")
        for i in range(2):
            p.dma(bp[:, i, :, :], b_d[d, i, :, :].rearrange("p (a c) -> p a c", c=16), w=["bp"], q="sp")
            p.dma(cp[:, i, :, :], c_d[d, i, :, :].rearrange("p (a c) -> p a c", c=16), w=["cp"], q="sp")
        lr, li, ldt = lam[:, 0, :], lam[:, 1, :], lam[:, 2, :]
        p.op("act", lambda e: e.activation(out=dt_[:], in_=ldt, func=AF.Exp), r=["lam"], w=["dt"])
        dve(lambda e: e.tensor_tensor(out=x_[:], in0=lr, in1=dt_[:], op=ALU.mult), ["lam", "dt"], ["x"])
        dve(lambda e: e.tensor_tensor(out=th_[:], in0=li, in1=dt_[:], op=ALU.mult), ["lam", "dt"], ["th"])
        evb = ev[:].unsqueeze(1).to_broadcast([128, 8, NE])
        dve(lambda e: e.tensor_tensor(out=xe[:], in0=x_[:].unsqueeze(2).to_broadcast([128, 8, NE]), in1=evb, op=ALU.mult), ["x", "ev"], ["xe"])
        dve(lambda e: e.tensor_tensor(out=the[:], in0=th_[:].unsqueeze(2).to_broadcast([128, 8, NE]), in1=evb, op=ALU.mult), ["th", "ev"], ["the"])
        p.op("act", lambda e: e.activation(out=mag[:], in_=xe[:], func=AF.Exp), r=["xe"], w=["mag"])
        inv2pi = 1.0 / (2.0 * np.pi)
        dve(lambda e: e.tensor_scalar(out=ys[:], in0=the[:], scalar1=inv2pi, scalar2=None, op0=ALU.mult), ["the"], ["ys"])
        dve(lambda e: e.tensor_scalar(out=yc[:], in0=the[:], scalar1=inv2pi, scalar2=0.25, op0=ALU.mult, op1=ALU.add), ["the"], ["yc"])
        for (yy, nm) in ((ys, "ys"), (yc, "yc")):
            dve(lambda e, yy=yy: e.tensor_copy(out=ki[:], in_=yy[:]), [nm], ["ki"])
            dve(lambda e: e.tensor_copy(out=kf[:], in_=ki[:]), ["ki"], ["kf"])
            dve(lambda e, yy=yy: e.tensor_tensor(out=yy[:], in0=yy[:], in1=kf[:], op=ALU.subtract), [nm, "kf"], [nm])
            dve(lambda e, yy=yy: e.tensor_scalar(out=m1[:], in0=yy[:], scalar1=0.5, scalar2=None, op0=ALU.is_gt), [nm], ["m1"])
            dve(lambda e, yy=yy: e.tensor_tensor(out=yy[:], in0=yy[:], in1=m1[:], op=ALU.subtract), [nm, "m1"], [nm])
            dve(lambda e, yy=yy: e.tensor_scalar(out=m1[:], in0=yy[:], scalar1=-0.5, scalar2=None, op0=ALU.is_lt), [nm], ["m1"])
            dve(lambda e, yy=yy: e.tensor_tensor(out=yy[:], in0=yy[:], in1=m1[:], op=ALU.add), [nm, "m1"], [nm])
        p.op("act", lambda e: e.activation(out=ys[:], in_=ys[:], func=AF.Sin, scale=2.0 * np.pi), r=["ys"], w=["ys"])
        p.op("act", lambda e: e.activation(out=yc[:], in_=yc[:], func=AF.Sin, scale=2.0 * np.pi), r=["yc"], w=["yc"])
        dve(lambda e: e.tensor_tensor(out=pr[:], in0=mag[:], in1=yc[:], op=ALU.mult), ["mag", "yc"], ["pr"])
        dve(lambda e: e.tensor_tensor(out=pi_[:], in0=mag[:], in1=ys[:], op=ALU.mult), ["mag", "ys"], ["pi"])
        dve(lambda e: e.tensor_scalar(out=npi[:], in0=pi_[:], scalar1=-1.0, scalar2=None, op0=ALU.mult), ["pi"], ["npi"])
        e1 = s5_eidx(1)
        a1r, a1i = pr[:, :, e1], pi_[:, :, e1]
        nr, den, rden, crr, cii, tA, tB = s1[0], s1[1], s1[2], s1[3], s1[4], s1[5], s1[6]
        dve(lambda e: e.tensor_scalar(out=nr[:], in0=a1r, scalar1=-1.0, scalar2=None, op0=ALU.add), ["pr"], ["nr"])
        dve(lambda e: e.tensor_tensor(out=den[:], in0=lr, in1=lr, op=ALU.mult), ["lam"], ["den"])
        dve(lambda e: e.tensor_tensor(out=tA[:], in0=li, in1=li, op=ALU.mult), ["lam"], ["tA"])
        dve(lambda e: e.tensor_tensor(out=den[:], in0=den[:], in1=tA[:], op=ALU.add), ["den", "tA"], ["den"])
        dve(lambda e: e.reciprocal(out=rden[:], in_=den[:]), ["den"], ["rden"])
        dve(lambda e: e.tensor_tensor(out=crr[:], in0=nr[:], in1=lr, op=ALU.mult), ["nr", "lam"], ["crr"])
        dve(lambda e: e.tensor_tensor(out=tA[:], in0=a1i, in1=li, op=ALU.mult), ["pi", "lam", "den"], ["tA"])
        dve(lambda e: e.tensor_tensor(out=crr[:], in0=crr[:], in1=tA[:], op=ALU.add), ["crr", "tA"], ["crr"])
        dve(lambda e: e.tensor_tensor(out=crr[:], in0=crr[:], in1=rden[:], op=ALU.mult), ["crr", "rden"], ["crr"])
        dve(lambda e: e.tensor_tensor(out=cii[:], in0=a1i, in1=lr, op=ALU.mult), ["pi", "lam"], ["cii"])
        dve(lambda e: e.tensor_tensor(out=tB[:], in0=nr[:], in1=li, op=ALU.mult), ["nr", "lam"], ["tB"])
        dve(lambda e: e.tensor_tensor(out=cii[:], in0=cii[:], in1=tB[:], op=ALU.subtract), ["cii", "tB"], ["cii"])
        dve(lambda e: e.tensor_tensor(out=cii[:], in0=cii[:], in1=rden[:], op=ALU.mult), ["cii", "rden"], ["cii"])
        br, bi = bp[:, 0, :, :], bp[:, 1, :, :]
        cr, ci = cp[:, 0, :, :], cp[:, 1, :, :]
        dve(lambda e: e.tensor_tensor(out=bbr[:], in0=br, in1=bc8(crr[:]), op=ALU.mult), ["bp", "crr"], ["bbr"])
        dve(lambda e: e.tensor_tensor(out=t1[:], in0=bi, in1=bc8(cii[:]), op=ALU.mult), ["bp", "cii"], ["t1"])
        dve(lambda e: e.tensor_tensor(out=bbr[:], in0=bbr[:], in1=t1[:], op=ALU.subtract), ["bbr", "t1"], ["bbr"])
        dve(lambda e: e.tensor_tensor(out=bbi[:], in0=bi, in1=bc8(crr[:]), op=ALU.mult), ["bp", "crr"], ["bbi"])
        dve(lambda e: e.tensor_tensor(out=t1[:], in0=br, in1=bc8(cii[:]), op=ALU.mult), ["bp", "cii", "bbr"], ["t1"])
        dve(lambda e: e.tensor_tensor(out=bbi[:], in0=bbi[:], in1=t1[:], op=ALU.add), ["bbi", "t1"], ["bbi"])
        dve(lambda e: e.tensor_scalar(out=nci[:], in0=ci, scalar1=-1.0, scalar2=None, op0=ALU.mult), ["cp"], ["nci"])
        BinRv = BinR[:].rearrange("p a (t c) -> p a t c", c=16); BinIv = BinI[:].rearrange("p a (t c) -> p a t c", c=16)
        CoRv = CoR[:].rearrange("p a (t c) -> p a t c", c=16); CoInv = CoIn[:].rearrange("p a (t c) -> p a t c", c=16)
        for tau in range(8):
            en = s5_eidx(-tau); ep = s5_eidx(tau)
            prn, pin, npin = bc8(pr[:, :, en]), bc8(pi_[:, :, en]), bc8(npi[:, :, en])
            prp, pip, npip = bc8(pr[:, :, ep]), bc8(pi_[:, :, ep]), bc8(npi[:, :, ep])
            dve(lambda e, prn=prn: e.tensor_tensor(out=t1[:], in0=bbr[:], in1=prn, op=ALU.mult), ["bbr", "pr", "bbi"], ["t1"])
            dve(lambda e, npin=npin: e.tensor_tensor(out=t2[:], in0=bbi[:], in1=npin, op=ALU.mult), ["bbi", "npi"], ["t2"])
            dve(lambda e, tau=tau: e.tensor_tensor(out=BinRv[:, :, tau, :], in0=t1[:], in1=t2[:], op=ALU.add), ["t1", "t2"], ["BinR:%d" % tau])
            dve(lambda e, prn=prn: e.tensor_tensor(out=t1[:], in0=bbi[:], in1=prn, op=ALU.mult), ["bbi", "pr", "BinR:%d" % tau], ["t1"])
            dve(lambda e, pin=pin: e.tensor_tensor(out=t2[:], in0=bbr[:], in1=pin, op=ALU.mult), ["bbr", "pi", "BinR:%d" % tau], ["t2"])
            dve(lambda e, tau=tau: e.tensor_tensor(out=BinIv[:, :, tau, :], in0=t1[:], in1=t2[:], op=ALU.add), ["t1", "t2"], ["BinI:%d" % tau])
            dve(lambda e, prp=prp: e.tensor_tensor(out=t1[:], in0=cr, in1=prp, op=ALU.mult), ["cp", "pr", "BinI:%d" % tau], ["t1"])
            dve(lambda e, pip=pip: e.tensor_tensor(out=t2[:], in0=nci[:], in1=pip, op=ALU.mult), ["nci", "pi", "BinI:%d" % tau], ["t2"])
            dve(lambda e, tau=tau: e.tensor_tensor(out=CoRv[:, :, tau, :], in0=t1[:], in1=t2[:], op=ALU.add), ["t1", "t2"], ["CoR:%d" % tau])
            dve(lambda e, npip=npip: e.tensor_tensor(out=t1[:], in0=cr, in1=npip, op=ALU.mult), ["cp", "npi", "CoR:%d" % tau], ["t1"])
            dve(lambda e, prp=prp: e.tensor_tensor(out=t2[:], in0=nci[:], in1=prp, op=ALU.mult), ["nci", "pr", "CoR:%d" % tau], ["t2"])
            dve(lambda e, tau=tau: e.tensor_tensor(out=CoInv[:, :, tau, :], in0=t1[:], in1=t2[:], op=ALU.add), ["t1", "t2"], ["CoIn:%d" % tau])
        allB = ["BinR:%d" % t for t in range(8)] + ["BinI:%d" % t for t in range(8)]
        allC = ["CoR:%d" % t for t in range(8)] + ["CoIn:%d" % t for t in range(8)]
        if DBG.get('stage', 9) < 2:
            p.fence(); continue
        for gp in range(8):
            pb = c.ps[gp % 2]
            pk = "ps%d" % (gp % 2)
            pbb = pb[:].bitcast(BF16)
            for ri, Bt in enumerate((BinR, BinI)):
                p.op("pe", lambda e, gp=gp, ri=ri, Bt=Bt, pbb=pbb: e.transpose(out=pbb[:, ri * 128:(ri + 1) * 128], in_=Bt[:, gp, :], identity=c.identb[:]),
                     r=allB + ["identb"], w=[pk + ":%d" % ri])
            for g2 in range(2):
                p.op("act", lambda e, gp=gp, pbb=pbb, g2=g2: e.activation(
                    out=BinT[:, gp, :, g2, 64 * g2:64 * g2 + 64],
                    in_=pbb[:, 0:256].rearrange("p (r n) -> p r n", r=2)[:, :, 64 * g2:64 * g2 + 64], func=AF.Copy),
                    r=[pk + ":0", pk + ":1", "BinTz"], w=["BinT:%d:%d" % (gp, g2)])
        if DBG.get('stage', 9) < 3:
            p.fence(); continue
        for g in range(16):
            gp, g2 = g // 2, g % 2
            pb = 2 + g % 2
            lo, hi = 64 * g2, 64 * g2 + 64
            p.op("pe", lambda e, gp=gp, lo=lo, hi=hi, pb=pb: e.matmul(c.ps[pb][:, 0:128], lhsT=BinR[lo:hi, gp, :], rhs=CoR[lo:hi, gp, :], start=True, stop=False),
                 r=allB + allC, w=["ps%d" % pb])
            p.op("pe", lambda e, gp=gp, lo=lo, hi=hi, pb=pb: e.matmul(c.ps[pb][:, 0:128], lhsT=BinI[lo:hi, gp, :], rhs=CoIn[lo:hi, gp, :], start=False, stop=True),
                 r=allB + allC, w=["ps%d" % pb])
            dve(lambda e, g=g, pb=pb: e.tensor_tensor(out=Kin[:, g, :], in0=c.ps[pb][:, 0:128], in1=mask[:], op=ALU.mult), ["ps%d" % pb, "mask"], ["Kin:%d" % g])
        p.fence()
        if DBG.get('stage', 9) < 4:
            continue
        blocks = [(0, 512), (512, 512), (1024, 128)]
        for gp in range(8):
            for ri in range(2):
                for bi_, (s, n) in enumerate(blocks):
                    bank = 2 + ri * 3 + bi_
                    for g2 in range(2):
                        p.op("pe", lambda e, gp=gp, ri=ri, g2=g2, s=s, n=n, bank=bank: e.matmul(
                            c.ps[bank][:, 0:n], lhsT=BinT[:, gp, ri, g2, :], rhs=Ub[:, 2 * gp + g2, s:s + n],
                            start=(g2 == 0), stop=(g2 == 1)),
                            r=["BinT:%d:0" % gp, "BinT:%d:1" % gp, "U:%d" % (2 * gp + g2)], w=["ps%d:0" % bank, "ps%d:1" % bank])
                    if DBG.get('nomm'):
                        continue
                    if not DBG.get('nov'):
                      p.op("act", lambda e, ri=ri, s=s, n=n, bank=bank: e.activation(out=V[ri][:, s:s + n], in_=c.ps[bank][:, 0:n], func=AF.Copy),
                         r=["ps%d:0" % bank, "ps%d:1" % bank], w=["V%d:%d" % (ri, bi_)])
                    p.op("pool", lambda e, ri=ri, s=s, n=n: e.tensor_copy(out=Q[0][ri][:, S5_PAD + s:S5_PAD + s + n], in_=V[ri][:, s:s + n]),
                         r=["V%d:%d" % (ri, bi_)], w=["Q0%d" % ri])
            if DBG.get('stage', 9) < 5:
                continue
            P0 = S5_PAD
            for k in range(11):
                sh = 2 ** k
                src, dst = Q[k % 2], Q[(k + 1) % 2]
                sn, dn = "Q%d" % (k % 2), "Q%d" % ((k + 1) % 2)
                ek = 15 + k
                Ar, Ai, nAi = pr[:, gp, ek:ek + 1], pi_[:, gp, ek:ek + 1], npi[:, gp, ek:ek + 1]
                dve(lambda e, src=src, dst=dst, sh=sh, Ar=Ar: e.scalar_tensor_tensor(
                    out=dst[0][:, P0:P0 + NC], in0=src[0][:, P0 - sh:P0 - sh + NC], scalar=Ar, in1=src[0][:, P0:P0 + NC], op0=ALU.mult, op1=ALU.add),
                    [sn + "0", "pr", "Qpad%d0" % (k % 2)], [dn + "0"])
                dve(lambda e, src=src, dst=dst, sh=sh, nAi=nAi: e.scalar_tensor_tensor(
                    out=dst[0][:, P0:P0 + NC], in0=src[1][:, P0 - sh:P0 - sh + NC], scalar=nAi, in1=dst[0][:, P0:P0 + NC], op0=ALU.mult, op1=ALU.add),
                    [sn + "1", "npi", dn + "0", "Qpad%d1" % (k % 2)], [dn + "0"])
                dve(lambda e, src=src, dst=dst, sh=sh, Ai=Ai: e.scalar_tensor_tensor(
                    out=dst[1][:, P0:P0 + NC], in0=src[0][:, P0 - sh:P0 - sh + NC], scalar=Ai, in1=src[1][:, P0:P0 + NC], op0=ALU.mult, op1=ALU.add),
                    [sn + "0", sn + "1", "pi"], [dn + "1"])
                dve(lambda e, src=src, dst=dst, sh=sh, Ar=Ar: e.scalar_tensor_tensor(
                    out=dst[1][:, P0:P0 + NC], in0=src[1][:, P0 - sh:P0 - sh + NC], scalar=Ar, in1=dst[1][:, P0:P0 + NC], op0=ALU.mult, op1=ALU.add),
                    [sn + "1", "pr", dn + "1"], [dn + "1"])
            fin = Q[11 % 2]
            fn_ = "Q%d" % (11 % 2)
            for ri in range(2):
                p.op("pool", lambda e, ri=ri, fin=fin: e.tensor_tensor(out=Zb[ri][:], in0=fin[ri][:, P0:P0 + NC], in1=V[ri][:], op=ALU.subtract),
                     r=[fn_ + "%d" % ri] + ["V%d:%d" % (ri, b_) for b_ in range(3)], w=["Zb%d" % ri])
            if DBG.get('stage', 9) < 6:
                continue
            yb_ = ysb[gp % 2]
            yk = "ysb%d" % (gp % 2)
            for t in range(9):
                bank = t % 2
                for g2 in range(2):
                    lo, hi = 64 * g2, 64 * g2 + 64
                    g = 2 * gp + g2
                    o = c.ps[bank][:, g2 * 128:(g2 + 1) * 128]
                    p.op("pe", lambda e, t=t, lo=lo, hi=hi, gp=gp, o=o: e.matmul(o, lhsT=Zb[0][lo:hi, t * 128:(t + 1) * 128], rhs=CoR[lo:hi, gp, :], start=True, stop=False),
                         r=["Zb0"] + allC, w=["ps%d" % bank])
                    p.op("pe", lambda e, t=t, lo=lo, hi=hi, gp=gp, o=o: e.matmul(o, lhsT=Zb[1][lo:hi, t * 128:(t + 1) * 128], rhs=CoIn[lo:hi, gp, :], start=False, stop=False),
                         r=["Zb1"] + allC, w=["ps%d" % bank])
                    p.op("pe", lambda e, t=t, g=g, o=o: e.matmul(o, lhsT=Ub[:, g, t * 128:(t + 1) * 128], rhs=Kin[:, g, :], start=False, stop=True),
                         r=["U:%d" % g, "Kin:%d" % g], w=["ps%d" % bank])
                p.op("act", lambda e, t=t, bank=bank, yb_=yb_: e.activation(out=yb_[:, t, :], in_=c.ps[bank][:, 0:256], func=AF.Copy),
                     r=["ps%d" % bank], w=[yk + ":%d" % t])
            p.dma(Y_d[d, :, :, gp * 256:(gp + 1) * 256], yb_[:, :, :], r=[yk + ":%d" % t for t in range(9)], w=["Yout:%d:%d" % (d, gp)] + [yk + ":%d" % t for t in range(9)], q="sp")
        p.fence()
    p.op("sp", None, r=[])
    return p.build()


L_TOK = 8208
S5_LP = S5_NC * 8


def s5_layout_inputs(hb, j, lam_re, lam_im, log_dt, b_re, b_im, c_re, c_im):
    gs = slice(16 * j, 16 * j + 16)
    hp = np.zeros((2, S5_LP, 256), np.float32)
    hp[0, :L_TOK] = hb[:, 256 * j:256 * j + 256]
    hp[1, :L_TOK] = hb[::-1, 256 * j:256 * j + 256]
    U = hp.reshape(2, S5_NC, 8, 16, 16).transpose(0, 2, 4, 3, 1).reshape(2, 128, 16, S5_NC)

    def gp_layout(a):
        sh = a.shape
        a = a.reshape((2, 8, 2, 64) + sh[3:])
        a = np.moveaxis(a, 1, 3)
        return a.reshape((2, 128, 8) + sh[3:])
    lam = np.stack([gp_layout(lam_re[:, gs]), gp_layout(lam_im[:, gs]),
                    gp_layout(np.broadcast_to(log_dt[:, gs, None], (2, 16, 64)))], 1)
    bpar = np.stack([gp_layout(b_re[:, gs]), gp_layout(b_im[:, gs])], 1).reshape(2, 2, 128, 128)
    cT_re = np.swapaxes(c_re[:, gs], 2, 3); cT_im = np.swapaxes(c_im[:, gs], 2, 3)
    cpar = np.stack([gp_layout(cT_re), gp_layout(cT_im)], 1).reshape(2, 2, 128, 128)
    return dict(U=np.ascontiguousarray(U), lam=np.ascontiguousarray(lam.astype(np.float32)),
                bpar=np.ascontiguousarray(bpar), cpar=np.ascontiguousarray(cpar))


def s5_consts():
    ev = np.broadcast_to(np.array(S5_EV, np.float32)[None, :], (128, S5_NE)).copy()
    tau = np.arange(128) // 16
    mask = (tau[None, :] >= tau[:, None]).astype(np.float32)
    return dict(ev=ev, mask=mask)


def s5_unlayout(Y):
    Y = Y.reshape(2, 128, 9, 16, 8, 16)
    y = Y.transpose(0, 2, 1, 4, 3, 5).reshape(2, S5_LP, 256)
    yf = y[0, :L_TOK]
    yb = y[1, :L_TOK][::-1]
    return yf, yb


N_META = 16
SEQ = 8192
ROWS_PER_CORE = L_TOK // 4


def _rows_split(a):
    out = []
    for b in range(2):
        for q in range(4):
            blk = np.zeros((NR,) + a.shape[2:], a.dtype)
            blk[:ROWS_PER_CORE] = a[b, q * ROWS_PER_CORE:(q + 1) * ROWS_PER_CORE]
            out.append(blk)
    return out


def _rows_join(blocks, width, dtype):
    out = np.zeros((2, L_TOK, width), dtype)
    for b in range(2):
        for q in range(4):
            out[b, q * ROWS_PER_CORE:(q + 1) * ROWS_PER_CORE] = blocks[b * 4 + q][:ROWS_PER_CORE]
    return out


def _rope_tables():
    n_real = SEQ
    row_ids = np.concatenate([np.zeros(N_META), np.repeat(np.arange(n_real // 64), 64)]).astype(np.float32)
    col_ids = np.concatenate([np.zeros(N_META), np.tile(np.arange(64), n_real // 64)]).astype(np.float32)
    inv_freq = (np.float32(10000.0) ** (-(np.arange(0, 32, 2, dtype=np.float32) / np.float32(32.0)))).astype(np.float32)
    ang_r = (row_ids[:, None] * inv_freq[None, :]).astype(np.float32)
    ang_c = (col_ids[:, None] * inv_freq[None, :]).astype(np.float32)
    cr, sr, cc, sc = np.cos(ang_r), np.sin(ang_r), np.cos(ang_c), np.sin(ang_c)
    C1 = np.concatenate([cr, cr, cc, cc], -1)
    S1 = np.concatenate([-sr, sr, -sc, sc], -1)
    ctab = np.tile(C1, (1, 20)).astype(np.float32)
    stab = np.tile(S1, (1, 20)).astype(np.float32)
    return ctab, stab


def _run(nc, in_maps):
    res = run_bass_kernel_spmd(nc, in_maps, core_ids=list(range(8)))
    return res.results


def _ffn_ln_inputs(i, inputs):
    lnp = np.stack([inputs["ln_gain"][i, 0], inputs["ln_bias"][i, 0], inputs["ln_gain"][i, 1], inputs["ln_bias"][i, 1]]).astype(np.float32)
    return dict(lnp=np.ascontiguousarray(lnp), wg=np.ascontiguousarray(inputs["ffn_w_gate"][i]),
                wu=np.ascontiguousarray(inputs["ffn_w_up"][i]), wd=np.ascontiguousarray(inputs["ffn_w_down"][i]))


def kernel(**inputs):
    import ml_dtypes
    bf = ml_dtypes.bfloat16
    inputs = {k: np.asarray(v) for k, v in inputs.items()}
    x = inputs["x"].astype(np.float32)
    meta = np.broadcast_to(inputs["meta_tokens"].astype(np.float32)[None], (2, N_META, D))
    h = np.concatenate([meta, x], axis=1)
    ctab, stab = _rope_tables()
    ctab_b = np.broadcast_to(ctab[None], (2,) + ctab.shape)
    stab_b = np.broadcast_to(stab[None], (2,) + stab.shape)
    ct_rows = _rows_split(ctab_b); st_rows = _rows_split(stab_b)
    consts = s5_consts()
    valid = (np.arange(NK) < L_TOK).astype(bf).reshape(NKT, 128, 1)
    for i in range(4):
        j = i // 2
        common = _ffn_ln_inputs(i, inputs)
        hrows = _rows_split(h)
        if i % 2 == 0:
            in_maps = []
            for b in range(2):
                for q in range(4):
                    m = s5_layout_inputs(h[b], q, inputs["s5_lambda_re"][j], inputs["s5_lambda_im"][j], inputs["s5_log_dt"][j],
                                         inputs["s5_b_re"][j], inputs["s5_b_im"][j], inputs["s5_c_re"][j], inputs["s5_c_im"][j])
                    m.update(consts)
                    in_maps.append(m)
            res = _run(build_s5scan(), in_maps)
            yf = np.zeros((2, L_TOK, D), np.float32); yb = np.zeros((2, L_TOK, D), np.float32)
            for b in range(2):
                for q in range(4):
                    a, bb_ = s5_unlayout(res[b * 4 + q]["Y"])
                    yf[b, :, 256 * q:256 * q + 256] = a
                    yb[b, :, 256 * q:256 * q + 256] = bb_
            yfr = _rows_split(yf); ybr = _rows_split(yb)
            in_maps = []
            for cidx in range(8):
                m = dict(common)
                m.update(hin=hrows[cidx], yf=yfr[cidx], yb=ybr[cidx], dskip=np.ascontiguousarray(inputs["s5_d"][j]),
                         wglu=np.ascontiguousarray(inputs["s5_w_glu"][j].reshape(8, 128, D)),
                         wo=np.ascontiguousarray(inputs["s5_w_out"][j].reshape(8, 128, D)))
                in_maps.append(m)
            res = _run(build_post("s5"), in_maps)
        else:
            gain = np.concatenate([np.tile(inputs["attn_q_gain"][j], 16), np.tile(inputs["attn_k_gain"][j], 4)]).astype(np.float32)
            in_maps = [dict(hin=hrows[cidx], wqkv=np.ascontiguousarray(inputs["attn_w_qkv"][j]), gain=gain,
                            ctab=ct_rows[cidx], stab=st_rows[cidx]) for cidx in range(8)]
            res = _run(build_qkv(), in_maps)
            qkv = _rows_join([r["qkv"] for r in res], 1536, bf)
            Wo = inputs["attn_w_out"][j]
            wo_p = np.zeros((8, 128, D), np.float32)
            for g in range(4):
                for ii in range(2):
                    wo_p[2 * g + ii, 0:64] = Wo[64 * (4 * g + ii):64 * (4 * g + ii) + 64]
                    wo_p[2 * g + ii, 64:128] = Wo[64 * (4 * g + 2 + ii):64 * (4 * g + 2 + ii) + 64]
            in_maps = []
            for b in range(2):
                kpad = np.zeros((NK, 4, 64), bf); kpad[:L_TOK] = qkv[b, :, 1024:1280].reshape(L_TOK, 4, 64)
                kT = np.zeros((4, 128, NK), bf); kT[:, 0:64, :] = kpad.transpose(1, 2, 0)
                vpad = np.zeros((NK, 4, 64), bf); vpad[:L_TOK] = qkv[b, :, 1280:1536].reshape(L_TOK, 4, 64)
                vA = np.zeros((4, NKT, 128, 128), bf)
                for g in range(4):
                    vv = vpad[:, g, :].reshape(NKT, 128, 64)
                    vA[g, :, :, 0:64] = vv; vA[g, :, :, 64:128] = valid
                vA = np.ascontiguousarray(vA.transpose(0, 2, 1, 3)).reshape(4, 128, NKT * 128)
                for q in range(4):
                    qrows = np.zeros((NR, 16, 64), bf)
                    qrows[:ROWS_PER_CORE] = qkv[b, q * ROWS_PER_CORE:(q + 1) * ROWS_PER_CORE, 0:1024].reshape(ROWS_PER_CORE, 16, 64)
                    qT = np.zeros((4, 128, 17, 4, 128), bf)
                    for g in range(4):
                        for jj in range(4):
                            qT[g, 0:64, :, jj, :] = qrows[:, 4 * g + jj, :].reshape(17, 128, 64).transpose(2, 0, 1)
                    m = dict(common)
                    m.update(hin=hrows[b * 4 + q], qT=qT.reshape(4, 128, 17 * 512), kT=kT, vA=vA, wo=wo_p)
                    in_maps.append(m)
            res = _run(build_post("att"), in_maps)
        h = _rows_join([r["hout"] for r in res], D, np.float32)
    return np.ascontiguousarray(h[:, N_META:, :]).astype(np.float32)
```

```python
import numpy as np
from contextlib import ExitStack
import concourse.bass as bass
import concourse.mybir as mybir
from concourse.bass_utils import run_bass_kernel_spmd
from concourse.alu_op_type import AluOpType as ALU

F32 = mybir.dt.float32
BF16 = mybir.dt.bfloat16
AF = mybir.ActivationFunctionType
AX = mybir.AxisListType

ENGS = ["pe", "dve", "act", "pool", "sp"]
N_DMA_SEMS = 8


class Prog:
    def __init__(self, name="k"):
        self.nc = bass.Bass("TRN2", target_bir_lowering=False)
        self.es = ExitStack()
        self.ins = []
        self.last_w = {}
        self.readers = {}
        self.dram_in = {}
        self.dram_out = {}

    def dram(self, name, shape, dtype=F32, kind="ExternalInput"):
        t = self.nc.dram_tensor(name, list(shape), dtype, kind=kind)
        return t.ap()

    def sbuf(self, name, shape, dtype=F32):
        return self.es.enter_context(self.nc.sbuf_tensor("sb_" + name, list(shape), dtype))

    def psum(self, name, shape, dtype=F32):
        return self.es.enter_context(self.nc.psum_tensor("pm_" + name, list(shape), dtype))

    def op(self, eng, fn, r=(), w=(), dma=False):
        deps = set()
        for k in r:
            if k in self.last_w:
                deps.add(self.last_w[k])
        for k in w:
            if k in self.last_w:
                deps.add(self.last_w[k])
            for i in self.readers.get(k, ()):
                deps.add(i)
        idx = len(self.ins)
        self.ins.append(dict(eng=eng, fn=fn, deps=deps, dma=dma))
        for k in w:
            self.last_w[k] = idx
            self.readers[k] = []
        for k in r:
            self.readers.setdefault(k, []).append(idx)
        return idx

    def dma(self, out, in_, r=(), w=(), q="sp", **kw):
        return self.op(q, lambda e: e.dma_start(out=out, in_=in_, **kw), r, w, dma=True)

    def build(self):
        nc = self.nc
        ins = self.ins
        target = [False] * len(ins)
        for rec in ins:
            for d in rec["deps"]:
                if rec["eng"] == "pe" and ins[d]["eng"] == "pe":
                    continue
                target[d] = True
        sems = {e: self.es.enter_context(nc.semaphore("s_" + e)) for e in ENGS}
        dsems = [self.es.enter_context(nc.semaphore("d%d" % i)) for i in range(N_DMA_SEMS)]
        cnt = {e: 0 for e in ENGS}
        dcnt = [0] * N_DMA_SEMS
        dnext = 0
        seen = {e: {} for e in ENGS}
        sig = [None] * len(ins)
        progs = {e: [] for e in ENGS}
        for i, rec in enumerate(ins):
            e = rec["eng"]
            waits = []
            for d in sorted(rec["deps"]):
                if e == "pe" and ins[d]["eng"] == "pe":
                    continue
                sh, sn, val = sig[d]
                if seen[e].get(sn, 0) < val:
                    seen[e][sn] = val
                    waits.append((sh, val))
            inc = None
            if rec["dma"]:
                j = dnext
                dnext = (dnext + 1) % N_DMA_SEMS
                if dcnt[j] > 0 and seen[e].get("d%d" % j, 0) < dcnt[j]:
                    seen[e]["d%d" % j] = dcnt[j]
                    waits.append((dsems[j], dcnt[j]))
                dcnt[j] += 16
                sig[i] = (dsems[j], "d%d" % j, dcnt[j])
                inc = (dsems[j], 16)
            elif target[i]:
                cnt[e] += 1
                sig[i] = (sems[e], "s_" + e, cnt[e])
                inc = (sems[e], 1)
                seen[e]["s_" + e] = max(seen[e].get("s_" + e, 0), 0)
            else:
                sig[i] = (None, None, 0)
            progs[e].append((waits, rec["fn"], inc))
        self.final = {e: cnt[e] for e in ENGS}
        self.max_sem = max(list(cnt.values()) + dcnt)

        def runner(ename):
            def body(eng):
                for waits, fn, inc in progs[ename]:
                    for sh, val in waits:
                        eng.wait_ge(sh, val)
                    if fn is None:
                        if inc is None:
                            continue
                        r = eng.nop()
                    else:
                        r = fn(eng)
                    if inc is not None:
                        r.then_inc(inc[0], inc[1])
            return body

        with nc.Block() as block:
            block.tensor(runner("pe"))
            block.vector(runner("dve"))
            block.scalar(runner("act"))
            block.gpsimd(runner("pool"))
            block.sync(runner("sp"))
        self.es.close()
        return nc


def _fence(self):
    lasts = set()
    seen_e = {}
    for i, rec in enumerate(self.ins):
        if rec["dma"]:
            lasts.add(i)
        else:
            seen_e[rec["eng"]] = i
    start = getattr(self, "_fence_at", 0)
    deps = set(i for i in lasts if i >= start) | set(seen_e.values())
    self._fence_at = len(self.ins)
    for e in ENGS:
        self.ins.append(dict(eng=e, fn=None, deps=set(deps), dma=False))
    self.last_w = {}
    self.readers = {}


Prog.fence = _fence


D = 1024
DFF = 2816
NT = 17
NR = NT * 128
ALPHA = 8.0 ** 0.25
LN_EPS = 1e-5
FG = 512


class Ctx:
    pass


def setup_common(p):
    c = Ctx()
    c.identf = p.sbuf("identf", [128, 128], F32)
    c.identb = p.sbuf("identb", [128, 128], BF16)
    p.op("pool", lambda e: e.memset(c.identf[:], 0.0), w=["identf"])
    p.op("pool", lambda e: e.affine_select(out=c.identf[:], in_=c.identf[:], pattern=[[-1, 128]],
                                           compare_op=ALU.not_equal, fill=1.0, base=0, channel_multiplier=1),
         r=["identf"], w=["identf"])
    p.op("dve", lambda e: e.tensor_copy(out=c.identb[:], in_=c.identf[:]), r=["identf"], w=["identb"])
    c.ps = [p.psum("ps%d" % i, [128, 512], F32) for i in range(8)]
    c.epsc = p.sbuf("epsc", [128, 1], F32)
    p.op("pool", lambda e: e.memset(c.epsc[:], LN_EPS), w=["epsc"])
    return c


def make_T(p, c, src, dstT, nt, name, psb=(0, 1), nparts=128):
    tmp = [p.sbuf("%s_cb%d" % (name, i), [128, 1024], BF16) for i in range(2)]
    for t in range(nt):
        tb = tmp[t % 2]
        tk = "%s_cb%d" % (name, t % 2)
        eng = "act" if t % 2 == 0 else "pool"
        if eng == "act":
            p.op("act", lambda e, t=t, tb=tb: e.activation(out=tb[:], in_=src[:, t, :], func=AF.Copy),
                 r=["%s:%d" % (name, t)], w=[tk])
        else:
            p.op("pool", lambda e, t=t, tb=tb: e.tensor_copy(out=tb[:], in_=src[:, t, :]),
                 r=["%s:%d" % (name, t)], w=[tk])
        pb = c.ps[psb[t % 2]]
        pk = "ps%d" % psb[t % 2]
        pbb = pb[:].bitcast(BF16)
        for k in range(8):
            p.op("pe", lambda e, k=k, tb=tb, pbb=pbb: e.transpose(out=pbb[:, k * 128:(k + 1) * 128],
                                                                 in_=tb[:, k * 128:(k + 1) * 128], identity=c.identb[:]),
                 r=[tk, "identb"], w=[pk + ":%d" % k])
        p.op("dve", lambda e, t=t, pbb=pbb: e.tensor_copy(
            out=dstT[:, :, t * 128:(t + 1) * 128], in_=pbb.rearrange("p (k n) -> p k n", k=8)),
            r=[pk + ":%d" % k for k in range(8)], w=["%sT:%d" % (name, t)])


_LN_UID = [0]


def layernorm(p, c, h, nt, name, gb, bb, lname):
    _LN_UID[0] += 1
    u = "%d" % _LN_UID[0]
    st = p.sbuf(lname + "_st" + u, [128, 2, 12], F32)
    mv = p.sbuf(lname + "_mv" + u, [128, 2, 2], F32)
    rs = p.sbuf(lname + "_rs" + u, [128, 2, 1], F32)
    nm = p.sbuf(lname + "_nm" + u, [128, 2, 1], F32)
    for t in range(nt):
        b = t % 2
        hk = "%s:%d" % (name, t)
        sk = "%s_st%d" % (lname, b)
        p.op("dve", lambda e, t=t, b=b: e.bn_stats(out=st[:, b, 0:6], in_=h[:, t, 0:512]), r=[hk], w=[sk + "a"])
        p.op("dve", lambda e, t=t, b=b: e.bn_stats(out=st[:, b, 6:12], in_=h[:, t, 512:1024]), r=[hk], w=[sk + "b"])
        p.op("dve", lambda e, b=b: e.bn_aggr(out=mv[:, b, :], in_=st[:, b, :]), r=[sk + "a", sk + "b"], w=[lname + "_mv%d" % b])
        p.op("act", lambda e, b=b: e.activation(out=rs[:, b, :], in_=mv[:, b, 1:2], func=AF.Sqrt, bias=c.epsc[:], scale=1.0),
             r=[lname + "_mv%d" % b, "epsc"], w=[lname + "_sd%d" % b])
        p.op("dve", lambda e, b=b: e.reciprocal(out=rs[:, b, :], in_=rs[:, b, :]), r=[lname + "_sd%d" % b], w=[lname + "_sd%d" % b])
        p.op("dve", lambda e, b=b: e.tensor_scalar(out=nm[:, b, :], in0=mv[:, b, 0:1], scalar1=rs[:, b, :], scalar2=-1.0, op0=ALU.mult, op1=ALU.mult),
             r=[lname + "_mv%d" % b, lname + "_sd%d" % b], w=[lname + "_nm%d" % b])
        p.op("act", lambda e, t=t, b=b: e.activation(out=h[:, t, :], in_=h[:, t, :], func=AF.Identity, scale=rs[:, b, :], bias=nm[:, b, :]),
             r=[hk, lname + "_nm%d" % b, lname + "_sd%d" % b], w=[hk])
        p.op("dve", lambda e, t=t: e.tensor_tensor(out=h[:, t, :], in0=h[:, t, :], in1=gb[:], op=ALU.mult), r=[hk, lname + "_g"], w=[hk])
        p.op("pool", lambda e, t=t: e.tensor_tensor(out=h[:, t, :], in0=h[:, t, :], in1=bb[:], op=ALU.add), r=[hk, lname + "_b"], w=[hk])


def ffn(p, c, h, hT, nt, name, wg, wu, wd, bufs):
    wgb, wub, wdb, actT, sg = bufs
    nrows = nt * 128
    wgv = wg.rearrange("(k p) f -> p k f", p=128)
    wuv = wu.rearrange("(k p) f -> p k f", p=128)
    wdv = wd.rearrange("(c p) d -> p c d", p=128)
    ngroups = (DFF + FG - 1) // FG
    blocks = [(s, min(512, nrows - s)) for s in range(0, nrows, 512)]
    it = 0
    for g in range(ngroups):
        f0 = g * FG
        fw = min(FG, DFF - f0)
        nfc = fw // 128
        b = g % 2
        p.dma(wgb[b][:, :, 0:fw], wgv[:, :, f0:f0 + fw], w=["wg%d" % b], q="pool")
        p.dma(wub[b][:, :, 0:fw], wuv[:, :, f0:f0 + fw], w=["wu%d" % b], q="pool")
        p.dma(wdb[b][:, 0:nfc, :], wdv[:, f0 // 128:f0 // 128 + nfc, :], w=["wd%d" % b], q="pool")
        for fc in range(nfc):
            for (s, n) in blocks:
                tiles = list(range(s // 128, (s + n) // 128))
                pg, pu = (4, 5) if it % 2 == 0 else (6, 7)
                sgi = it % 2
                it += 1
                for k in range(8):
                    p.op("pe", lambda e, k=k, b=b, fc=fc, s=s, n=n, pg=pg: e.matmul(
                        c.ps[pg][:, 0:n], lhsT=wgb[b][:, k, fc * 128:(fc + 1) * 128], rhs=hT[:, k, s:s + n],
                        start=(k == 0), stop=(k == 7)),
                        r=["wg%d" % b] + ["%sT:%d" % (name, t) for t in tiles], w=["ps%d" % pg])
                for k in range(8):
                    p.op("pe", lambda e, k=k, b=b, fc=fc, s=s, n=n, pu=pu: e.matmul(
                        c.ps[pu][:, 0:n], lhsT=wub[b][:, k, fc * 128:(fc + 1) * 128], rhs=hT[:, k, s:s + n],
                        start=(k == 0), stop=(k == 7)),
                        r=["wu%d" % b] + ["%sT:%d" % (name, t) for t in tiles], w=["ps%d" % pu])
                p.op("act", lambda e, n=n, pg=pg, sgi=sgi: e.activation(out=sg[sgi][:, 0:n], in_=c.ps[pg][:, 0:n], func=AF.Silu),
                     r=["ps%d" % pg], w=["sg%d" % sgi])
                p.op("dve", lambda e, n=n, s=s, fc=fc, pu=pu, sgi=sgi: e.tensor_tensor(
                    out=actT[:, fc, s:s + n], in0=sg[sgi][:, 0:n], in1=c.ps[pu][:, 0:n], op=ALU.mult),
                    r=["sg%d" % sgi, "ps%d" % pu], w=["actT:%d:%d" % (fc, t) for t in tiles])
        for t in range(nt):
            for half in range(2):
                pd = 2 + (t * 2 + half) % 2
                for fc in range(nfc):
                    p.op("pe", lambda e, t=t, half=half, fc=fc, b=b, pd=pd, nfc=nfc: e.matmul(
                        c.ps[pd][:, :], lhsT=actT[:, fc, t * 128:(t + 1) * 128], rhs=wdb[b][:, fc, half * 512:(half + 1) * 512],
                        start=(fc == 0), stop=(fc == nfc - 1)),
                        r=["actT:%d:%d" % (fc, t), "wd%d" % b], w=["ps%d" % pd])
                hk = "%s:%d:%d" % (name, t, half)
                if g == 0:
                    p.op("dve", lambda e, t=t, half=half, pd=pd: e.scalar_tensor_tensor(
                        out=h[:, t, half * 512:(half + 1) * 512], in0=h[:, t, half * 512:(half + 1) * 512], scalar=ALPHA,
                        in1=c.ps[pd][:, :], op0=ALU.mult, op1=ALU.add),
                        r=["ps%d" % pd, "%s:%d" % (name, t)], w=[hk])
                else:
                    p.op("dve", lambda e, t=t, half=half, pd=pd: e.tensor_tensor(
                        out=h[:, t, half * 512:(half + 1) * 512], in0=h[:, t, half * 512:(half + 1) * 512],
                        in1=c.ps[pd][:, :], op=ALU.add),
                        r=["ps%d" % pd, hk], w=[hk])
    for t in range(nt):
        p.op("pool", None, r=["%s:%d:0" % (name, t), "%s:%d:1" % (name, t)], w=["%s:%d" % (name, t)])


def alloc_ffn_bufs(p, nt):
    wgb = [p.sbuf("wgb%d" % i, [128, 8, FG], BF16) for i in range(2)]
    wub = [p.sbuf("wub%d" % i, [128, 8, FG], BF16) for i in range(2)]
    wdb = [p.sbuf("wdb%d" % i, [128, FG // 128, 1024], BF16) for i in range(2)]
    actT = p.sbuf("actT", [128, FG // 128, nt * 128], BF16)
    sg = [p.sbuf("sg%d" % i, [128, 512], BF16) for i in range(2)]
    return wgb, wub, wdb, actT, sg


QK_EPS = 1e-6
NQK = 1280


def build_qkv():
    nt = NT
    p = Prog()
    c = setup_common(p)
    hin = p.dram("hin", [NR, D])
    wqkv = p.dram("wqkv", [D, 1536])
    gain = p.dram("gain", [NQK])
    ctab = p.dram("ctab", [NR, NQK])
    stab = p.dram("stab", [NR, NQK])
    out = p.dram("qkv", [NR, 1536], BF16, kind="ExternalOutput")
    h = p.sbuf("h", [128, nt, D], F32)
    hT = p.sbuf("hT", [128, 8, NR], BF16)
    wsb = p.sbuf("wsb", [128, 8, 1536], BF16)
    gb = p.sbuf("gainb", [128, NQK], F32)
    qeps = p.sbuf("qeps", [128, 1], F32)
    p.op("pool", lambda e: e.memset(qeps[:], QK_EPS), w=["qeps"])
    hv = hin.rearrange("(t p) d -> p t d", p=128)
    for t in range(nt):
        p.dma(h[:, t, :], hv[:, t, :], w=["h:%d" % t], q="sp")
    p.dma(gb[:], gain.partition_broadcast(128), w=["gainb"])
    wv = wqkv.rearrange("(k p) f -> p k f", p=128)
    for k in range(8):
        p.dma(wsb[:, k, :], wv[:, k, :], w=["wsb:%d" % k], q="pool")
    make_T(p, c, h, hT, nt, "h")
    xq = [p.sbuf("xq%d" % i, [128, 1536], F32) for i in range(2)]
    sq = [p.sbuf("sq%d" % i, [128, NQK], F32) for i in range(2)]
    t1 = [p.sbuf("t1%d" % i, [128, NQK], F32) for i in range(2)]
    ct = [p.sbuf("ct%d" % i, [128, NQK], F32) for i in range(2)]
    stt = [p.sbuf("stt%d" % i, [128, NQK], F32) for i in range(2)]
    ss = [p.sbuf("ss%d" % i, [128, 20], F32) for i in range(2)]
    ob = [p.sbuf("ob%d" % i, [128, 1536], BF16) for i in range(2)]
    cv = ctab.rearrange("(t p) d -> p t d", p=128)
    sv = stab.rearrange("(t p) d -> p t d", p=128)
    ov = out.rearrange("(t p) d -> p t d", p=128)
    for t in range(nt):
        b = t % 2
        B = "%d" % b
        p.dma(ct[b][:], cv[:, t, :], w=["ct" + B], q="sp")
        p.dma(stt[b][:], sv[:, t, :], w=["stt" + B], q="sp")
        for cb in range(3):
            pb = 2 + (t * 3 + cb) % 6
            for k in range(8):
                p.op("pe", lambda e, k=k, t=t, cb=cb, pb=pb: e.matmul(
                    c.ps[pb][:, :], lhsT=hT[:, k, t * 128:(t + 1) * 128], rhs=wsb[:, k, cb * 512:(cb + 1) * 512],
                    start=(k == 0), stop=(k == 7)), r=["hT:%d" % t] + ["wsb:%d" % kk for kk in range(8)], w=["ps%d" % pb])
            p.op("act", lambda e, cb=cb, pb=pb, b=b: e.activation(out=xq[b][:, cb * 512:(cb + 1) * 512], in_=c.ps[pb][:, :], func=AF.Copy),
                 r=["ps%d" % pb], w=["xq%s:%d" % (B, cb)])
        xk = ["xq%s:%d" % (B, cb) for cb in range(3)]
        x = xq[b]
        p.op("dve", lambda e, b=b, x=x: e.tensor_tensor(out=sq[b][:], in0=x[:, 0:NQK], in1=x[:, 0:NQK], op=ALU.mult), r=xk, w=["sq" + B])
        p.op("dve", lambda e, b=b: e.tensor_reduce(out=ss[b][:], in_=sq[b][:].rearrange("p (h d) -> p h d", d=64), axis=AX.X, op=ALU.add),
             r=["sq" + B], w=["ss" + B])
        p.op("act", lambda e, b=b: e.activation(out=ss[b][:], in_=ss[b][:], func=AF.Sqrt, bias=qeps[:], scale=1.0 / 64),
             r=["ss" + B, "qeps"], w=["ss" + B])
        p.op("dve", lambda e, b=b: e.reciprocal(out=ss[b][:], in_=ss[b][:]), r=["ss" + B], w=["ss" + B])
        p.op("pool", lambda e, b=b, x=x: e.tensor_tensor(out=sq[b][:].rearrange("p (h d) -> p h d", d=64),
                                                       in0=x[:, 0:NQK].rearrange("p (h d) -> p h d", d=64),
                                                       in1=ss[b][:].unsqueeze(2).to_broadcast([128, 20, 64]), op=ALU.mult),
             r=xk + ["ss" + B, "sq" + B], w=["sq" + B])
        p.op("pool", lambda e, b=b: e.tensor_tensor(out=sq[b][:], in0=sq[b][:], in1=gb[:], op=ALU.mult), r=["sq" + B, "gainb"], w=["sq" + B])
        p.op("dve", lambda e, b=b: e.tensor_tensor(out=t1[b][:], in0=sq[b][:], in1=ct[b][:], op=ALU.mult), r=["sq" + B, "ct" + B], w=["t1" + B])
        xv4 = sq[b][:].rearrange("p (m two d) -> p m two d", two=2, d=16)
        sv4 = stt[b][:].rearrange("p (m two d) -> p m two d", two=2, d=16)
        cv4 = ct[b][:].rearrange("p (m two d) -> p m two d", two=2, d=16)
        p.op("pool", lambda e, xv4=xv4, sv4=sv4, cv4=cv4: e.tensor_tensor(out=cv4[:, :, 0, :], in0=xv4[:, :, 1, :], in1=sv4[:, :, 0, :], op=ALU.mult),
             r=["sq" + B, "stt" + B, "t1" + B], w=["ct" + B + "a"])
        p.op("pool", lambda e, xv4=xv4, sv4=sv4, cv4=cv4: e.tensor_tensor(out=cv4[:, :, 1, :], in0=xv4[:, :, 0, :], in1=sv4[:, :, 1, :], op=ALU.mult),
             r=["sq" + B, "stt" + B, "t1" + B], w=["ct" + B + "b"])
        p.op("dve", lambda e, b=b: e.tensor_tensor(out=ob[b][:, 0:NQK], in0=t1[b][:], in1=ct[b][:], op=ALU.add),
             r=["t1" + B, "ct" + B + "a", "ct" + B + "b"], w=["ob" + B + "q", "ct" + B])
        p.op("act", lambda e, b=b, x=x: e.activation(out=ob[b][:, NQK:1536], in_=x[:, NQK:1536], func=AF.Copy), r=xk, w=["ob" + B + "v"])
        p.dma(ov[:, t, :], ob[b][:], r=["ob" + B + "q", "ob" + B + "v"], w=["out:%d" % t], q="sp")
    p.op("sp", None, r=["out:%d" % t for t in range(nt)])
    return p.build()


DBG = {}
NK = 8320
NKT = 65
ARENA = 35328


def build_post(mode):
    nt = NT
    p = Prog()
    c = setup_common(p)
    hin = p.dram("hin", [NR, D])
    lnp = p.dram("lnp", [4, D])
    wg = p.dram("wg", [D, DFF]); wu = p.dram("wu", [D, DFF]); wd = p.dram("wd", [DFF, D])
    wo = p.dram("wo", [8, 128, D])
    hout = p.dram("hout", [NR, D], kind="ExternalOutput")
    h = p.sbuf("h", [128, nt, D], F32)
    A = p.sbuf("arenaA", [128, 8, NR], BF16)
    B = p.sbuf("arenaB", [128, ARENA], BF16)
    gb = p.sbuf("gb", [128, D], F32); bb = p.sbuf("bb", [128, D], F32)
    hv = hin.rearrange("(t p) d -> p t d", p=128)
    for t in range(nt):
        p.dma(h[:, t, :], hv[:, t, :], w=["h:%d" % t], q="sp")
    off = [0]

    def carve(n):
        a = B[:, off[0]:off[0] + n]
        off[0] += n
        return a

    if mode == "att":
        qT_d = p.dram("qT", [4, 128, 17 * 512], BF16)
        kT_d = p.dram("kT", [4, 128, NK], BF16)
        vA_d = p.dram("vA", [4, 128, NKT * 128], BF16)
        kT = carve(NK); vA = carve(NKT * 128).rearrange("p (k f) -> p k f", f=128); qT = carve(17 * 512)
        pT = [carve(512), carve(512), carve(512)]
        rb = p.sbuf("rb", [128, 512], F32)
        wo_sb = carve(8192).rearrange("p (k f) -> p k f", k=8)
        for k in range(8):
            p.dma(wo_sb[:, k, :], wo[k, :, :], w=["wo:%d" % k], q="pool")
        oT = A
        it = 0
        for g in range(DBG.get('ng', 4)):
            p.dma(kT[:, :], kT_d[g, :, :], w=["kT"], q="sp")
            p.dma(vA[:, :, :], vA_d[g, :, :].rearrange("p (k f) -> p k f", f=128), w=["vA"], q="sp")
            p.dma(qT[:, :], qT_d[g, :, :], w=["qT"], q="sp")
            for qb in range(DBG.get('nqb', 17)):
                ob = 4 + (g * 17 + qb) % 2
                db = 6 + (g * 17 + qb) % 2
                LOOK = 2

                def score(kt, qb=qb):
                    sb = 1 + kt % 3
                    p.op("pe", lambda e, kt=kt, qb=qb, sb=sb: e.matmul(
                        c.ps[sb][:, :], lhsT=kT[:, kt * 128:(kt + 1) * 128], rhs=qT[:, qb * 512:(qb + 1) * 512],
                        start=True, stop=True), r=["kT", "qT"], w=["ps%d" % sb])
                    pi = kt % 3
                    p.op("act", lambda e, sb=sb, pi=pi: e.activation(out=pT[pi][:, :], in_=c.ps[sb][:, :], func=AF.Exp, scale=0.125),
                         r=["ps%d" % sb], w=["pT%d" % pi])
                for kt in range(min(LOOK, NKT)):
                    score(kt)
                for kt in range(NKT):
                    if kt + LOOK < NKT:
                        score(kt + LOOK)
                    pi = kt % 3
                    p.op("pe", lambda e, kt=kt, ob=ob, pi=pi: e.matmul(
                        c.ps[ob][:, :], lhsT=vA[:, kt, :], rhs=pT[pi][:, :],
                        start=(kt == 0), stop=(kt == NKT - 1)), r=["vA", "pT%d" % pi], w=["ps%d" % ob])
                O = "ps%d" % ob
                p.op("act", lambda e, ob=ob: e.activation(out=rb[0:64, :], in_=c.ps[ob][64:128, :], func=AF.Ln), r=[O], w=["rb"])
                p.op("act", lambda e: e.activation(out=rb[0:64, :], in_=rb[0:64, :], func=AF.Exp, scale=-1.0), r=["rb"], w=["rb"])
                p.op("dve", lambda e, g=g, qb=qb, ob=ob: e.tensor_tensor(
                    out=oT[0:64, 2 * g:2 * g + 2, qb * 128:(qb + 1) * 128],
                    in0=c.ps[ob][0:64, 0:256].rearrange("p (j n) -> p j n", j=2),
                    in1=rb[0:64, 0:256].rearrange("p (j n) -> p j n", j=2), op=ALU.mult),
                    r=[O, "rb"], w=["xT:%d" % qb, O + "a"])
                p.op("dve", lambda e, g=g, qb=qb, ob=ob: e.tensor_tensor(
                    out=oT[64:128, 2 * g:2 * g + 2, qb * 128:(qb + 1) * 128],
                    in0=c.ps[ob][0:64, 256:512].rearrange("p (j n) -> p j n", j=2),
                    in1=rb[0:64, 256:512].rearrange("p (j n) -> p j n", j=2), op=ALU.mult),
                    r=[O, "rb"], w=["xT:%d" % qb, O + "b"])
                p.op("dve", None, r=[O + "a", O + "b"], w=[O])
        xT = oT
        w_sb = wo_sb
    else:
        yf_d = p.dram("yf", [NR, D]); yb_d = p.dram("yb", [NR, D])
        dsk = p.dram("dskip", [D])
        wglu = p.dram("wglu", [8, 128, D])
        gT = A
        zT = carve(8 * NR).rearrange("p (k n) -> p k n", k=8)
        wglu_sb = carve(8192).rearrange("p (k f) -> p k f", k=8)
        wo_sb = carve(8192).rearrange("p (k f) -> p k f", k=8)
        sgs = [carve(512), carve(512)]
        for k in range(8):
            p.dma(wglu_sb[:, k, :], wglu[k, :, :], w=["wglu:%d" % k], q="pool")
            p.dma(wo_sb[:, k, :], wo[k, :, :], w=["wo:%d" % k], q="pool")
        dB = gb
        p.dma(dB[:], dsk.partition_broadcast(128), w=["ln_g"])
        yt = [p.sbuf("yt%d" % i, [128, D], F32) for i in range(2)]
        _yt2 = p.sbuf("yt2", [128, D], F32)
        yt2 = [_yt2, _yt2]
        _g16 = p.sbuf("g16", [128, D], BF16)
        g16 = [_g16, _g16]
        yfv = yf_d.rearrange("(t p) d -> p t d", p=128); ybv = yb_d.rearrange("(t p) d -> p t d", p=128)
        for t in range(nt):
            b = t % 2
            Bk = "%d" % b
            p.dma(yt[b][:], yfv[:, t, :], w=["yt" + Bk], q="sp")
            p.dma(yt2[b][:], ybv[:, t, :], w=["yt2"], q="sp")
            p.op("dve", lambda e, b=b: e.tensor_tensor(out=yt[b][:], in0=yt[b][:], in1=yt2[b][:], op=ALU.add), r=["yt" + Bk, "yt2"], w=["yt" + Bk])
            p.op("pool", lambda e, b=b, t=t: e.tensor_tensor(out=yt2[b][:], in0=h[:, t, :], in1=dB[:], op=ALU.mult), r=["h:%d" % t, "ln_g", "yt" + Bk], w=["yt2"])
            p.op("dve", lambda e, b=b: e.tensor_tensor(out=yt[b][:], in0=yt[b][:], in1=yt2[b][:], op=ALU.add), r=["yt" + Bk, "yt2"], w=["yt" + Bk])
            p.op("act", lambda e, b=b: e.activation(out=g16[b][:], in_=yt[b][:], func=AF.Gelu), r=["yt" + Bk], w=["g16"])
            pb = c.ps[t % 2]
            pk = "ps%d" % (t % 2)
            pbb = pb[:].bitcast(BF16)
            for k in range(8):
                p.op("pe", lambda e, k=k, b=b, pbb=pbb: e.transpose(out=pbb[:, k * 128:(k + 1) * 128], in_=g16[b][:, k * 128:(k + 1) * 128], identity=c.identb[:]),
                     r=["g16", "identb"], w=[pk + ":%d" % k])
            p.op("dve", lambda e, t=t, pbb=pbb: e.tensor_copy(out=gT[:, :, t * 128:(t + 1) * 128], in_=pbb.rearrange("p (k n) -> p k n", k=8)),
                 r=[pk + ":%d" % k for k in range(8)], w=["gT:%d" % t])
        blocks = [(s, min(512, NR - s)) for s in range(0, NR, 512)]
        it = 0
        for m in range(8):
            for (s, n) in blocks:
                tiles = list(range(s // 128, (s + n) // 128))
                pg = 2 + it % 2
                si = it % 2
                it += 1
                for k in range(8):
                    p.op("pe", lambda e, k=k, m=m, s=s, n=n, pg=pg: e.matmul(
                        c.ps[pg][:, 0:n], lhsT=wglu_sb[:, k, m * 128:(m + 1) * 128], rhs=gT[:, k, s:s + n], start=(k == 0), stop=(k == 7)),
                        r=["wglu:%d" % k] + ["gT:%d" % t for t in tiles], w=["ps%d" % pg])
                p.op("act", lambda e, n=n, pg=pg, si=si: e.activation(out=sgs[si][:, 0:n], in_=c.ps[pg][:, 0:n], func=AF.Sigmoid), r=["ps%d" % pg], w=["sgs%d" % si])
                p.op("dve", lambda e, m=m, s=s, n=n, si=si: e.tensor_tensor(out=zT[:, m, s:s + n], in0=sgs[si][:, 0:n], in1=gT[:, m, s:s + n], op=ALU.mult),
                     r=["sgs%d" % si] + ["gT:%d" % t for t in tiles], w=["zT:%d:%d" % (m, t) for t in tiles])
        for t in range(nt):
            p.op("pool", None, r=["zT:%d:%d" % (m, t) for m in range(8)], w=["xT:%d" % t])
        xT = zT
        w_sb = wo_sb
    p.dma(gb[:], lnp[0, :].partition_broadcast(128), r=["ln_g"], w=["ln_g"])
    p.dma(bb[:], lnp[1, :].partition_broadcast(128), w=["ln_b"])
    for t in range(nt):
        for half in range(2):
            pd = 2 + (t * 2 + half) % 2
            for k in range(8):
                p.op("pe", lambda e, t=t, half=half, k=k, pd=pd: e.matmul(
                    c.ps[pd][:, :], lhsT=xT[:, k, t * 128:(t + 1) * 128], rhs=w_sb[:, k, half * 512:(half + 1) * 512],
                    start=(k == 0), stop=(k == 7)), r=["xT:%d" % t, "wo:%d" % k], w=["ps%d" % pd])
            p.op("dve", lambda e, t=t, half=half, pd=pd: e.scalar_tensor_tensor(
                out=h[:, t, half * 512:(half + 1) * 512], in0=h[:, t, half * 512:(half + 1) * 512], scalar=ALPHA,
                in1=c.ps[pd][:, :], op0=ALU.mult, op1=ALU.add), r=["ps%d" % pd, "h:%d" % t], w=["h:%d:%d" % (t, half)])
        p.op("pool", None, r=["h:%d:0" % t, "h:%d:1" % t], w=["h:%d" % t])
    layernorm(p, c, h, nt, "h", gb, bb, "ln")
    p.fence()
    p.dma(gb[:], lnp[2, :].partition_broadcast(128), w=["ln_g"])
    p.dma(bb[:], lnp[3, :].partition_broadcast(128), w=["ln_b"])
    off[0] = 0
    wgb = [carve(4096).rearrange("p (k f) -> p k f", k=8) for i in range(2)]
    wub = [carve(4096).rearrange("p (k f) -> p k f", k=8) for i in range(2)]
    wdb = [carve(4096).rearrange("p (k f) -> p k f", k=4) for i in range(2)]
    actT = carve(4 * NR).rearrange("p (k n) -> p k n", k=4)
    sg = [carve(512), carve(512)]
    hT = A
    make_T(p, c, h, hT, nt, "h")
    ffn(p, c, h, hT, nt, "h", wg, wu, wd, (wgb, wub, wdb, actT, sg))
    layernorm(p, c, h, nt, "h", gb, bb, "ln")
    ov = hout.rearrange("(t p) d -> p t d", p=128)
    for t in range(nt):
        p.dma(ov[:, t, :], h[:, t, :], r=["h:%d" % t], w=["out:%d" % t])
    p.op("sp", None, r=["out:%d" % t for t in range(nt)])
    return p.build()


S5_NC = 1152
S5_PAD = 1024
S5_EV = [float(e) for e in range(-7, 8)] + [8.0 * 2 ** k for k in range(11)] + [8.0 * (k + 1) for k in range(16)]
S5_NE = len(S5_EV)


def s5_eidx(e):
    return S5_EV.index(float(e))


def build_s5scan():
    p = Prog()
    c = setup_common(p)
    NC = S5_NC
    U_d = p.dram("U", [2, 128, 16, NC])
    lam_d = p.dram("lam", [2, 3, 128, 8])
    b_d = p.dram("bpar", [2, 2, 128, 8 * 16])
    c_d = p.dram("cpar", [2, 2, 128, 8 * 16])
    ev_d = p.dram("ev", [128, S5_NE])
    mask_d = p.dram("mask", [128, 128])
    Y_d = p.dram("Y", [2, 128, 9, 16 * 128], kind="ExternalOutput")
    Ub = p.sbuf("Ub", [128, 16, NC], BF16)
    ev = p.sbuf("ev", [128, S5_NE], F32)
    mask = p.sbuf("mask", [128, 128], F32)
    p.dma(ev[:], ev_d[:, :], w=["ev"])
    p.dma(mask[:], mask_d[:, :], w=["mask"])
    lam = p.sbuf("lam", [128, 3, 8], F32)
    bp = p.sbuf("bp", [128, 2, 8, 16], F32)
    cp = p.sbuf("cp", [128, 2, 8, 16], F32)
    NE = S5_NE

    def T(name, shape, dt=F32):
        return p.sbuf(name, shape, dt)
    dt_ = T("dt", [128, 8]); x_ = T("x", [128, 8]); th_ = T("th", [128, 8])
    xe = T("xe", [128, 8, NE]); the = T("the", [128, 8, NE]); mag = T("mag", [128, 8, NE])
    ys = T("ys", [128, 8, NE]); yc = T("yc", [128, 8, NE]); ki = T("ki", [128, 8, NE], mybir.dt.int32)
    kf = T("kf", [128, 8, NE]); m1 = T("m1", [128, 8, NE])
    pr = T("pr", [128, 8, NE]); pi_ = T("pi", [128, 8, NE]); npi = T("npi", [128, 8, NE])
    s1 = [T("s1_%d" % i, [128, 8]) for i in range(8)]
    bbr = T("bbr", [128, 8, 16]); bbi = T("bbi", [128, 8, 16]); nci = T("nci", [128, 8, 16])
    t1 = T("t1", [128, 8, 16]); t2 = T("t2", [128, 8, 16])
    BinR = T("BinR", [128, 8, 128], BF16); BinI = T("BinI", [128, 8, 128], BF16)
    CoR = T("CoR", [128, 8, 128], BF16); CoIn = T("CoIn", [128, 8, 128], BF16)
    BinT = T("BinT", [128, 8, 2, 2, 128], BF16)
    p.op("pool", lambda e: e.memset(BinT[:], 0.0), w=["BinTz"])
    Kin = T("Kin", [128, 16, 128], BF16)
    NB = NC // 16
    Qs = [[[T("Q%d%d%d" % (pp, i, j), [128, NB, 24]) for j in range(2)] for i in range(2)] for pp in range(2)]
    TTs = [[[T("TT%d%d%d" % (pp, i, j), [128, 64 + NB]) for j in range(2)] for i in range(2)] for pp in range(2)]
    tmpA = [T("tmpA%d" % j, [128, NB, 16]) for j in range(2)]
    CoRz = T("CoRz", [128, 8, 2, 128], BF16); CoInz = T("CoInz", [128, 8, 2, 128], BF16)
    p.op("pool", lambda e: e.memset(CoRz[:], 0.0), w=["CoRz0"])
    p.op("pool", lambda e: e.memset(CoInz[:], 0.0), w=["CoInz0"])
    Vs = [[T("V%d%d" % (pp, j), [128, NC]) for j in range(2)] for pp in range(2)]
    Zbs = [[T("Zb%d%d" % (pp, j), [128, NC], BF16) for j in range(2)] for pp in range(2)]
    ysb = [T("ysb%d" % i, [128, 9, 256]) for i in range(2)]
    for pp in range(2):
        for i in range(2):
            for j in range(2):
                p.op("pool", lambda e, pp=pp, i=i, j=j: e.memset(Qs[pp][i][j][:, :, 0:8], 0.0), w=["Qpad%d%d%d" % (pp, i, j)])
                p.op("pool", lambda e, pp=pp, i=i, j=j: e.memset(TTs[pp][i][j][:, 0:64], 0.0), w=["Tpad%d%d%d" % (pp, i, j)])

    def dve(fn, r, w):
        p.op("dve", fn, r=r, w=w)

    def bc8(ap):
        return ap.unsqueeze(2).to_broadcast([128, 8, 16])

    for d in range(2):
        for g in range(16):
            p.dma(Ub[:, g, :], U_d[d, :, g, :], w=["U:%d" % g], q="pool")
        for i in range(3):
            p.dma(lam[:, i, :], lam_d[d, i, :, :], w=["lam"], q="sp")
        for i in range(2):
            p.dma(bp[:, i, :, :], b_d[d, i, :, :].rearrange("p (a c) -> p a c", c=16), w=["bp"], q="sp")
            p.dma(cp[:, i, :, :], c_d[d, i, :, :].rearrange("p (a c) -> p a c", c=16), w=["cp"], q="sp")
        lr, li, ldt = lam[:, 0, :], lam[:, 1, :], lam[:, 2, :]
        p.op("act", lambda e: e.activation(out=dt_[:], in_=ldt, func=AF.Exp), r=["lam"], w=["dt"])
        dve(lambda e: e.tensor_tensor(out=x_[:], in0=lr, in1=dt_[:], op=ALU.mult), ["lam", "dt"], ["x"])
        dve(lambda e: e.tensor_tensor(out=th_[:], in0=li, in1=dt_[:], op=ALU.mult), ["lam", "dt"], ["th"])
        evb = ev[:].unsqueeze(1).to_broadcast([128, 8, NE])
        dve(lambda e: e.tensor_tensor(out=xe[:], in0=x_[:].unsqueeze(2).to_broadcast([128, 8, NE]), in1=evb, op=ALU.mult), ["x", "ev"], ["xe"])
        dve(lambda e: e.tensor_tensor(out=the[:], in0=th_[:].unsqueeze(2).to_broadcast([128, 8, NE]), in1=evb, op=ALU.mult), ["th", "ev"], ["the"])
        p.op("act", lambda e: e.activation(out=mag[:], in_=xe[:], func=AF.Exp), r=["xe"], w=["mag"])
        inv2pi = 1.0 / (2.0 * np.pi)
        dve(lambda e: e.tensor_scalar(out=ys[:], in0=the[:], scalar1=inv2pi, scalar2=None, op0=ALU.mult), ["the"], ["ys"])
        dve(lambda e: e.tensor_scalar(out=yc[:], in0=the[:], scalar1=inv2pi, scalar2=0.25, op0=ALU.mult, op1=ALU.add), ["the"], ["yc"])
        for (yy, nm) in ((ys, "ys"), (yc, "yc")):
            dve(lambda e, yy=yy: e.tensor_copy(out=ki[:], in_=yy[:]), [nm], ["ki"])
            dve(lambda e: e.tensor_copy(out=kf[:], in_=ki[:]), ["ki"], ["kf"])
            dve(lambda e, yy=yy: e.tensor_tensor(out=yy[:], in0=yy[:], in1=kf[:], op=ALU.subtract), [nm, "kf"], [nm])
            dve(lambda e, yy=yy: e.tensor_scalar(out=m1[:], in0=yy[:], scalar1=0.5, scalar2=None, op0=ALU.is_gt), [nm], ["m1"])
            dve(lambda e, yy=yy: e.tensor_tensor(out=yy[:], in0=yy[:], in1=m1[:], op=ALU.subtract), [nm, "m1"], [nm])
            dve(lambda e, yy=yy: e.tensor_scalar(out=m1[:], in0=yy[:], scalar1=-0.5, scalar2=None, op0=ALU.is_lt), [nm], ["m1"])
            dve(lambda e, yy=yy: e.tensor_tensor(out=yy[:], in0=yy[:], in1=m1[:], op=ALU.add), [nm, "m1"], [nm])
        p.op("act", lambda e: e.activation(out=ys[:], in_=ys[:], func=AF.Sin, scale=2.0 * np.pi), r=["ys"], w=["ys"])
        p.op("act", lambda e: e.activation(out=yc[:], in_=yc[:], func=AF.Sin, scale=2.0 * np.pi), r=["yc"], w=["yc"])
        dve(lambda e: e.tensor_tensor(out=pr[:], in0=mag[:], in1=yc[:], op=ALU.mult), ["mag", "yc"], ["pr"])
        dve(lambda e: e.tensor_tensor(out=pi_[:], in0=mag[:], in1=ys[:], op=ALU.mult), ["mag", "ys"], ["pi"])
        dve(lambda e: e.tensor_scalar(out=npi[:], in0=pi_[:], scalar1=-1.0, scalar2=None, op0=ALU.mult), ["pi"], ["npi"])
        e1 = s5_eidx(1)
        a1r, a1i = pr[:, :, e1], pi_[:, :, e1]
        nr, den, rden, crr, cii, tA, tB = s1[0], s1[1], s1[2], s1[3], s1[4], s1[5], s1[6]
        dve(lambda e: e.tensor_scalar(out=nr[:], in0=a1r, scalar1=-1.0, scalar2=None, op0=ALU.add), ["pr"], ["nr"])
        dve(lambda e: e.tensor_tensor(out=den[:], in0=lr, in1=lr, op=ALU.mult), ["lam"], ["den"])
        dve(lambda e: e.tensor_tensor(out=tA[:], in0=li, in1=li, op=ALU.mult), ["lam"], ["tA"])
        dve(lambda e: e.tensor_tensor(out=den[:], in0=den[:], in1=tA[:], op=ALU.add), ["den", "tA"], ["den"])
        dve(lambda e: e.reciprocal(out=rden[:], in_=den[:]), ["den"], ["rden"])
        dve(lambda e: e.tensor_tensor(out=crr[:], in0=nr[:], in1=lr, op=ALU.mult), ["nr", "lam"], ["crr"])
        dve(lambda e: e.tensor_tensor(out=tA[:], in0=a1i, in1=li, op=ALU.mult), ["pi", "lam", "den"], ["tA"])
        dve(lambda e: e.tensor_tensor(out=crr[:], in0=crr[:], in1=tA[:], op=ALU.add), ["crr", "tA"], ["crr"])
        dve(lambda e: e.tensor_tensor(out=crr[:], in0=crr[:], in1=rden[:], op=ALU.mult), ["crr", "rden"], ["crr"])
        dve(lambda e: e.tensor_tensor(out=cii[:], in0=a1i, in1=lr, op=ALU.mult), ["pi", "lam"], ["cii"])
        dve(lambda e: e.tensor_tensor(out=tB[:], in0=nr[:], in1=li, op=ALU.mult), ["nr", "lam"], ["tB"])
        dve(lambda e: e.tensor_tensor(out=cii[:], in0=cii[:], in1=tB[:], op=ALU.subtract), ["cii", "tB"], ["cii"])
        dve(lambda e: e.tensor_tensor(out=cii[:], in0=cii[:], in1=rden[:], op=ALU.mult), ["cii", "rden"], ["cii"])
        br, bi = bp[:, 0, :, :], bp[:, 1, :, :]
        cr, ci = cp[:, 0, :, :], cp[:, 1, :, :]
        dve(lambda e: e.tensor_tensor(out=bbr[:], in0=br, in1=bc8(crr[:]), op=ALU.mult), ["bp", "crr"], ["bbr"])
        dve(lambda e: e.tensor_tensor(out=t1[:], in0=bi, in1=bc8(cii[:]), op=ALU.mult), ["bp", "cii"], ["t1"])
        dve(lambda e: e.tensor_tensor(out=bbr[:], in0=bbr[:], in1=t1[:], op=ALU.subtract), ["bbr", "t1"], ["bbr"])
        dve(lambda e: e.tensor_tensor(out=bbi[:], in0=bi, in1=bc8(crr[:]), op=ALU.mult), ["bp", "crr"], ["bbi"])
        dve(lambda e: e.tensor_tensor(out=t1[:], in0=br, in1=bc8(cii[:]), op=ALU.mult), ["bp", "cii", "bbr"], ["t1"])
        dve(lambda e: e.tensor_tensor(out=bbi[:], in0=bbi[:], in1=t1[:], op=ALU.add), ["bbi", "t1"], ["bbi"])
        dve(lambda e: e.tensor_scalar(out=nci[:], in0=ci, scalar1=-1.0, scalar2=None, op0=ALU.mult), ["cp"], ["nci"])
        BinRv = BinR[:].rearrange("p a (t c) -> p a t c", c=16); BinIv = BinI[:].rearrange("p a (t c) -> p a t c", c=16)
        CoRv = CoR[:].rearrange("p a (t c) -> p a t c", c=16); CoInv = CoIn[:].rearrange("p a (t c) -> p a t c", c=16)
        for tau in range(8):
            en = s5_eidx(-tau); ep = s5_eidx(tau)
            prn, pin, npin = bc8(pr[:, :, en]), bc8(pi_[:, :, en]), bc8(npi[:, :, en])
            prp, pip, npip = bc8(pr[:, :, ep]), bc8(pi_[:, :, ep]), bc8(npi[:, :, ep])
            dve(lambda e, prn=prn: e.tensor_tensor(out=t1[:], in0=bbr[:], in1=prn, op=ALU.mult), ["bbr", "pr", "bbi"], ["t1"])
            dve(lambda e, npin=npin: e.tensor_tensor(out=t2[:], in0=bbi[:], in1=npin, op=ALU.mult), ["bbi", "npi"], ["t2"])
            dve(lambda e, tau=tau: e.tensor_tensor(out=BinRv[:, :, tau, :], in0=t1[:], in1=t2[:], op=ALU.add), ["t1", "t2"], ["BinR:%d" % tau])
            dve(lambda e, prn=prn: e.tensor_tensor(out=t1[:], in0=bbi[:], in1=prn, op=ALU.mult), ["bbi", "pr", "BinR:%d" % tau], ["t1"])
            dve(lambda e, pin=pin: e.tensor_tensor(out=t2[:], in0=bbr[:], in1=pin, op=ALU.mult), ["bbr", "pi", "BinR:%d" % tau], ["t2"])
            dve(lambda e, tau=tau: e.tensor_tensor(out=BinIv[:, :, tau, :], in0=t1[:], in1=t2[:], op=ALU.add), ["t1", "t2"], ["BinI:%d" % tau])
            dve(lambda e, prp=prp: e.tensor_tensor(out=t1[:], in0=cr, in1=prp, op=ALU.mult), ["cp", "pr", "BinI:%d" % tau], ["t1"])
            dve(lambda e, pip=pip: e.tensor_tensor(out=t2[:], in0=nci[:], in1=pip, op=ALU.mult), ["nci", "pi", "BinI:%d" % tau], ["t2"])
            dve(lambda e, tau=tau: e.tensor_tensor(out=CoRv[:, :, tau, :], in0=t1[:], in1=t2[:], op=ALU.add), ["t1", "t2"], ["CoR:%d" % tau])
            dve(lambda e, npip=npip: e.tensor_tensor(out=t1[:], in0=cr, in1=npip, op=ALU.mult), ["cp", "npi", "CoR:%d" % tau], ["t1"])
            dve(lambda e, prp=prp: e.tensor_tensor(out=t2[:], in0=nci[:], in1=prp, op=ALU.mult), ["nci", "pr", "CoR:%d" % tau], ["t2"])
            dve(lambda e, tau=tau: e.tensor_tensor(out=CoInv[:, :, tau, :], in0=t1[:], in1=t2[:], op=ALU.add), ["t1", "t2"], ["CoIn:%d" % tau])
        for g2 in range(2):
            lo, hi = 64 * g2, 64 * g2 + 64
            p.op("pool", lambda e, g2=g2, lo=lo, hi=hi: e.tensor_copy(out=CoRz[lo:hi, :, g2, :], in_=CoR[lo:hi, :, :]),
                 r=["CoR:%d" % t for t in range(8)] + ["CoRz0"], w=["CoRz:%d" % g2])
            p.op("pool", lambda e, g2=g2, lo=lo, hi=hi: e.tensor_copy(out=CoInz[lo:hi, :, g2, :], in_=CoIn[lo:hi, :, :]),
                 r=["CoIn:%d" % t for t in range(8)] + ["CoInz0"], w=["CoInz:%d" % g2])
        allB = ["BinR:%d" % t for t in range(8)] + ["BinI:%d" % t for t in range(8)]
        allC = ["CoR:%d" % t for t in range(8)] + ["CoIn:%d" % t for t in range(8)]
        if DBG.get('stage', 9) < 2:
            p.fence(); continue
        for gp in range(8):
            pb = c.ps[gp % 2]
            pk = "ps%d" % (gp % 2)
            pbb = pb[:].bitcast(BF16)
            for ri, Bt in enumerate((BinR, BinI)):
                p.op("pe", lambda e, gp=gp, ri=ri, Bt=Bt, pbb=pbb: e.transpose(out=pbb[:, ri * 128:(ri + 1) * 128], in_=Bt[:, gp, :], identity=c.identb[:]),
                     r=allB + ["identb"], w=[pk + ":%d" % ri])
            for g2 in range(2):
                p.op("act", lambda e, gp=gp, pbb=pbb, g2=g2: e.activation(
                    out=BinT[:, gp, :, g2, 64 * g2:64 * g2 + 64],
                    in_=pbb[:, 0:256].rearrange("p (r n) -> p r n", r=2)[:, :, 64 * g2:64 * g2 + 64], func=AF.Copy),
                    r=[pk + ":0", pk + ":1", "BinTz"], w=["BinT:%d:%d" % (gp, g2)])
        if DBG.get('stage', 9) < 3:
            p.fence(); continue
        for g in range(16):
            gp, g2 = g // 2, g % 2
            pb = 2 + g % 2
            lo, hi = 64 * g2, 64 * g2 + 64
            p.op("pe", lambda e, gp=gp, lo=lo, hi=hi, pb=pb: e.matmul(c.ps[pb][:, 0:128], lhsT=BinR[lo:hi, gp, :], rhs=CoR[lo:hi, gp, :], start=True, stop=False),
                 r=allB + allC, w=["ps%d" % pb])
            p.op("pe", lambda e, gp=gp, lo=lo, hi=hi, pb=pb: e.matmul(c.ps[pb][:, 0:128], lhsT=BinI[lo:hi, gp, :], rhs=CoIn[lo:hi, gp, :], start=False, stop=True),
                 r=allB + allC, w=["ps%d" % pb])
            dve(lambda e, g=g, pb=pb: e.tensor_tensor(out=Kin[:, g, :], in0=c.ps[pb][:, 0:128], in1=mask[:], op=ALU.mult), ["ps%d" % pb, "mask"], ["Kin:%d" % g])
        p.fence()
        if DBG.get('stage', 9) < 4:
            continue
        blocks = [(0, 512), (512, 512), (1024, 128)]
        for gp in range(8):
            PP = gp % 2
            Q, TT_, V, Zb = Qs[PP], TTs[PP], Vs[PP], Zbs[PP]
            X = "x%d" % PP
            for ri in range(2):
                for bi_, (s, n) in enumerate(blocks):
                    bank = 2 + ri * 3 + bi_
                    for g2 in range(2):
                        p.op("pe", lambda e, Q=Q, TT_=TT_, V=V, Zb=Zb, gp=gp, ri=ri, g2=g2, s=s, n=n, bank=bank: e.matmul(
                            c.ps[bank][:, 0:n], lhsT=BinT[:, gp, ri, g2, :], rhs=Ub[:, 2 * gp + g2, s:s + n],
                            start=(g2 == 0), stop=(g2 == 1)),
                            r=["BinT:%d:0" % gp, "BinT:%d:1" % gp, "U:%d" % (2 * gp + g2)], w=["ps%d:0" % bank, "ps%d:1" % bank])
                    if DBG.get('nomm'):
                        continue
                    if not DBG.get('nov'):
                      p.op("act", lambda e, Q=Q, TT_=TT_, V=V, Zb=Zb, ri=ri, s=s, n=n, bank=bank: e.activation(out=V[ri][:, s:s + n], in_=c.ps[bank][:, 0:n], func=AF.Copy),
                         r=["ps%d:0" % bank, "ps%d:1" % bank], w=["V%d:%d" % (ri, bi_) + X])
                    p.op("act", lambda e, Q=Q, ri=ri, s=s, n=n, bank=bank: e.activation(out=Q[0][ri][:, s // 16:(s + n) // 16, 8:24],
                                                                          in_=c.ps[bank][:, 0:n].rearrange("p (b k) -> p b k", k=16), func=AF.Copy),
                         r=["ps%d:0" % bank, "ps%d:1" % bank], w=["Q0%d" % ri + X])
            if DBG.get('stage', 9) < 5:
                continue

            def cstep(dst, src, sh, Ar, Ai, nAi, sl, rk, wk):
                dve(lambda e, Q=Q, TT_=TT_, V=V, Zb=Zb: e.scalar_tensor_tensor(out=sl(dst[0], 0), in0=sl(src[0], -sh), scalar=Ar, in1=sl(src[0], 0), op0=ALU.mult, op1=ALU.add),
                    [rk + "0" + X, "pr"], [wk + "0" + X])
                dve(lambda e, Q=Q, TT_=TT_, V=V, Zb=Zb: e.scalar_tensor_tensor(out=sl(dst[0], 0), in0=sl(src[1], -sh), scalar=nAi, in1=sl(dst[0], 0), op0=ALU.mult, op1=ALU.add),
                    [rk + "1" + X, "npi", wk + "0" + X], [wk + "0" + X])
                dve(lambda e, Q=Q, TT_=TT_, V=V, Zb=Zb: e.scalar_tensor_tensor(out=sl(dst[1], 0), in0=sl(src[0], -sh), scalar=Ai, in1=sl(src[1], 0), op0=ALU.mult, op1=ALU.add),
                    [rk + "0" + X, rk + "1" + X, "pi"], [wk + "1" + X])
                dve(lambda e, Q=Q, TT_=TT_, V=V, Zb=Zb: e.scalar_tensor_tensor(out=sl(dst[1], 0), in0=sl(src[1], -sh), scalar=Ar, in1=sl(dst[1], 0), op0=ALU.mult, op1=ALU.add),
                    [rk + "1" + X, "pr", wk + "1" + X], [wk + "1" + X])
            for k in range(4):
                sh = 2 ** k
                ek = 15 + k
                cstep(Q[(k + 1) % 2], Q[k % 2], sh, pr[:, gp, ek:ek + 1], pi_[:, gp, ek:ek + 1], npi[:, gp, ek:ek + 1],
                      lambda buf, off: buf[:, :, 8 + off:24 + off], "Q%d" % (k % 2), "Q%d" % ((k + 1) % 2))
            for ri in range(2):
                dve(lambda e, Q=Q, TT_=TT_, V=V, Zb=Zb, ri=ri: e.tensor_copy(out=TT_[0][ri][:, 64:64 + NB], in_=Q[0][ri][:, :, 23]), ["Q0%d" % ri + X], ["T0%d" % ri + X])
            for m in range(7):
                sh = 2 ** m
                ek = 15 + 4 + m
                cstep(TT_[(m + 1) % 2], TT_[m % 2], sh, pr[:, gp, ek:ek + 1], pi_[:, gp, ek:ek + 1], npi[:, gp, ek:ek + 1],
                      lambda buf, off: buf[:, 64 + off:64 + NB + off], "T%d" % (m % 2), "T%d" % ((m + 1) % 2))
            Tf = TT_[7 % 2]
            tk = "T%d" % (7 % 2)
            e0 = 26
            Pr = pr[:, gp, e0:e0 + 16].unsqueeze(1).to_broadcast([128, NB, 16])
            Pi = pi_[:, gp, e0:e0 + 16].unsqueeze(1).to_broadcast([128, NB, 16])
            nPi = npi[:, gp, e0:e0 + 16].unsqueeze(1).to_broadcast([128, NB, 16])
            Tr = Tf[0][:, 63:63 + NB].unsqueeze(2).to_broadcast([128, NB, 16])
            Ti = Tf[1][:, 63:63 + NB].unsqueeze(2).to_broadcast([128, NB, 16])
            Qr, Qi = Q[0][0][:, :, 8:24], Q[0][1][:, :, 8:24]

            def pl(fn, r, w):
                p.op("pool", fn, r=r, w=w)
            pl(lambda e, Q=Q, TT_=TT_, V=V, Zb=Zb, Pr=Pr, Tr=Tr: e.tensor_tensor(out=tmpA[0][:], in0=Pr, in1=Tr, op=ALU.mult), ["pr", tk + "0" + X], ["tmpA0"])
            pl(lambda e, Q=Q, TT_=TT_, V=V, Zb=Zb, Qr=Qr: e.tensor_tensor(out=Qr, in0=Qr, in1=tmpA[0][:], op=ALU.add), ["Q00" + X, "tmpA0"], ["Q00" + X])
            pl(lambda e, Q=Q, TT_=TT_, V=V, Zb=Zb, nPi=nPi, Ti=Ti: e.tensor_tensor(out=tmpA[0][:], in0=nPi, in1=Ti, op=ALU.mult), ["npi", tk + "1" + X, "Q00" + X], ["tmpA0"])
            pl(lambda e, Q=Q, TT_=TT_, V=V, Zb=Zb, Qr=Qr: e.tensor_tensor(out=Qr, in0=Qr, in1=tmpA[0][:], op=ALU.add), ["Q00" + X, "tmpA0"], ["Q00" + X])
            pl(lambda e, Q=Q, TT_=TT_, V=V, Zb=Zb, Pr=Pr, Ti=Ti: e.tensor_tensor(out=tmpA[1][:], in0=Pr, in1=Ti, op=ALU.mult), ["pr", tk + "1" + X], ["tmpA1"])
            pl(lambda e, Q=Q, TT_=TT_, V=V, Zb=Zb, Qi=Qi: e.tensor_tensor(out=Qi, in0=Qi, in1=tmpA[1][:], op=ALU.add), ["Q01" + X, "tmpA1"], ["Q01" + X])
            pl(lambda e, Q=Q, TT_=TT_, V=V, Zb=Zb, Pi=Pi, Tr=Tr: e.tensor_tensor(out=tmpA[1][:], in0=Pi, in1=Tr, op=ALU.mult), ["pi", tk + "0" + X, "Q01" + X], ["tmpA1"])
            pl(lambda e, Q=Q, TT_=TT_, V=V, Zb=Zb, Qi=Qi: e.tensor_tensor(out=Qi, in0=Qi, in1=tmpA[1][:], op=ALU.add), ["Q01" + X, "tmpA1"], ["Q01" + X])
            for ri in range(2):
                p.op("pool", lambda e, Q=Q, TT_=TT_, V=V, Zb=Zb, ri=ri: e.tensor_tensor(out=Zb[ri][:].rearrange("p (b k) -> p b k", k=16), in0=Q[0][ri][:, :, 8:24],
                                                               in1=V[ri][:].rearrange("p (b k) -> p b k", k=16), op=ALU.subtract),
                     r=["Q0%d" % ri + X] + ["V%d:%d" % (ri, b_) + X for b_ in range(3)], w=["Zb%d" % ri + X])
            if DBG.get('stage', 9) < 6:
                continue
            yb_ = ysb[gp % 2]
            yk = "ysb%d" % (gp % 2)
            for t in range(9):
                bank = t % 2
                for g2 in range(2):
                    lo, hi = 64 * g2, 64 * g2 + 64
                    g = 2 * gp + g2
                    o = c.ps[bank][:, g2 * 128:(g2 + 1) * 128]
                    p.op("pe", lambda e, Q=Q, TT_=TT_, V=V, Zb=Zb, t=t, g2=g2, gp=gp, o=o: e.matmul(o, lhsT=Zb[0][:, t * 128:(t + 1) * 128], rhs=CoRz[:, gp, g2, :], start=True, stop=False),
                         r=["Zb0" + X, "CoRz:0", "CoRz:1"], w=["ps%d" % bank])
                    p.op("pe", lambda e, Q=Q, TT_=TT_, V=V, Zb=Zb, t=t, g2=g2, gp=gp, o=o: e.matmul(o, lhsT=Zb[1][:, t * 128:(t + 1) * 128], rhs=CoInz[:, gp, g2, :], start=False, stop=False),
                         r=["Zb1" + X, "CoInz:0", "CoInz:1"], w=["ps%d" % bank])
                    p.op("pe", lambda e, Q=Q, TT_=TT_, V=V, Zb=Zb, t=t, g=g, o=o: e.matmul(o, lhsT=Ub[:, g, t * 128:(t + 1) * 128], rhs=Kin[:, g, :], start=False, stop=True),
                         r=["U:%d" % g, "Kin:%d" % g], w=["ps%d" % bank])
                p.op("act", lambda e, Q=Q, TT_=TT_, V=V, Zb=Zb, t=t, bank=bank, yb_=yb_: e.activation(out=yb_[:, t, :], in_=c.ps[bank][:, 0:256], func=AF.Copy),
                     r=["ps%d" % bank], w=[yk + ":%d" % t])
            p.dma(Y_d[d, :, :, gp * 256:(gp + 1) * 256], yb_[:, :, :], r=[yk + ":%d" % t for t in range(9)], w=["Yout:%d:%d" % (d, gp)] + [yk + ":%d" % t for t in range(9)], q="sp")
        p.fence()
    p.op("sp", None, r=[])
    return p.build()


L_TOK = 8208
S5_LP = S5_NC * 8


def s5_layout_inputs(hb, j, lam_re, lam_im, log_dt, b_re, b_im, c_re, c_im):
    gs = slice(16 * j, 16 * j + 16)
    hp = np.zeros((2, S5_LP, 256), np.float32)
    hp[0, :L_TOK] = hb[:, 256 * j:256 * j + 256]
    hp[1, :L_TOK] = hb[::-1, 256 * j:256 * j + 256]
    U = hp.reshape(2, S5_NC, 8, 16, 16).transpose(0, 2, 4, 3, 1).reshape(2, 128, 16, S5_NC)

    def gp_layout(a):
        sh = a.shape
        a = a.reshape((2, 8, 2, 64) + sh[3:])
        a = np.moveaxis(a, 1, 3)
        return a.reshape((2, 128, 8) + sh[3:])
    lam = np.stack([gp_layout(lam_re[:, gs]), gp_layout(lam_im[:, gs]),
                    gp_layout(np.broadcast_to(log_dt[:, gs, None], (2, 16, 64)))], 1)
    bpar = np.stack([gp_layout(b_re[:, gs]), gp_layout(b_im[:, gs])], 1).reshape(2, 2, 128, 128)
    cT_re = np.swapaxes(c_re[:, gs], 2, 3); cT_im = np.swapaxes(c_im[:, gs], 2, 3)
    cpar = np.stack([gp_layout(cT_re), gp_layout(cT_im)], 1).reshape(2, 2, 128, 128)
    return dict(U=np.ascontiguousarray(U), lam=np.ascontiguousarray(lam.astype(np.float32)),
                bpar=np.ascontiguousarray(bpar), cpar=np.ascontiguousarray(cpar))


def s5_consts():
    ev = np.broadcast_to(np.array(S5_EV, np.float32)[None, :], (128, S5_NE)).copy()
    tau = np.arange(128) // 16
    mask = (tau[None, :] >= tau[:, None]).astype(np.float32)
    return dict(ev=ev, mask=mask)


def s5_unlayout(Y):
    Y = Y.reshape(2, 128, 9, 16, 8, 16)
    y = Y.transpose(0, 2, 1, 4, 3, 5).reshape(2, S5_LP, 256)
    yf = y[0, :L_TOK]
    yb = y[1, :L_TOK][::-1]
    return yf, yb


N_META = 16
SEQ = 8192
ROWS_PER_CORE = L_TOK // 4


def _rows_split(a):
    out = []
    for b in range(2):
        for q in range(4):
            blk = np.zeros((NR,) + a.shape[2:], a.dtype)
            blk[:ROWS_PER_CORE] = a[b, q * ROWS_PER_CORE:(q + 1) * ROWS_PER_CORE]
            out.append(blk)
    return out


def _rows_join(blocks, width, dtype):
    out = np.zeros((2, L_TOK, width), dtype)
    for b in range(2):
        for q in range(4):
            out[b, q * ROWS_PER_CORE:(q + 1) * ROWS_PER_CORE] = blocks[b * 4 + q][:ROWS_PER_CORE]
    return out


def _rope_tables():
    n_real = SEQ
    row_ids = np.concatenate([np.zeros(N_META), np.repeat(np.arange(n_real // 64), 64)]).astype(np.float32)
    col_ids = np.concatenate([np.zeros(N_META), np.tile(np.arange(64), n_real // 64)]).astype(np.float32)
    inv_freq = (np.float32(10000.0) ** (-(np.arange(0, 32, 2, dtype=np.float32) / np.float32(32.0)))).astype(np.float32)
    ang_r = (row_ids[:, None] * inv_freq[None, :]).astype(np.float32)
    ang_c = (col_ids[:, None] * inv_freq[None, :]).astype(np.float32)
    cr, sr, cc, sc = np.cos(ang_r), np.sin(ang_r), np.cos(ang_c), np.sin(ang_c)
    C1 = np.concatenate([cr, cr, cc, cc], -1)
    S1 = np.concatenate([-sr, sr, -sc, sc], -1)
    ctab = np.tile(C1, (1, 20)).astype(np.float32)
    stab = np.tile(S1, (1, 20)).astype(np.float32)
    return ctab, stab


def _run(nc, in_maps):
    res = run_bass_kernel_spmd(nc, in_maps, core_ids=list(range(8)))
    return res.results


def _ffn_ln_inputs(i, inputs):
    lnp = np.stack([inputs["ln_gain"][i, 0], inputs["ln_bias"][i, 0], inputs["ln_gain"][i, 1], inputs["ln_bias"][i, 1]]).astype(np.float32)
    return dict(lnp=np.ascontiguousarray(lnp), wg=np.ascontiguousarray(inputs["ffn_w_gate"][i]),
                wu=np.ascontiguousarray(inputs["ffn_w_up"][i]), wd=np.ascontiguousarray(inputs["ffn_w_down"][i]))


def kernel(**inputs):
    import ml_dtypes
    bf = ml_dtypes.bfloat16
    inputs = {k: np.asarray(v) for k, v in inputs.items()}
    x = inputs["x"].astype(np.float32)
    meta = np.broadcast_to(inputs["meta_tokens"].astype(np.float32)[None], (2, N_META, D))
    h = np.concatenate([meta, x], axis=1)
    ctab, stab = _rope_tables()
    ctab_b = np.broadcast_to(ctab[None], (2,) + ctab.shape)
    stab_b = np.broadcast_to(stab[None], (2,) + stab.shape)
    ct_rows = _rows_split(ctab_b); st_rows = _rows_split(stab_b)
    consts = s5_consts()
    valid = (np.arange(NK) < L_TOK).astype(bf).reshape(NKT, 128, 1)
    for i in range(4):
        j = i // 2
        common = _ffn_ln_inputs(i, inputs)
        hrows = _rows_split(h)
        if i % 2 == 0:
            in_maps = []
            for b in range(2):
                for q in range(4):
                    m = s5_layout_inputs(h[b], q, inputs["s5_lambda_re"][j], inputs["s5_lambda_im"][j], inputs["s5_log_dt"][j],
                                         inputs["s5_b_re"][j], inputs["s5_b_im"][j], inputs["s5_c_re"][j], inputs["s5_c_im"][j])
                    m.update(consts)
                    in_maps.append(m)
            res = _run(build_s5scan(), in_maps)
            yf = np.zeros((2, L_TOK, D), np.float32); yb = np.zeros((2, L_TOK, D), np.float32)
            for b in range(2):
                for q in range(4):
                    a, bb_ = s5_unlayout(res[b * 4 + q]["Y"])
                    yf[b, :, 256 * q:256 * q + 256] = a
                    yb[b, :, 256 * q:256 * q + 256] = bb_
            yfr = _rows_split(yf); ybr = _rows_split(yb)
            in_maps = []
            for cidx in range(8):
                m = dict(common)
                m.update(hin=hrows[cidx], yf=yfr[cidx], yb=ybr[cidx], dskip=np.ascontiguousarray(inputs["s5_d"][j]),
                         wglu=np.ascontiguousarray(inputs["s5_w_glu"][j].reshape(8, 128, D)),
                         wo=np.ascontiguousarray(inputs["s5_w_out"][j].reshape(8, 128, D)))
                in_maps.append(m)
            res = _run(build_post("s5"), in_maps)
        else:
            gain = np.concatenate([np.tile(inputs["attn_q_gain"][j], 16), np.tile(inputs["attn_k_gain"][j], 4)]).astype(np.float32)
            in_maps = [dict(hin=hrows[cidx], wqkv=np.ascontiguousarray(inputs["attn_w_qkv"][j]), gain=gain,
                            ctab=ct_rows[cidx], stab=st_rows[cidx]) for cidx in range(8)]
            res = _run(build_qkv(), in_maps)
            qkv = _rows_join([r["qkv"] for r in res], 1536, bf)
            Wo = inputs["attn_w_out"][j]
            wo_p = np.zeros((8, 128, D), np.float32)
            for g in range(4):
                for ii in range(2):
                    wo_p[2 * g + ii, 0:64] = Wo[64 * (4 * g + ii):64 * (4 * g + ii) + 64]
                    wo_p[2 * g + ii, 64:128] = Wo[64 * (4 * g + 2 + ii):64 * (4 * g + 2 + ii) + 64]
            in_maps = []
            for b in range(2):
                kpad = np.zeros((NK, 4, 64), bf); kpad[:L_TOK] = qkv[b, :, 1024:1280].reshape(L_TOK, 4, 64)
                kT = np.zeros((4, 128, NK), bf); kT[:, 0:64, :] = kpad.transpose(1, 2, 0)
                vpad = np.zeros((NK, 4, 64), bf); vpad[:L_TOK] = qkv[b, :, 1280:1536].reshape(L_TOK, 4, 64)
                vA = np.zeros((4, NKT, 128, 128), bf)
                for g in range(4):
                    vv = vpad[:, g, :].reshape(NKT, 128, 64)
                    vA[g, :, :, 0:64] = vv; vA[g, :, :, 64:128] = valid
                vA = np.ascontiguousarray(vA.transpose(0, 2, 1, 3)).reshape(4, 128, NKT * 128)
                for q in range(4):
                    qrows = np.zeros((NR, 16, 64), bf)
                    qrows[:ROWS_PER_CORE] = qkv[b, q * ROWS_PER_CORE:(q + 1) * ROWS_PER_CORE, 0:1024].reshape(ROWS_PER_CORE, 16, 64)
                    qT = np.zeros((4, 128, 17, 4, 128), bf)
                    for g in range(4):
                        for jj in range(4):
                            qT[g, 0:64, :, jj, :] = qrows[:, 4 * g + jj, :].reshape(17, 128, 64).transpose(2, 0, 1)
                    m = dict(common)
                    m.update(hin=hrows[b * 4 + q], qT=qT.reshape(4, 128, 17 * 512), kT=kT, vA=vA, wo=wo_p)
                    in_maps.append(m)
            res = _run(build_post("att"), in_maps)
        h = _rows_join([r["hout"] for r in res], D, np.float32)
    return np.ascontiguousarray(h[:, N_META:, :]).astype(np.float32)
```
